# Optimizing a Trainium2 kernel written in Bass

```python
import jax, jax.numpy as jnp
from jax import lax
import numpy as np

D_MODEL = 4096
BATCH = 2
SEQ = 8192
DEPTH = 2

CHUNK = 64
QBLOCK = 128
CONV_WIDTH = 3
D_CONV = D_MODEL // 2
N_HEADS_B = 8
HEAD_DIM_QK = D_MODEL // 32
HEAD_DIM_V = 2 * HEAD_DIM_QK
D_ATTN = N_HEADS_B * HEAD_DIM_V
D_QK = N_HEADS_B * 2 * HEAD_DIM_QK
D_FF = ((8 * D_MODEL // 3 + 255) // 256) * 256
D_IN = 3 * D_CONV + 2 * D_QK + D_ATTN + 2 * D_MODEL
EPS = 1e-6

kernel_name = "hybrid_shortconv_diffattn_convffn_block"


def rmsnorm(x, g):
    xf = x.astype(jnp.float32)
    y = xf * lax.rsqrt(jnp.mean(xf * xf, axis=-1, keepdims=True) + EPS)
    return (y * g.astype(jnp.float32)).astype(x.dtype)


def causal_dwconv(x, w):
    s = x.shape[1]
    xp = jnp.pad(x, ((0, 0), (CONV_WIDTH - 1, 0), (0, 0)))
    return sum(w[k] * xp[:, k:k + s] for k in range(CONV_WIDTH))


def alibi_slopes(n_heads):
    return 2.0 ** (-8.0 * jnp.arange(1, n_heads + 1, dtype=jnp.float32) / n_heads)


def lambda_init(layer_idx):
    return 0.8 - 0.6 * float(np.exp(-0.3 * layer_idx))


def diff_attention(q, k, v, lam, lam_init, subln_g):
    b, s = q.shape[0], q.shape[1]
    nb = s // QBLOCK
    scale = HEAD_DIM_QK ** -0.5
    slopes = alibi_slopes(N_HEADS_B)
    key_pos = jnp.arange(s)
    q_blocks = q.reshape(b, nb, QBLOCK, N_HEADS_B, 2, HEAD_DIM_QK).transpose(1, 0, 2, 3, 4, 5)
    starts = jnp.arange(nb) * QBLOCK

    def one_block(args):
        q_blk, start = args
        qry_pos = start + jnp.arange(QBLOCK)
        scores = jnp.einsum('bqhcd,bkhcd->bchqk', q_blk, k,
                            preferred_element_type=jnp.float32) * scale
        dist = jnp.abs(qry_pos[:, None] - key_pos[None, :]).astype(jnp.float32)
        bias = -slopes[:, None, None] * dist[None]
        visible = (key_pos[None, :] // CHUNK) <= (qry_pos[:, None] // CHUNK)
        scores = jnp.where(visible[None, None, None], scores + bias[None, None], -1e30)
        p = jax.nn.softmax(scores, axis=-1)
        a = p[:, 0] - lam * p[:, 1]
        return jnp.einsum('bhqk,bkhd->bqhd', a.astype(v.dtype), v)

    out = lax.map(one_block, (q_blocks, starts))
    out = out.transpose(1, 0, 2, 3, 4).reshape(b, s, N_HEADS_B, HEAD_DIM_V)
    out = rmsnorm(out, subln_g) * (1.0 - lam_init)
    return out.reshape(b, s, D_ATTN)


def setup_inputs(seed: int = 0) -> dict:
    key = jax.random.key(seed)
    ks = jax.random.split(key, 20)
    f32 = jnp.float32

    def w(k, shape, fan_in):
        return jax.random.normal(k, shape, f32) * fan_in ** -0.5

    def gain(k, shape):
        return 1.0 + 0.02 * jax.random.normal(k, shape, f32)

    return {
        "x": jax.random.normal(ks[0], (BATCH, SEQ, D_MODEL), f32),
        "w_in": w(ks[1], (DEPTH, D_MODEL, D_IN), D_MODEL),
        "b_gate": 0.01 * jax.random.normal(ks[2], (DEPTH, 2 * D_MODEL), f32),
        "conv_a": w(ks[3], (DEPTH, CONV_WIDTH, D_CONV), CONV_WIDTH),
        "w_a_out": w(ks[4], (DEPTH, D_CONV, D_MODEL), D_CONV),
        "lam_q1": 0.1 * jax.random.normal(ks[5], (DEPTH, HEAD_DIM_QK), f32),
        "lam_k1": 0.1 * jax.random.normal(ks[6], (DEPTH, HEAD_DIM_QK), f32),
        "lam_q2": 0.1 * jax.random.normal(ks[7], (DEPTH, HEAD_DIM_QK), f32),
        "lam_k2": 0.1 * jax.random.normal(ks[8], (DEPTH, HEAD_DIM_QK), f32),
        "subln_g": gain(ks[9], (DEPTH, HEAD_DIM_V)),
        "w_b_out": w(ks[10], (DEPTH, D_ATTN, D_MODEL), D_ATTN),
        "w_o": w(ks[11], (DEPTH, D_MODEL, D_MODEL), D_MODEL),
        "norm_mix_pre": gain(ks[12], (DEPTH, D_MODEL)),
        "norm_mix_post": gain(ks[13], (DEPTH, D_MODEL)),
        "w_ffn_in": w(ks[14], (DEPTH, D_MODEL, 2 * D_FF), D_MODEL),
        "conv_ffn": w(ks[15], (DEPTH, CONV_WIDTH, D_FF), CONV_WIDTH),
        "w_ffn_out": w(ks[16], (DEPTH, D_FF, D_MODEL), D_FF),
        "norm_ffn_pre": gain(ks[17], (DEPTH, D_MODEL)),
        "norm_ffn_post": gain(ks[18], (DEPTH, D_MODEL)),
    }


def reference(x, w_in, b_gate, conv_a, w_a_out, lam_q1, lam_k1, lam_q2, lam_k2, subln_g,
              w_b_out, w_o, norm_mix_pre, norm_mix_post, w_ffn_in, conv_ffn, w_ffn_out,
              norm_ffn_pre, norm_ffn_post):
    b, s, _ = x.shape
    for l in range(DEPTH):
        h = rmsnorm(x, norm_mix_pre[l])
        proj = h @ w_in[l]
        o = 0
        a_in = proj[..., o:o + D_CONV]; o += D_CONV
        a_b = proj[..., o:o + D_CONV]; o += D_CONV
        a_c = proj[..., o:o + D_CONV]; o += D_CONV
        q = proj[..., o:o + D_QK].reshape(b, s, N_HEADS_B, 2, HEAD_DIM_QK); o += D_QK
        k = proj[..., o:o + D_QK].reshape(b, s, N_HEADS_B, 2, HEAD_DIM_QK); o += D_QK
        v = proj[..., o:o + D_ATTN].reshape(b, s, N_HEADS_B, HEAD_DIM_V); o += D_ATTN
        gates = jax.nn.sigmoid(proj[..., o:o + 2 * D_MODEL] + b_gate[l])
        g_a, g_b = gates[..., :D_MODEL], gates[..., D_MODEL:]

        y_a = (a_b * causal_dwconv(a_c * a_in, conv_a[l])) @ w_a_out[l]

        lam_init = lambda_init(l)
        lam = (jnp.exp(jnp.sum(lam_q1[l].astype(jnp.float32) * lam_k1[l].astype(jnp.float32)))
               - jnp.exp(jnp.sum(lam_q2[l].astype(jnp.float32) * lam_k2[l].astype(jnp.float32)))
               + lam_init)
        y_b = diff_attention(q, k, v, lam, lam_init, subln_g[l]) @ w_b_out[l]

        mix = (g_a * y_a + g_b * y_b) @ w_o[l]
        x = x + rmsnorm(mix, norm_mix_post[l])

        h2 = rmsnorm(x, norm_ffn_pre[l])
        up2 = h2 @ w_ffn_in[l]
        gate_f, up_f = up2[..., :D_FF], up2[..., D_FF:]
        act = jax.nn.gelu(causal_dwconv(gate_f, conv_ffn[l]), approximate=True)
        ffn = (act * up_f) @ w_ffn_out[l]
        x = x + rmsnorm(ffn, norm_ffn_post[l])
    return x
```

```python
import math
from contextlib import ExitStack

import numpy as np
import concourse.bass as bass
import concourse.mybir as mybir
from concourse.bass_utils import run_bass_kernel_spmd

F32 = mybir.dt.float32
BF16 = mybir.dt.bfloat16
U8 = mybir.dt.uint8
AF = mybir.ActivationFunctionType
ALU = mybir.AluOpType
AX = mybir.AxisListType

EPS = 1e-6
COMPUTE = ("pe", "act", "dve", "pool")
QUEUES = ("sp", "act", "pool")
ARENA_BYTES = 207 * 1024


class Op:
    __slots__ = ("stream", "fn", "kind", "cs", "deps", "sig", "tick", "idx")


class Prog:
    def __init__(self, nc, n_lanes=8, same_engine_sync=True):
        self.nc = nc
        self.ops = []
        self.n_lanes = n_lanes
        self.same_engine_sync = same_engine_sync
        self.lane_rr = {q: 0 for q in QUEUES}
        self.lane_last = {}
        self.lane_cnt = {}
        self.reg_w = {}
        self.reg_r = {}
        self.last_cs = {}
        self.pending_barrier = {}
        self.outputs = []
        self.cc_cnt = 0

    def _record(self, op, reads, writes):
        psr = [k for k in reads if k[0] == "ps"]
        if psr:
            reads = [k for k in reads if k[0] != "ps"]
            writes = list(writes) + psr
        deps = {}
        for k in reads:
            w = self.reg_w.get(k)
            if w is not None:
                deps[w.idx] = w
        for k in writes:
            w = self.reg_w.get(k)
            if w is not None:
                deps[w.idx] = w
            for r in self.reg_r.get(k, {}).values():
                deps[r.idx] = r
        pb = self.pending_barrier.pop(op.stream, None)
        if pb:
            for d in pb:
                deps[d.idx] = d
        if op.kind == "dma":
            prev = self.lane_last.get(op.cs)
            if prev is not None:
                deps[prev.idx] = prev
        out = []
        for d in deps.values():
            if d.kind == "cmp" and d.stream == op.stream and op.kind == "cmp":
                if op.stream == "pe" or not self.same_engine_sync:
                    continue
            d.sig = True
            out.append(d)
        op.deps = out
        op.idx = len(self.ops)
        self.ops.append(op)
        for k in reads:
            self.reg_r.setdefault(k, {})[op.cs] = op
        for k in writes:
            self.reg_w[k] = op
            self.reg_r[k] = {}
        self.last_cs[op.cs] = op
        if op.kind != "cmp":
            self.lane_last[op.cs] = op
        return op

    def add(self, stream, fn, reads=(), writes=()):
        op = Op()
        op.stream = stream; op.fn = fn; op.kind = "cmp"; op.cs = stream
        op.sig = False; op.tick = None
        return self._record(op, reads, writes)

    def dma(self, queue, fn, reads=(), writes=(), output=False):
        op = Op()
        lane = "%s_l%d" % (queue, self.lane_rr[queue])
        self.lane_rr[queue] = (self.lane_rr[queue] + 1) % self.n_lanes
        op.stream = queue; op.fn = fn; op.kind = "dma"; op.cs = lane
        op.sig = True
        self.lane_cnt[lane] = self.lane_cnt.get(lane, 0) + 1
        op.tick = 16 * self.lane_cnt[lane]
        self._record(op, reads, writes)
        if output:
            self.outputs.append(op)
        return op

    def cc(self, fn, reads=(), writes=()):
        op = Op()
        op.stream = "pool"; op.fn = fn; op.kind = "cc"; op.cs = "cc"
        op.sig = True
        self.cc_cnt += 1
        op.tick = self.cc_cnt
        return self._record(op, reads, writes)

    def barrier(self):
        lst = [op for cs, op in self.last_cs.items() if cs != "cc"]
        for s in ("pe", "act", "dve", "pool", "sp"):
            self.pending_barrier[s] = list(lst)

    def _assign(self):
        cnt = {s: 0 for s in COMPUTE}
        for op in self.ops:
            if op.kind == "cmp" and op.sig:
                cnt[op.stream] += 1
                op.tick = cnt[op.stream]

    def check(self):
        self._assign()
        streams = {}
        for op in self.ops:
            streams.setdefault(op.stream, []).append(op)
        pos = {s: 0 for s in streams}
        val = {}
        progress = True
        while progress:
            progress = False
            for s, lst in streams.items():
                while pos[s] < len(lst):
                    op = lst[pos[s]]
                    if not all(val.get(d.cs, 0) >= d.tick for d in op.deps):
                        break
                    if op.kind == "dma":
                        val[op.cs] = val.get(op.cs, 0) + 16
                        assert val[op.cs] == op.tick
                    elif op.kind == "cc" or op.sig:
                        val[op.cs] = val.get(op.cs, 0) + 1
                        assert val[op.cs] == op.tick
                    pos[s] += 1
                    progress = True
        stuck = {s: (pos[s], len(lst)) for s, lst in streams.items() if pos[s] < len(lst)}
        if stuck:
            raise RuntimeError("deadlock in generated program %s" % stuck)

    def emit(self, stack):
        nc = self.nc
        self._assign()
        sems = {}
        for s in COMPUTE:
            sems[s] = stack.enter_context(nc.semaphore("s_" + s))
        for lane in self.lane_cnt:
            sems[lane] = stack.enter_context(nc.semaphore("s_" + lane))
        if self.cc_cnt:
            sems["cc"] = stack.enter_context(nc.semaphore("s_cc"))
        fin = Op(); fin.stream = "sp"; fin.kind = "fin"; fin.cs = "sp"; fin.sig = False
        fin.deps = list(self.outputs); fin.fn = None; fin.idx = len(self.ops)
        by_stream = {s: [] for s in ("pe", "act", "dve", "pool", "sp")}
        for op in self.ops + [fin]:
            by_stream[op.stream].append(op)
        block = stack.enter_context(nc.Block())

        def run_stream(eng, lst):
            seen = {}
            for op in lst:
                need = {}
                for d in op.deps:
                    if need.get(d.cs, 0) < d.tick:
                        need[d.cs] = d.tick
                for sname, t in need.items():
                    if seen.get(sname, 0) >= t:
                        continue
                    eng.wait_ge(sems[sname], t)
                    seen[sname] = t
                if op.fn is None:
                    continue
                ins = op.fn(eng)
                if op.kind == "dma":
                    ins.then_inc(sems[op.cs], 16)
                elif op.kind == "cc":
                    ins.then_inc(sems[op.cs])
                elif op.sig:
                    ins.then_inc(sems[op.cs], 1)

        @block.tensor
        def _(e):
            run_stream(e, by_stream["pe"])

        @block.scalar
        def _(e):
            run_stream(e, by_stream["act"])

        @block.vector
        def _(e):
            run_stream(e, by_stream["dve"])

        @block.gpsimd
        def _(e):
            run_stream(e, by_stream["pool"])

        @block.sync
        def _(e):
            run_stream(e, by_stream["sp"])


class Arena:
    def __init__(self, nc, nbytes):
        self.ap = nc.alloc_sbuf_tensor("arena", [128, nbytes], U8).ap()
        self.nbytes = nbytes

    def view(self, off, shape, dt):
        esz = {F32: 4, BF16: 2, U8: 1}[dt]
        n = int(np.prod(shape))
        assert off % 4 == 0 and off + n * esz <= self.nbytes, (off, n * esz, self.nbytes)
        v = self.ap[:, off:off + n * esz]
        if dt != U8:
            v = v.bitcast(dt)
        if len(shape) == 2:
            v = v.rearrange("p (a b) -> p a b", b=shape[1])
        elif len(shape) == 3:
            v = v.rearrange("p (a b c) -> p a b c", b=shape[1], c=shape[2])
        return v


class Layout:
    def __init__(self, base=0, limit=None):
        self.off = base
        self.limit = limit

    def take(self, nbytes):
        o = self.off
        self.off += (nbytes + 31) // 32 * 32
        if self.limit is not None:
            assert self.off <= self.limit, ("SBUF layout overflow", self.off, self.limit)
        return o


class WStream:
    def __init__(self, P, views, queue="pool"):
        self.P = P; self.views = views; self.R = len(views); self.queue = queue
        self.blocks = []; self.next_issue = 0; self.next_use = 0

    def plan(self, blocks):
        self.blocks.extend(blocks)

    def _issue(self):
        j = self.next_issue
        if j >= len(self.blocks):
            return
        self.next_issue += 1
        src = self.blocks[j]
        slot = j % self.R
        dst = self.views[slot][:, 0:src.shape[1], 0:src.shape[2]]
        self.P.dma(self.queue, lambda e: e.dma_start(out=dst, in_=src), writes=[("W", slot)])

    def start(self):
        for _ in range(self.R):
            self._issue()

    def acquire(self, expect):
        j = self.next_use
        assert j < self.next_issue and self.blocks[j] is expect, "weight plan mismatch at block %d" % j
        slot = j % self.R
        return self.views[slot], ("W", slot)

    def release(self):
        self.next_use += 1
        self._issue()


def wblocks(wv, k_chunks, c0, ncols):
    return [wv[:, kb:kb + min(8, k_chunks - kb), c0:c0 + ncols] for kb in range(0, k_chunks, 8)]


def wview(ap2d):
    return ap2d.rearrange("(kc p) n -> p kc n", p=128)


def make_cfg(D=4096, H=8, DFF=11008, T=2048, NR=4, NB=2, L=2, slopes=None, ring=4):
    c = dict(D=D, H=H, DFF=DFF, T=T, NR=NR, NB=NB, L=L, ring=ring)
    c["DC"] = D // 2
    c["KD"] = D // 128
    c["KC"] = c["DC"] // 128
    c["KF"] = DFF // 128
    c["DQK"] = H * 256
    c["DA"] = H * 256
    c["KA"] = c["DA"] // 128
    c["DIN"] = 3 * c["DC"] + 2 * c["DQK"] + c["DA"] + 2 * D
    c["o_ain"] = 0
    c["o_ab"] = c["DC"]
    c["o_ac"] = 2 * c["DC"]
    c["o_q"] = 3 * c["DC"]
    c["o_k"] = c["o_q"] + c["DQK"]
    c["o_v"] = c["o_k"] + c["DQK"]
    c["o_g"] = c["o_v"] + c["DA"]
    c["VE"] = 264
    c["HK"] = min(H, 4)
    c["NT"] = T // 512
    c["NBr"] = T // 128
    c["EOFF"] = NR * c["NBr"] - 1
    c["NE"] = c["NBr"] * (NR + 1) - 2
    if slopes is None:
        slopes = [2.0 ** (-8.0 * (i + 1) / H) for i in range(H)]
    c["slopes"] = slopes
    c["lam_init"] = [0.8 - 0.6 * math.exp(-0.3 * l) for l in range(8)]
    assert D % 512 == 0 and T % 512 == 0 and c["DC"] % 128 == 0 and DFF % 128 == 0
    return c


class Ctx:
    pass


def bank(X, b):
    return X.PS[:, b * 512:(b + 1) * 512]


def load_featmajor(X, src_rows, nk, C, stage_view, dst_view, name):
    P = X.P
    src = src_rows.rearrange("k (c p) -> c k p", p=128)
    P.dma("sp", lambda e: e.dma_start(out=stage_view[0:C, 0:nk, :], in_=src), writes=[("fstage",)])
    for k in range(nk):
        pv = bank(X, 7)[:, 0:C]
        P.add("pe", lambda e, k=k, pv=pv: e.transpose(pv, stage_view[0:C, k, :], X.ident[0:C, 0:C]),
              reads=[("fstage",), ("ident",)], writes=[("ps", 7)])
        P.add("dve", lambda e, k=k, pv=pv: e.tensor_copy(out=dst_view[:, k, :], in_=pv),
              reads=[("ps", 7)], writes=[(name,)])


def rstd_ops(P, sv, key, n):
    P.add("dve", lambda e: e.tensor_scalar(out=sv, in0=sv, scalar1=1.0 / n, scalar2=EPS, op0=ALU.mult, op1=ALU.add),
          reads=[key], writes=[key])
    P.add("act", lambda e: e.activation(out=sv, in_=sv, func=AF.Sqrt), reads=[key], writes=[key])
    P.add("dve", lambda e: e.reciprocal(out=sv, in_=sv), reads=[key], writes=[key])


def rms_rows(X, xs_v, xkey, junk_v, rows, D, gb_v, gbkey):
    P = X.P
    j = X.ssn[0] % 16
    X.ssn[0] += 1
    ss_v = X.ss[:, j:j + 1]
    sskey = ("ss", j)
    P.add("dve", lambda e: e.memset(ss_v[0:rows, :], 0.0), writes=[sskey])
    P.add("act", lambda e: e.activation(out=junk_v[0:rows, :], in_=xs_v[0:rows, :], func=AF.Square,
                                        accum_out=ss_v[0:rows, :]),
          reads=[xkey, sskey], writes=[("junk",), sskey])
    rstd_ops(P, ss_v[0:rows, :], sskey, D)
    P.add("dve", lambda e: e.scalar_tensor_tensor(out=xs_v[0:rows, :], in0=xs_v[0:rows, :], scalar=ss_v[0:rows, 0:1],
                                                  in1=gb_v[0:rows, :], op0=ALU.mult, op1=ALU.mult),
          reads=[xkey, sskey, gbkey], writes=[xkey])


def transpose_rows(X, xs_v, xkey, rows, KD, hT, hname, col0):
    P = X.P
    for c4 in range(0, KD, 4):
        n = min(4, KD - c4)
        b = X.bank_rr[0] % 8
        X.bank_rr[0] += 1
        pv = bank(X, b).rearrange("p (a b) -> p a b", b=128)

        def tr(e, c4=c4, n=n, pv=pv):
            last = None
            for i in range(n):
                last = e.transpose(pv[:, i, 0:rows], xs_v[0:rows, (c4 + i) * 128:(c4 + i + 1) * 128],
                                   X.ident[0:rows, 0:rows])
            return last
        P.add("pe", tr, reads=[xkey, ("ident",)], writes=[("ps", b)])
        dst = hT[:, c4:c4 + n, col0:col0 + rows]
        srcv = pv[:, 0:n, 0:rows]
        wk = [(hname, c4 + i) for i in range(n)]
        if (c4 // 4) % 2 == 0:
            P.add("act", lambda e, dst=dst, srcv=srcv: e.activation(out=dst, in_=srcv, func=AF.Copy),
                  reads=[("ps", b)], writes=wk)
        else:
            P.add("dve", lambda e, dst=dst, srcv=srcv: e.tensor_copy(out=dst, in_=srcv),
                  reads=[("ps", b)], writes=wk)


def load_gb(X, gb, src_row):
    X.P.dma("sp", lambda e: e.dma_start(out=gb, in_=src_row.partition_broadcast(128)), writes=[("gb",)])


def norm_stage_a(X, t, s, xin, xkey_fn, xs, junk, gb):
    P = X.P
    b = s % 2
    r0 = t * 512 + s * 128
    xv = xs[b]
    P.dma("sp", lambda e: e.dma_start(out=xv, in_=xin[r0:r0 + 128, :]), reads=[xkey_fn(t)], writes=[("xs", b)])
    rms_rows(X, xv, ("xs", b), junk, 128, X.cfg["D"], gb, ("gb",))


def norm_stage_b(X, s, xs, hT, hname):
    b = s % 2
    transpose_rows(X, xs[b], ("xs", b), 128, X.cfg["KD"], hT, hname, s * 128)


def norm_tile(X, t, xin, xkey_fn, gsrc, xs, junk, gb, hT, hname, halo_src=None):
    P = X.P; cfg = X.cfg; D = cfg["D"]; KD = cfg["KD"]
    load_gb(X, gb, gsrc)

    def stage_a(s):
        b = s % 2
        r0 = t * 512 + s * 128
        xv = xs[b]
        P.dma("sp", lambda e, xv=xv, r0=r0: e.dma_start(out=xv, in_=xin[r0:r0 + 128, :]),
              reads=[xkey_fn(t)], writes=[("xs", b)])
        rms_rows(X, xv, ("xs", b), junk, 128, D, gb, ("gb",))

    def stage_b(s):
        b = s % 2
        transpose_rows(X, xs[b], ("xs", b), 128, KD, hT, hname, s * 128)
    stage_a(0); stage_a(1); stage_b(0); stage_a(2); stage_b(1); stage_a(3); stage_b(2); stage_b(3)
    if halo_src is not None:
        halo_rows(X, halo_src[0], halo_src[1], xs[0], ("xs", 0), xs[1], ("xs", 1), junk, gb, hT, hname)


def halo_rows(X, hall, hkey, xv, xkey, tmp4, tkey, junk, gb, hT, hname):
    P = X.P; cfg = X.cfg; D = cfg["D"]; NR = cfg["NR"]
    src = hall.rearrange("(r two) d -> two r d", two=2)
    import os
    if os.environ.get("HALO_DIRECT"):
        P.dma("sp", lambda e: e.dma_start(out=xv[0:2, :], in_=hall[0:2, :]), reads=[hkey], writes=[xkey])
        rms_rows(X, xv, xkey, junk, 2, D, gb, ("gb",))
        transpose_rows(X, xv, xkey, 2, cfg["KD"], hT, hname, 512)
        return
    t4 = tmp4[0:2, 0:NR * D].rearrange("p (r d) -> p r d", d=D) if NR * D <= tmp4.shape[1] else None
    for r in range(NR):
        P.dma("sp", lambda e, r=r: e.dma_start(out=tmp4[0:2, 0:D], in_=src[:, r, :]), reads=[hkey], writes=[tkey])
        if r == 0:
            P.add("dve", lambda e, r=r: e.tensor_scalar(out=xv[0:2, :], in0=tmp4[0:2, 0:D], scalar1=X.sel2[0:2, r:r + 1],
                                                         scalar2=None, op0=ALU.mult),
                  reads=[tkey, ("sel",)], writes=[xkey])
        else:
            P.add("dve", lambda e, r=r: e.scalar_tensor_tensor(out=xv[0:2, :], in0=tmp4[0:2, 0:D], scalar=X.sel2[0:2, r:r + 1],
                                                                in1=xv[0:2, :], op0=ALU.mult, op1=ALU.add),
                  reads=[tkey, ("sel",), xkey], writes=[xkey])
    rms_rows(X, xv, xkey, junk, 2, D, gb, ("gb",))
    transpose_rows(X, xv, xkey, 2, cfg["KD"], hT, hname, 512)


def mm_ws(X, blk_list, nch, bank0, rhs_fn, rkeys_fn, ncols=512, col0=0):
    P = X.P
    nkb = len(blk_list)
    for kb, blk in enumerate(blk_list):
        wv, wkey = X.ws.acquire(blk)
        kc = blk.shape[1]
        for n in range(nch):
            def mm(e, wv=wv, kb=kb, kc=kc, n=n):
                last = None
                for k in range(kc):
                    last = e.matmul(bank(X, bank0 + n)[:, col0:col0 + ncols], wv[:, k, n * 128:(n + 1) * 128],
                                    rhs_fn(kb * 8 + k),
                                    start=(kb == 0 and k == 0), stop=(kb == nkb - 1 and k == kc - 1))
                return last
            P.add("pe", mm, reads=[wkey] + [rkeys_fn(kb * 8 + k) for k in range(kc)], writes=[("ps", bank0 + n)])
        X.ws.release()


def mm_ws_halo(X, blk_list, nch, b, rhs_fn, rkeys_fn, first):
    P = X.P
    nkb = len(blk_list)
    for kb, blk in enumerate(blk_list):
        wv, wkey = X.ws.acquire(blk)
        kc = blk.shape[1]

        def mm(e, wv=wv, kb=kb, kc=kc):
            last = None
            for n in range(nch):
                for k in range(kc):
                    last = e.matmul(bank(X, b)[:, n * 2:n * 2 + 2], wv[:, k, n * 128:(n + 1) * 128], rhs_fn(kb * 8 + k),
                                    start=(kb == 0 and k == 0 and n == 0), stop=(kb == nkb - 1 and k == kc - 1),
                                    skip_group_check=True)
            return last
        P.add("pe", mm, reads=[wkey] + [rkeys_fn(kb * 8 + k) for k in range(kc)], writes=[("ps", b)])
        X.ws.release()


def mm_as(X, blk_list, bank0, lhs_fn, lkeys_fn):
    P = X.P
    nkb = len(blk_list)
    for kb, blk in enumerate(blk_list):
        wv, wkey = X.ws.acquire(blk)
        kc = blk.shape[1]
        ncol = blk.shape[2]
        for s in range(4):
            def mm(e, wv=wv, kb=kb, kc=kc, s=s, ncol=ncol):
                last = None
                for k in range(kc):
                    last = e.matmul(bank(X, bank0 + s)[:, 0:ncol], lhs_fn(kb * 8 + k, s), wv[:, k, 0:ncol],
                                    start=(kb == 0 and k == 0), stop=(kb == nkb - 1 and k == kc - 1))
                return last
            P.add("pe", mm, reads=[wkey] + [lkeys_fn(kb * 8 + k) for k in range(kc)], writes=[("ps", bank0 + s)])
        X.ws.release()


def conv3(P, ext, ekey, cw, c, cbv, ckey, wkey):
    P.add("dve", lambda e: e.tensor_scalar(out=cbv, in0=ext[:, 0:512], scalar1=cw[:, 0, c:c + 1], scalar2=None, op0=ALU.mult),
          reads=[ekey, wkey], writes=[ckey])
    for kk in (1, 2):
        P.add("dve", lambda e, kk=kk: e.scalar_tensor_tensor(out=cbv, in0=ext[:, kk:kk + 512], scalar=cw[:, kk, c:c + 1],
                                                              in1=cbv, op0=ALU.mult, op1=ALU.add),
              reads=[ekey, wkey, ckey], writes=[ckey])


def out_proj_post(X, t, plan_out, actT, aname, xin, xkey_fn, gsrc, xout, okey_fn, res, junk3, xs, gb, halo_out=None):
    P = X.P; cfg = X.cfg; D = cfg["D"]
    ssq = X.ssq
    P.add("dve", lambda e: e.memset(ssq, 0.0), writes=[("ssq",)])
    for cb in range(D // 512):
        b0 = 4 * (cb % 2)
        mm_as(X, plan_out[cb], b0, lambda k, s: actT[:, k, s * 128:(s + 1) * 128], lambda k: (aname, k))
        for s in range(4):
            P.add("dve", lambda e, s=s, cb=cb, b0=b0: e.tensor_copy(out=res[:, s, cb * 512:(cb + 1) * 512], in_=bank(X, b0 + s)),
                  reads=[("ps", b0 + s)], writes=[("res", s, cb)])
            P.add("act", lambda e, s=s, cb=cb: e.activation(out=junk3, in_=res[:, s, cb * 512:(cb + 1) * 512], func=AF.Square,
                                                            accum_out=ssq[:, s, cb:cb + 1]),
                  reads=[("res", s, cb), ("ssq",)], writes=[("junk3",), ("ssq",)])
    P.barrier()
    load_gb(X, gb, gsrc)
    for s in range(4):
        b = s % 2
        xv = xs[b]
        r0 = t * 512 + s * 128
        P.dma("sp", lambda e, xv=xv, r0=r0: e.dma_start(out=xv, in_=xin[r0:r0 + 128, :]), reads=[xkey_fn(t)], writes=[("xs", b)])
        j = X.ssn[0] % 16
        X.ssn[0] += 1
        sv = X.ss[:, j:j + 1]
        P.add("dve", lambda e, s=s, sv=sv: e.reduce_sum(out=sv, in_=ssq[:, s, 0:D // 512], axis=AX.X),
              reads=[("ssq",)], writes=[("ss", j)])
        rstd_ops(P, sv, ("ss", j), D)
        P.add("dve", lambda e, s=s, sv=sv: e.scalar_tensor_tensor(out=res[:, s, :], in0=res[:, s, :], scalar=sv, in1=gb,
                                                                  op0=ALU.mult, op1=ALU.mult),
              reads=[("res", s, cb_) for cb_ in range(D // 512)] + [("ss", j), ("gb",)], writes=[("res", s)])
        P.add("dve", lambda e, s=s, xv=xv: e.tensor_tensor(out=xv, in0=xv, in1=res[:, s, :], op=ALU.add),
              reads=[("res", s), ("xs", b)], writes=[("xs", b)])
        P.dma("sp", lambda e, xv=xv, r0=r0: e.dma_start(out=xout[r0:r0 + 128, :], in_=xv),
              reads=[("xs", b)], writes=[okey_fn(t)], output=X.is_out(xout))
        if halo_out is not None and t == cfg["NT"] - 1 and s == 3:
            P.dma("sp", lambda e, xv=xv: e.dma_start(out=halo_out[0][0:2, :], in_=xv[126:128, :]),
                  reads=[("xs", b)], writes=[halo_out[1]], output=X.is_out(halo_out[0]))
    P.barrier()


def kv_plan(X, l):
    cfg = X.cfg
    wv = wview(X.dr["w_in"][l])
    plan = {"k": [], "v": []}
    for cg in range(0, 2 * cfg["H"], 4):
        plan["k"].append(wblocks(wv, cfg["KD"], cfg["o_k"] + cg * 128, 512))
    for vb in range(cfg["DA"] // 512):
        plan["v"].append(wblocks(wv, cfg["KD"], cfg["o_v"] + vb * 512, 512))
    seq = []
    for t in range(cfg["NT"]):
        for lst in plan["k"] + plan["v"]:
            seq.extend(lst)
    return plan, seq


def kv_phase(X, l, plan, xin, xkey_fn):
    P = X.P; cfg = X.cfg; D = cfg["D"]; KD = cfg["KD"]; H = cfg["H"]; NBr = cfg["NBr"]
    A = X.A
    L = Layout(0, X.act_limit)
    hT = A.view(L.take(KD * 514 * 2), [KD, 514], BF16)
    hT_b = A.view(L.take(KD * 514 * 2), [KD, 514], BF16)
    xs = [A.view(L.take(D * 4), [D], F32) for _ in range(2)]
    junk = A.view(L.take(D * 2), [D], BF16)
    gb = A.view(L.take(D * 4), [D], F32)
    kst = A.view(L.take(2 * H * 512 * 2), [2 * H, 512], BF16)
    VE = cfg["VE"]; HK = cfg["HK"]
    vst = [A.view(L.take(H * VE * 2), [H, VE], BF16) for _ in range(4)]
    dr = X.dr
    for s in range(4):
        P.add("dve", lambda e, s=s: e.memset(vst[s][:, :, 256:VE], 1.0), writes=[("vst", s)])
    P.dma("sp", lambda e: e.dma_start(out=dr["xh_loc"][0:2, :], in_=xin[cfg["T"] - 2:cfg["T"], :]),
          reads=[xkey_fn(cfg["NT"] - 1)], writes=[X.lockey("xh", 0)], output=X.is_out(dr["xh_loc"]))
    if X.after_kv_tile is not None:
        X.allgather("xh")
    cnt = 0
    NT = cfg["NT"]
    hTs = [hT, hT_b]
    hnames = ["hT", "hTb"]
    norm_tile(X, 0, xin, xkey_fn, dr["norm_mix_pre"][l:l + 1, :], xs, junk, gb, hTs[0], hnames[0])
    for t in range(NT):
        hTc = hTs[t % 2]; hn = hnames[t % 2]
        hTn = hTs[(t + 1) % 2]; hnn = hnames[(t + 1) % 2]
        for gi, cg in enumerate(range(0, 2 * H, 4)):
            b0 = 4 * (cnt % 2); cnt += 1
            mm_ws(X, plan["k"][gi], 4, b0, lambda k, hTc=hTc: hTc[:, k, 0:512], lambda k, hn=hn: (hn, k))
            for n in range(4):
                c = cg + n
                if n % 2 == 0:
                    P.add("act", lambda e, c=c, n=n, b0=b0: e.activation(out=kst[:, c, :], in_=bank(X, b0 + n), func=AF.Copy),
                          reads=[("ps", b0 + n)], writes=[("kst", c)])
                else:
                    P.add("dve", lambda e, c=c, n=n, b0=b0: e.tensor_copy(out=kst[:, c, :], in_=bank(X, b0 + n)),
                          reads=[("ps", b0 + n)], writes=[("kst", c)])
            kh = cg // (2 * HK); cl = cg % (2 * HK)
            kdst = dr["kT_loc"][(t, kh)].rearrange("(c p) t -> p c t", p=128)[:, cl:cl + 4, :]
            P.dma("sp", lambda e, cg=cg, kdst=kdst: e.dma_start(out=kdst, in_=kst[:, cg:cg + 4, :]),
                  reads=[("kst", cg + i) for i in range(4)], writes=[X.lockey(("kT", t, kh), cg)], output=X.is_out(dr["kT_loc"][(t, kh)]))
        for vb in range(cfg["DA"] // 512):
            b0 = 4 * (cnt % 2); cnt += 1
            mm_as(X, plan["v"][vb], b0, lambda k, s, hTc=hTc: hTc[:, k, s * 128:(s + 1) * 128], lambda k, hn=hn: (hn, k))
            for s in range(4):
                src = bank(X, b0 + s).rearrange("p (h e) -> p h e", e=256)
                dst = vst[s][:, 2 * vb:2 * vb + 2, 0:256]
                if s % 2 == 0:
                    P.add("act", lambda e, src=src, dst=dst: e.activation(out=dst, in_=src, func=AF.Copy),
                          reads=[("ps", b0 + s)], writes=[("vst", s)])
                else:
                    P.add("dve", lambda e, src=src, dst=dst: e.tensor_copy(out=dst, in_=src),
                          reads=[("ps", b0 + s)], writes=[("vst", s)])
            if t + 1 < NT:
                nvb = cfg["DA"] // 512
                if vb == 0:
                    norm_stage_a(X, t + 1, 0, xin, xkey_fn, xs, junk, gb)
                    norm_stage_a(X, t + 1, 1, xin, xkey_fn, xs, junk, gb)
                if vb == min(1, nvb - 1):
                    norm_stage_b(X, 0, xs, hTn, hnn)
                    norm_stage_b(X, 1, xs, hTn, hnn)
                    norm_stage_a(X, t + 1, 2, xin, xkey_fn, xs, junk, gb)
                    norm_stage_a(X, t + 1, 3, xin, xkey_fn, xs, junk, gb)
                if vb == nvb - 1:
                    norm_stage_b(X, 2, xs, hTn, hnn)
                    norm_stage_b(X, 3, xs, hTn, hnn)
        for s in range(4):
            for vh in range(H // 2):
                vdst = dr["v_loc"][(t, vh)].rearrange("(h p) (k e) -> p h k e", p=128, e=VE)[:, :, s, :]
                P.dma("sp", lambda e, s=s, vh=vh, vdst=vdst: e.dma_start(out=vdst, in_=vst[s][:, 2 * vh:2 * vh + 2, :]),
                      reads=[("vst", s)], writes=[X.lockey(("v", t, vh), s)], output=X.is_out(dr["v_loc"][(t, vh)]))
        if X.after_kv_tile is not None:
            X.after_kv_tile(t)
    P.barrier()


def mixer_plan(X, l):
    cfg = X.cfg; KD = cfg["KD"]; KC = cfg["KC"]; KA = cfg["KA"]; D = cfg["D"]
    wv = wview(X.dr["w_in"][l])
    wa = wview(X.dr["w_a_out"][l]); wb = wview(X.dr["w_b_out"][l]); wo = wview(X.dr["w_o"][l])
    plan = {"ain": [], "ac": [], "ab": [], "q": [], "ga": [], "ya": [], "gb": [], "yb": [], "out": []}
    for c0 in range(0, KC, 4):
        n = min(4, KC - c0) * 128
        plan["ain"].append(wblocks(wv, KD, cfg["o_ain"] + c0 * 128, n))
        plan["ac"].append(wblocks(wv, KD, cfg["o_ac"] + c0 * 128, n))
        plan["ab"].append(wblocks(wv, KD, cfg["o_ab"] + c0 * 128, n))
    for c0 in range(0, 2 * cfg["H"], 4):
        plan["q"].append(wblocks(wv, KD, cfg["o_q"] + c0 * 128, 512))
    for og in range(KD // 4):
        plan["ga"].append(wblocks(wv, KD, cfg["o_g"] + og * 512, 512))
        plan["ya"].append(wblocks(wa, KC, og * 512, 512))
        plan["gb"].append(wblocks(wv, KD, cfg["o_g"] + D + og * 512, 512))
        plan["yb"].append(wblocks(wb, KA, og * 512, 512))
        plan["out"].append(wblocks(wo, KD, og * 512, 512))
    seq = []
    for gi in range(len(plan["ain"])):
        seq.extend(plan["ain"][gi]); seq.extend(plan["ac"][gi])
    for t in range(cfg["NT"]):
        for gi in range(len(plan["ain"])):
            seq.extend(plan["ain"][gi]); seq.extend(plan["ac"][gi]); seq.extend(plan["ab"][gi])
        for lst in plan["q"]:
            seq.extend(lst)
        for og in range(KD // 4):
            for nm in ("ga", "ya", "gb", "yb"):
                seq.extend(plan[nm][og])
        for og in range(KD // 4):
            seq.extend(plan["out"][og])
    return plan, seq


def mixer_phase(X, l, plan, xin, xkey_fn, xout, okey_fn):
    P = X.P; cfg = X.cfg; A = X.A; dr = X.dr
    D = cfg["D"]; KD = cfg["KD"]; KC = cfg["KC"]; H = cfg["H"]; KA = cfg["KA"]; NR = cfg["NR"]; NBr = cfg["NBr"]
    T = cfg["T"]
    scale = 128.0 ** -0.5
    L = Layout(0, X.act_limit)
    o_hT = L.take(KD * 514 * 2)
    o_ua = L.take(KC * 512 * 2)
    o_at = L.take(KA * 512 * 2)
    o_qt = L.take(max(2 * H * 512 * 2, 4 * 512 * 4 * 2, D * 4))
    VE = cfg["VE"]; HK = cfg["HK"]
    o_mT = L.take(max(KD * 512 * 2, 2 * D * 4, 2 * (2 * T * 2) + 2 * (NBr * VE * 2) + 64))
    o_S = L.take(max(2 * D * 4 - (L.off - o_mT) if False else 0, 32768))
    hT = A.view(o_hT, [KD, 514], BF16)
    mix = A.view(0, [4, D], F32)
    assert 4 * D * 4 <= o_mT, "mix overlay must stay below mT"
    uaT = A.view(o_ua, [KC, 512], BF16)
    attnT = A.view(o_at, [KA, 512], BF16)
    QT = A.view(o_qt, [2 * H, 512], BF16)
    mT = A.view(o_mT, [KD, 512], BF16)
    xs1 = [A.view(o_mT + i * D * 4, [D], F32) for i in range(2)]
    gb1 = A.view(o_at, [D], F32) if KA * 512 * 2 >= D * 4 else None
    LS = Layout(o_S, o_S + 32768)
    junk1 = A.view(LS.take(D * 2), [D], BF16)
    if gb1 is None:
        gb1 = A.view(LS.take(D * 4), [D], F32)
    ain_t = [A.view(LS.take(512 * 4), [512], F32) for _ in range(4)]
    uext = [A.view(LS.take(516 * 4), [516], F32) for _ in range(4)]
    cbuf = [A.view(LS.take(512 * 4), [512], F32) for _ in range(2)]
    LK = Layout(o_mT, o_S)
    kbuf = [A.view(LK.take(2 * T * 2), [2, T], BF16) for _ in range(2)]
    vbuf = [A.view(LK.take(NBr * VE * 2), [NBr, VE], BF16) for _ in range(2)]
    L2 = Layout(o_S, o_S + 32768)
    PT = [A.view(L2.take(512 * 2), [512], BF16) for _ in range(4)]
    E32 = [A.view(L2.take(512 * 4), [512], F32) for _ in range(2)]
    mtp = A.view(L2.take(256 * 4), [256], F32)
    obuf = [A.view(L2.take(4 * 260 * 4), [4, 260], F32) for _ in range(1)]
    junk2 = A.view(L2.take(256 * 2), [256], BF16)
    mm1 = A.view(L2.take(H * 2 * 256 * 4), [H, 2, 256], F32)
    L3 = Layout(o_qt, o_mT)
    ga_t = [A.view(L3.take(512 * 4), [512], F32) for _ in range(4)]
    m_t = [A.view(L3.take(512 * 4), [512], F32) for _ in range(4)]
    xs4 = [A.view(o_S + i * D * 4, [D], F32) for i in range(2)] if 2 * D * 4 <= 32768 else None
    assert xs4 is not None
    gb4 = A.view(o_qt, [D], F32)
    junk3 = X.junk3

    load_featmajor(X, dr["conv_a"][l], 3, KC, X.fstage, X.cwa, "cwa")
    load_featmajor(X, dr["b_gate"][l:l + 1, :], 1, 2 * KD, X.fstage, X.bgT, "bgT")
    lv = [A.view(o_S + i * 512, [128], F32) for i in range(4)]
    for i, nm in enumerate(("lam_q1", "lam_k1", "lam_q2", "lam_k2")):
        P.dma("sp", lambda e, i=i, nm=nm: e.dma_start(out=lv[i], in_=dr[nm][l:l + 1, :].partition_broadcast(128)),
              writes=[("lv", i)])
    lsum = X.lsum
    for j in range(2):
        P.add("dve", lambda e, j=j: e.tensor_tensor(out=lv[2 * j], in0=lv[2 * j], in1=lv[2 * j + 1], op=ALU.mult),
              reads=[("lv", 2 * j), ("lv", 2 * j + 1)], writes=[("lv", 2 * j)])
        P.add("dve", lambda e, j=j: e.reduce_sum(out=lsum[:, j:j + 1], in_=lv[2 * j], axis=AX.X),
              reads=[("lv", 2 * j)], writes=[("lsum",)])
    P.add("act", lambda e: e.activation(out=lsum[:, 0:2], in_=lsum[:, 0:2], func=AF.Exp), reads=[("lsum",)], writes=[("lsum",)])
    P.add("dve", lambda e: e.tensor_tensor(out=X.nlam, in0=lsum[:, 1:2], in1=lsum[:, 0:1], op=ALU.subtract),
          reads=[("lsum",)], writes=[("nlam",)])
    P.add("dve", lambda e: e.tensor_scalar(out=X.nlam, in0=X.nlam, scalar1=X.laminit[:, l:l + 1], scalar2=None, op0=ALU.subtract),
          reads=[("nlam",), ("laminit",)], writes=[("nlam",)])
    P.dma("sp", lambda e: e.dma_start(out=X.sgb, in_=dr["subln_g"][l:l + 1, :].partition_broadcast(128)), writes=[("sgb",)])
    P.add("dve", lambda e: e.tensor_scalar(out=X.sgb, in0=X.sgb, scalar1=X.omlam[:, l:l + 1], scalar2=None, op0=ALU.mult),
          reads=[("sgb",), ("laminit",)], writes=[("sgb",)])
    P.barrier()

    load_gb(X, gb1, dr["norm_mix_pre"][l:l + 1, :])
    halo_rows(X, dr["xh_all"], ("xh_all",), xs1[0], ("xs", 0), xs1[1], ("xs", 1), junk1, gb1, hT, "hT")
    uprev = X.uprev
    for gi, c0 in enumerate(range(0, KC, 4)):
        nch = min(4, KC - c0)
        mm_ws_halo(X, plan["ain"][gi], nch, 0, lambda k: hT[:, k, 512:514], lambda k: ("hT", k), True)
        mm_ws_halo(X, plan["ac"][gi], nch, 1, lambda k: hT[:, k, 512:514], lambda k: ("hT", k), True)
        P.add("act", lambda e, nch=nch: e.activation(out=ain_t[0][:, 0:2 * nch], in_=bank(X, 0)[:, 0:2 * nch], func=AF.Copy),
              reads=[("ps", 0)], writes=[("ain_t", 0)])
        P.add("dve", lambda e, nch=nch, c0=c0: e.tensor_tensor(out=uprev[:, c0:c0 + nch, :],
                                                                in0=ain_t[0][:, 0:2 * nch].rearrange("p (n two) -> p n two", two=2),
                                                                in1=bank(X, 1)[:, 0:2 * nch].rearrange("p (n two) -> p n two", two=2),
                                                                op=ALU.mult),
              reads=[("ain_t", 0), ("ps", 1)], writes=[("uprev", c0 + i) for i in range(nch)])
    P.barrier()

    cnt = 0
    for t in range(cfg["NT"]):
        norm_tile(X, t, xin, xkey_fn, dr["norm_mix_pre"][l:l + 1, :], xs1, junk1, gb1, hT, "hT")
        P.barrier()
        for gi, c0 in enumerate(range(0, KC, 4)):
            nch = min(4, KC - c0)
            mm_ws(X, plan["ain"][gi], nch, 0, lambda k: hT[:, k, 0:512], lambda k: ("hT", k))
            for n in range(nch):
                P.add("act", lambda e, n=n: e.activation(out=ain_t[n], in_=bank(X, n), func=AF.Copy),
                      reads=[("ps", n)], writes=[("ain_t", n)])
            mm_ws(X, plan["ac"][gi], nch, 4, lambda k: hT[:, k, 0:512], lambda k: ("hT", k))
            for n in range(nch):
                c = c0 + n
                P.add("dve", lambda e, n=n, c=c: e.tensor_copy(out=uext[n][:, 0:2], in_=uprev[:, c, :]),
                      reads=[("uprev", c)], writes=[("uext", n)])
                P.add("dve", lambda e, n=n: e.tensor_tensor(out=uext[n][:, 2:514], in0=ain_t[n], in1=bank(X, 4 + n), op=ALU.mult),
                      reads=[("ain_t", n), ("ps", 4 + n)], writes=[("uext", n)])
                P.add("dve", lambda e, n=n, c=c: e.tensor_copy(out=uprev[:, c, :], in_=uext[n][:, 512:514]),
                      reads=[("uext", n)], writes=[("uprev", c)])
            mm_ws(X, plan["ab"][gi], nch, 0, lambda k: hT[:, k, 0:512], lambda k: ("hT", k))
            for n in range(nch):
                c = c0 + n
                cbv = cbuf[n % 2]
                conv3(P, uext[n], ("uext", n), X.cwa, c, cbv, ("cbuf", n % 2), ("cwa",))
                P.add("dve", lambda e, n=n, c=c, cbv=cbv: e.tensor_tensor(out=uaT[:, c, :], in0=cbv, in1=bank(X, n), op=ALU.mult),
                      reads=[("cbuf", n % 2), ("ps", n)], writes=[("uaT", c)])
        for gi, c0 in enumerate(range(0, 2 * H, 4)):
            b0 = 4 * (cnt % 2); cnt += 1
            mm_ws(X, plan["q"][gi], 4, b0, lambda k: hT[:, k, 0:512], lambda k: ("hT", k))
            for n in range(4):
                c = c0 + n
                if n % 2 == 0:
                    P.add("act", lambda e, c=c, n=n, b0=b0: e.activation(out=QT[:, c, :], in_=bank(X, b0 + n), func=AF.Copy),
                          reads=[("ps", b0 + n)], writes=[("QT", c)])
                else:
                    P.add("dve", lambda e, c=c, n=n, b0=b0: e.tensor_copy(out=QT[:, c, :], in_=bank(X, b0 + n)),
                          reads=[("ps", b0 + n)], writes=[("QT", c)])
        P.barrier()
        P.dma("sp", lambda e: e.dma_start(out=mm1, in_=dr["mm1"]), writes=[("mm1",)])
        iters = [(qg, h, r) for qg in range(2) for h in range(H) for r in range(NR)]

        def load_kv(i):
            if i >= len(iters):
                return
            qg, h, r = iters[i]
            bi = i % 2
            kh = h // HK; cl = 2 * (h % HK); vh = h // 2; h2 = h % 2
            for tk in range(cfg["NT"]):
                ksrc = dr["kT_all"][(tk, kh)].rearrange("(r c p) t -> r p c t", r=NR, p=128)[r, :, cl:cl + 2, :]
                P.dma("sp", lambda e, tk=tk, ksrc=ksrc: e.dma_start(out=kbuf[bi][:, :, tk * 512:(tk + 1) * 512], in_=ksrc),
                      reads=[("kT_all", tk, kh)], writes=[("kbuf", bi, tk)])
                vsrc = dr["v_all"][(tk, vh)].rearrange("(r h p) (k e) -> r p h k e", r=NR, p=128, e=VE)[r, :, h2, :, :]
                P.dma("sp", lambda e, tk=tk, vsrc=vsrc: e.dma_start(out=vbuf[bi][:, tk * 4:(tk + 1) * 4, :], in_=vsrc),
                      reads=[("v_all", tk, vh)], writes=[("vbuf", bi, tk)])

        blocks = []
        for i, (qg, h, r) in enumerate(iters):
            lg = 2 * t + qg
            todo = [j for j in range(NBr) if NBr * (NR - 1) + 2 * lg - (NBr * r + j) >= -1]
            for j in todo:
                blocks.append(dict(i=i, qg=qg, h=h, r=r, j=j, lg=lg,
                                   first=(r == 0 and j == todo[0]), last=(r == NR - 1 and j == todo[-1]),
                                   it_last=(j == todo[-1])))
        NBLK = len(blocks)
        LA = 2
        SB = (4, 5, 6)

        def rec_sc(n):
            b = blocks[n]
            sb = SB[n % 3]; bi = b["i"] % 2; j = b["j"]; h = b["h"]; q0 = b["qg"] * 256

            def sc(e, sb=sb, bi=bi, j=j, h=h, q0=q0):
                last = None
                for m in range(2):
                    last = e.matmul(bank(X, sb)[:, m * 256:(m + 1) * 256], kbuf[bi][:, m, j * 128:(j + 1) * 128],
                                    QT[:, 2 * h + m, q0:q0 + 256], start=True, stop=True)
                return last
            P.add("pe", sc, reads=[("kbuf", bi, j // 4), ("QT", 2 * h), ("QT", 2 * h + 1)], writes=[("ps", sb)])

        def rec_exp_av(n):
            b = blocks[n]
            sb = SB[n % 3]; bi = b["i"] % 2; j = b["j"]; h = b["h"]; r = b["r"]; lg = b["lg"]
            jg = NBr * r + j
            e_idx = 2 * lg - jg + cfg["EOFF"]
            d = jg - 2 * lg
            pdiag = d >= 0 and (d % NBr) in (0, 1)
            pi = n % 4
            bias = X.colb[:, h, e_idx:e_idx + 1]
            if not pdiag:
                P.add("act", lambda e: e.activation(out=PT[pi], in_=bank(X, sb), func=AF.Exp, bias=bias, scale=scale),
                      reads=[("ps", sb), ("colb",)], writes=[("PT", pi)])
            else:
                rc = d // NBr; typ = d % NBr
                ei = n % 2
                P.add("act", lambda e: e.activation(out=E32[ei], in_=bank(X, sb), func=AF.Exp, bias=bias, scale=scale),
                      reads=[("ps", sb), ("colb",)], writes=[("E32", ei)])
                P.add("dve", lambda e: e.tensor_scalar(out=mtp, in0=mm1[:, h, typ, :], scalar1=X.sel[:, rc:rc + 1],
                                                       scalar2=1.0, op0=ALU.mult, op1=ALU.add),
                      reads=[("mm1",), ("sel",)], writes=[("mtp",)])
                for m in range(2):
                    P.add("dve", lambda e, m=m: e.tensor_tensor(out=PT[pi][:, m * 256:(m + 1) * 256],
                                                                in0=E32[ei][:, m * 256:(m + 1) * 256], in1=mtp, op=ALU.mult),
                          reads=[("E32", ei), ("mtp",)], writes=[("PT", pi)])
            is_first = b["first"]; is_last = b["last"]

            def av(e):
                last = None
                for sub in range(2):
                    for m in range(2):
                        last = e.matmul(bank(X, sub * 2 + m)[:, 0:257], PT[pi][:, m * 256 + sub * 128:m * 256 + sub * 128 + 128],
                                        vbuf[bi][:, j, 0:257], start=is_first, stop=is_last)
                return last
            P.add("pe", av, reads=[("PT", pi), ("vbuf", bi, j // 4)], writes=[("ps", 0), ("ps", 1), ("ps", 2), ("ps", 3)])

        def rec_finalize(b):
            h = b["h"]; q0 = b["qg"] * 256
            fi = 0
            ob = obuf[fi]
            for q in range(4):
                if q % 2 == 0:
                    P.add("act", lambda e, q=q: e.activation(out=ob[:, q, 0:257], in_=bank(X, q)[:, 0:257], func=AF.Copy),
                          reads=[("ps", q)], writes=[("obuf", fi, q)])
                else:
                    P.add("dve", lambda e, q=q: e.tensor_copy(out=ob[:, q, 0:257], in_=bank(X, q)[:, 0:257]),
                          reads=[("ps", q)], writes=[("obuf", fi, q)])
            rd = X.rd
            for sub in range(2):
                O1 = ob[:, sub * 2, :]; O2 = ob[:, sub * 2 + 1, :]
                k1 = ("obuf", fi, sub * 2); k2 = ("obuf", fi, sub * 2 + 1)
                rds = rd[:, 2 * sub:2 * sub + 2]
                P.add("dve", lambda e, O1=O1, rds=rds: e.reciprocal(out=rds[:, 0:1], in_=O1[:, 256:257]), reads=[k1], writes=[("rd", sub)])
                P.add("dve", lambda e, O2=O2, rds=rds: e.reciprocal(out=rds[:, 1:2], in_=O2[:, 256:257]), reads=[k2], writes=[("rd", sub)])
                P.add("dve", lambda e, rds=rds: e.tensor_tensor(out=rds[:, 1:2], in0=rds[:, 1:2], in1=X.nlam, op=ALU.mult),
                      reads=[("rd", sub), ("nlam",)], writes=[("rd", sub)])
                P.add("dve", lambda e, O1=O1, rds=rds: e.tensor_scalar(out=O1[:, 0:256], in0=O1[:, 0:256], scalar1=rds[:, 0:1], scalar2=None, op0=ALU.mult),
                      reads=[k1, ("rd", sub)], writes=[k1])
                P.add("dve", lambda e, O1=O1, O2=O2, rds=rds: e.scalar_tensor_tensor(out=O1[:, 0:256], in0=O2[:, 0:256], scalar=rds[:, 1:2], in1=O1[:, 0:256],
                                                                                     op0=ALU.mult, op1=ALU.add),
                      reads=[k1, k2, ("rd", sub)], writes=[k1])
                jj = X.ssn[0] % 16
                X.ssn[0] += 1
                sv = X.ss[:, jj:jj + 1]
                P.add("dve", lambda e, sv=sv: e.memset(sv, 0.0), writes=[("ss", jj)])
                P.add("act", lambda e, O1=O1, sv=sv: e.activation(out=junk2, in_=O1[:, 0:256], func=AF.Square, accum_out=sv),
                      reads=[k1, ("ss", jj)], writes=[("junk2",), ("ss", jj)])
                rstd_ops(P, sv, ("ss", jj), 256)
                P.add("dve", lambda e, O1=O1, sv=sv: e.scalar_tensor_tensor(out=O1[:, 0:256], in0=O1[:, 0:256], scalar=sv, in1=X.sgb,
                                                                            op0=ALU.mult, op1=ALU.mult),
                      reads=[k1, ("ss", jj), ("sgb",)], writes=[k1])

            def deferred():
                pv = bank(X, 7).rearrange("p (a b) -> p a b", b=128)

                def tr(e):
                    last = None
                    for sub in range(2):
                        for ee in range(2):
                            last = e.transpose(pv[:, sub * 2 + ee, :], ob[:, sub * 2, ee * 128:(ee + 1) * 128], X.ident)
                    return last
                P.add("pe", tr, reads=[("obuf", fi, 0), ("obuf", fi, 2), ("ident",)], writes=[("ps", 7)])
                for sub in range(2):
                    c0 = q0 + sub * 128
                    P.add("act", lambda e, sub=sub, c0=c0: e.activation(out=attnT[:, 2 * h:2 * h + 2, c0:c0 + 128],
                                                                        in_=pv[:, 2 * sub:2 * sub + 2, :], func=AF.Copy),
                          reads=[("ps", 7)], writes=[("attnT", 2 * h), ("attnT", 2 * h + 1)])
            return deferred

        load_kv(0)
        load_kv(1)
        pending = []
        for n in range(min(LA, NBLK)):
            rec_sc(n)
        for n in range(NBLK):
            if n + LA < NBLK:
                rec_sc(n + LA)
            rec_exp_av(n)
            b = blocks[n]
            if b["it_last"]:
                load_kv(b["i"] + 2)
            if b["last"]:
                pending.append((n + 8, rec_finalize(b)))
            while pending and (pending[0][0] <= n or n == NBLK - 1):
                pending.pop(0)[1]()
        P.barrier()
        for og in range(KD // 4):
            mm_ws(X, plan["ga"][og], 4, 0, lambda k: hT[:, k, 0:512], lambda k: ("hT", k))
            for n in range(4):
                c = og * 4 + n
                P.add("act", lambda e, n=n, c=c: e.activation(out=ga_t[n], in_=bank(X, n), func=AF.Sigmoid, bias=X.bgT[:, 0, c:c + 1]),
                      reads=[("ps", n), ("bgT",)], writes=[("ga_t", n)])
            mm_ws(X, plan["ya"][og], 4, 4, lambda k: uaT[:, k, :], lambda k: ("uaT", k))
            for n in range(4):
                P.add("dve", lambda e, n=n: e.tensor_tensor(out=m_t[n], in0=ga_t[n], in1=bank(X, 4 + n), op=ALU.mult),
                      reads=[("ga_t", n), ("ps", 4 + n)], writes=[("m_t", n)])
            mm_ws(X, plan["gb"][og], 4, 0, lambda k: hT[:, k, 0:512], lambda k: ("hT", k))
            for n in range(4):
                c = KD + og * 4 + n
                P.add("act", lambda e, n=n, c=c: e.activation(out=ga_t[n], in_=bank(X, n), func=AF.Sigmoid, bias=X.bgT[:, 0, c:c + 1]),
                      reads=[("ps", n), ("bgT",)], writes=[("ga_t", n)])
            mm_ws(X, plan["yb"][og], 4, 4, lambda k: attnT[:, k, :], lambda k: ("attnT", k))
            for n in range(4):
                c = og * 4 + n
                P.add("dve", lambda e, n=n: e.tensor_tensor(out=ga_t[n], in0=ga_t[n], in1=bank(X, 4 + n), op=ALU.mult),
                      reads=[("ga_t", n), ("ps", 4 + n)], writes=[("ga_t", n)])
                P.add("dve", lambda e, n=n, c=c: e.tensor_tensor(out=mT[:, c, :], in0=m_t[n], in1=ga_t[n], op=ALU.add),
                      reads=[("ga_t", n), ("m_t", n)], writes=[("mT", c)])
        P.barrier()
        out_proj_post(X, t, plan["out"], mT, "mT", xin, xkey_fn, dr["norm_mix_post"][l:l + 1, :], xout, okey_fn,
                      mix, junk3, xs4, gb4, halo_out=(dr["mh_loc"], X.lockey("mh", 0)))


def ffn_plan(X, l):
    cfg = X.cfg; KD = cfg["KD"]; KF = cfg["KF"]; DFF = cfg["DFF"]; D = cfg["D"]
    w1v = wview(X.dr["w_ffn_in"][l]); w2v = wview(X.dr["w_ffn_out"][l])
    plan = {"gate": [], "up": [], "out": []}
    for c0 in range(0, KF, 4):
        n = min(4, KF - c0) * 128
        plan["gate"].append(wblocks(w1v, KD, c0 * 128, n))
        plan["up"].append(wblocks(w1v, KD, DFF + c0 * 128, n))
    for cb in range(D // 512):
        plan["out"].append(wblocks(w2v, KF, cb * 512, 512))
    seq = []
    for t in range(cfg["NT"]):
        for gi in range(len(plan["gate"])):
            seq.extend(plan["gate"][gi]); seq.extend(plan["up"][gi])
        for cb in range(D // 512):
            seq.extend(plan["out"][cb])
    return plan, seq


def ffn_phase(X, l, plan, xin, xkey_fn, xout, okey_fn):
    P = X.P; cfg = X.cfg; A = X.A; dr = X.dr
    D = cfg["D"]; KD = cfg["KD"]; KF = cfg["KF"]
    L = Layout(0, X.act_limit)
    oA = L.take(max(KD * 514 * 2, 4 * D * 4))
    h2T = A.view(oA, [KD, 514], BF16)
    res = A.view(oA, [4, D], F32)
    oB = L.take(max(KF * 512 * 2, 2 * D * 4 + D * 2 + D * 4))
    actT = A.view(oB, [KF, 512], BF16)
    xs = [A.view(oB + i * D * 4, [D], F32) for i in range(2)]
    junk = A.view(oB + 2 * D * 4, [D], BF16)
    gb = A.view(oB + 2 * D * 4 + D * 2, [D], F32)
    gext = [A.view(L.take(516 * 4), [516], F32) for _ in range(4)]
    cbuf = [A.view(L.take(512 * 4), [512], F32) for _ in range(2)]
    gprev = X.gprev
    load_featmajor(X, dr["conv_ffn"][l], 3, KF, X.fstage, X.cwf, "cwf")
    P.barrier()
    for t in range(cfg["NT"]):
        norm_tile(X, t, xin, xkey_fn, dr["norm_ffn_pre"][l:l + 1, :], xs, junk, gb, h2T, "h2T",
                  halo_src=(dr["mh_all"], ("mh_all",)) if t == 0 else None)
        P.barrier()
        for gi, c0 in enumerate(range(0, KF, 4)):
            nch = min(4, KF - c0)
            if t == 0:
                blk_list = plan["gate"][gi]
                nkb = len(blk_list)
                for kb, blk in enumerate(blk_list):
                    wv, wkey = X.ws.acquire(blk)
                    kc = blk.shape[1]
                    for n in range(nch):
                        def mm(e, wv=wv, kb=kb, kc=kc, n=n, nkb=nkb):
                            last = None
                            for k in range(kc):
                                last = e.matmul(bank(X, n), wv[:, k, n * 128:(n + 1) * 128], h2T[:, kb * 8 + k, 0:512],
                                                start=(kb == 0 and k == 0), stop=(kb == nkb - 1 and k == kc - 1))
                            return last
                        P.add("pe", mm, reads=[wkey] + [("h2T", kb * 8 + k) for k in range(kc)], writes=[("ps", n)])

                    def mmh(e, wv=wv, kb=kb, kc=kc, nch=nch, nkb=nkb):
                        last = None
                        for n in range(nch):
                            for k in range(kc):
                                last = e.matmul(bank(X, 4)[:, n * 2:n * 2 + 2], wv[:, k, n * 128:(n + 1) * 128],
                                                h2T[:, kb * 8 + k, 512:514],
                                                start=(kb == 0 and k == 0 and n == 0), stop=(kb == nkb - 1 and k == kc - 1),
                                                skip_group_check=True)
                        return last
                    P.add("pe", mmh, reads=[wkey] + [("h2T", kb * 8 + k) for k in range(kc)], writes=[("ps", 4)])
                    X.ws.release()
                for n in range(nch):
                    P.add("dve", lambda e, n=n: e.tensor_copy(out=gext[n][:, 0:2], in_=bank(X, 4)[:, n * 2:n * 2 + 2]),
                          reads=[("ps", 4)], writes=[("gext", n)])
            else:
                mm_ws(X, plan["gate"][gi], nch, 0, lambda k: h2T[:, k, 0:512], lambda k: ("h2T", k))
                for n in range(nch):
                    c = c0 + n
                    P.add("dve", lambda e, n=n, c=c: e.tensor_copy(out=gext[n][:, 0:2], in_=gprev[:, c, :]),
                          reads=[("gprev", c)], writes=[("gext", n)])
            mm_ws(X, plan["up"][gi], nch, 4, lambda k: h2T[:, k, 0:512], lambda k: ("h2T", k))
            for n in range(nch):
                c = c0 + n
                ge = gext[n]; cbv = cbuf[n % 2]
                P.add("act", lambda e, n=n, ge=ge: e.activation(out=ge[:, 2:514], in_=bank(X, n), func=AF.Copy),
                      reads=[("ps", n)], writes=[("gext", n)])
                P.add("dve", lambda e, c=c, ge=ge: e.tensor_copy(out=gprev[:, c, :], in_=ge[:, 512:514]),
                      reads=[("gext", n)], writes=[("gprev", c)])
                conv3(P, ge, ("gext", n), X.cwf, c, cbv, ("cbuf", n % 2), ("cwf",))
                P.add("act", lambda e, cbv=cbv: e.activation(out=cbv, in_=cbv, func=AF.Gelu_apprx_tanh),
                      reads=[("cbuf", n % 2)], writes=[("cbuf", n % 2)])
                P.add("dve", lambda e, c=c, n=n, cbv=cbv: e.tensor_tensor(out=actT[:, c, :], in0=cbv, in1=bank(X, 4 + n), op=ALU.mult),
                      reads=[("cbuf", n % 2), ("ps", 4 + n)], writes=[("actT", c)])
        P.barrier()
        out_proj_post(X, t, plan["out"], actT, "actT", xin, xkey_fn, dr["norm_ffn_post"][l:l + 1, :], xout, okey_fn,
                      res, X.junk3, xs, gb)


WEIGHT_SPECS = [
    ("w_in", lambda c: [c["L"], c["D"], c["DIN"]]), ("b_gate", lambda c: [c["L"], 2 * c["D"]]),
    ("conv_a", lambda c: [c["L"], 3, c["DC"]]), ("w_a_out", lambda c: [c["L"], c["DC"], c["D"]]),
    ("lam_q1", lambda c: [c["L"], 128]), ("lam_k1", lambda c: [c["L"], 128]),
    ("lam_q2", lambda c: [c["L"], 128]), ("lam_k2", lambda c: [c["L"], 128]),
    ("subln_g", lambda c: [c["L"], 256]), ("w_b_out", lambda c: [c["L"], c["DA"], c["D"]]),
    ("w_o", lambda c: [c["L"], c["D"], c["D"]]), ("norm_mix_pre", lambda c: [c["L"], c["D"]]),
    ("norm_mix_post", lambda c: [c["L"], c["D"]]), ("w_ffn_in", lambda c: [c["L"], c["D"], 2 * c["DFF"]]),
    ("conv_ffn", lambda c: [c["L"], 3, c["DFF"]]), ("w_ffn_out", lambda c: [c["L"], c["DFF"], c["D"]]),
    ("norm_ffn_pre", lambda c: [c["L"], c["D"]]), ("norm_ffn_post", lambda c: [c["L"], c["D"]]),
]


def build_program(cfg, mode="fused"):
    nc = bass.Bass("TRN2", target_bir_lowering=False)
    D = cfg["D"]; T = cfg["T"]; H = cfg["H"]; NR = cfg["NR"]; NBr = cfg["NBr"]; L = cfg["L"]
    X = Ctx()
    X.cfg = cfg; X.nc = nc
    dr = {}
    outs = set()

    def ext_in(name, shape, dt=F32):
        dr[name] = nc.dram_tensor(name, shape, dt, kind="ExternalInput").ap()

    def ext_out(name, shape, dt=F32):
        dr[name] = nc.dram_tensor(name, shape, dt, kind="ExternalOutput").ap()
        outs.add(name)

    def internal(name, shape, dt=F32):
        dr[name] = nc.dram_tensor(name, shape, dt).ap()

    for nm, shp in WEIGHT_SPECS:
        ext_in(nm, shp(cfg))
    for nm, shp in (("ident", [128, 128]), ("colb", [128, H, cfg["NE"]]), ("mm1", [128, H, 2, 256]),
                    ("sel", [128, NR]), ("sel2", [128, NR]), ("laminit", [128, 2 * L])):
        ext_in(nm, shp)
    ext_in("x", [T, D])
    HK = cfg["HK"]; VE = cfg["VE"]; NT = cfg["NT"]
    dr["kT_loc"] = {}; dr["kT_all"] = {}; dr["v_loc"] = {}; dr["v_all"] = {}
    kv_list = [("xh", None, [2, D], F32), ("mh", None, [2, D], F32)]
    for t in range(NT):
        for kh in range(H // HK):
            kv_list.append(("kT", (t, kh), [2 * HK * 128, 512], BF16))
        for vh in range(H // 2):
            kv_list.append(("v", (t, vh), [2 * 128, 4 * VE], BF16))
    for nm, idx, shp, dt in kv_list:
        sfx = "" if idx is None else "_%d_%d" % idx
        loc, al = nm + "_loc" + sfx, nm + "_all" + sfx
        shp_all = [NR * shp[0], shp[1]]
        if mode == "fused":
            internal(loc, shp, dt); internal(al, shp_all, dt)
        else:
            producer = "B" if nm == "mh" else "A"
            consumer = "C" if nm == "mh" else "B"
            if mode == producer:
                ext_out(loc, shp, dt)
            else:
                internal(loc, shp, dt)
            if mode == consumer:
                ext_in(al, shp_all, dt)
            else:
                internal(al, shp_all, dt)
        if idx is not None:
            dr[nm + "_loc"][idx] = dr.pop(loc); dr[nm + "_all"][idx] = dr.pop(al)
    if mode == "fused":
        internal("xmid", [T, D]); internal("x2", [T, D]); ext_out("out", [T, D])
    elif mode == "A":
        pass
    else:
        ext_out("out", [T, D])
    X.dr = dr
    X.is_out = lambda ap: ap.tensor.name in outs

    with ExitStack() as stack:
        A = Arena(nc, ARENA_BYTES)
        X.A = A
        X.PS = nc.alloc_psum_tensor("ps", [128, 4096], F32).ap()
        P = Prog(nc)
        X.P = P
        top = Layout(0)
        sizes = []
        R = cfg["ring"]
        Lp = Layout(ARENA_BYTES - (R * 8192 + 10240), ARENA_BYTES)
        X.act_limit = Lp.off
        wviews = [A.view(Lp.take(8 * 512 * 2), [8, 512], BF16) for _ in range(R)]
        X.ident = A.view(Lp.take(512), [128], F32)
        X.sel = A.view(Lp.take(NR * 4), [NR], F32)
        X.sel2 = A.view(Lp.take(NR * 4), [NR], F32)
        lam2 = A.view(Lp.take(2 * L * 4), [2 * L], F32)
        X.laminit = lam2[:, 0:L]
        X.omlam = lam2[:, L:2 * L]
        X.colb = A.view(Lp.take(H * cfg["NE"] * 4), [H, cfg["NE"]], F32)
        X.ss = A.view(Lp.take(64), [16], F32)
        X.ssq = A.view(Lp.take(4 * 8 * 4), [4, 8], F32)
        X.rd = A.view(Lp.take(16), [4], F32)
        X.fin_rr = [0]
        X.lsum = A.view(Lp.take(8), [2], F32)
        X.nlam = A.view(Lp.take(4), [1], F32)
        X.sgb = A.view(Lp.take(1024), [256], F32)
        X.junk3 = A.view(Lp.take(1024), [512], BF16)
        X.fstage = A.view(Lp.take(3 * 128 * 4), [3, 128], F32)
        X.cwa = A.view(Lp.take(3 * cfg["KC"] * 4), [3, cfg["KC"]], F32)
        X.cwf = A.view(Lp.take(3 * cfg["KF"] * 4), [3, cfg["KF"]], F32)
        X.bgT = A.view(Lp.take(2 * cfg["KD"] * 4), [1, 2 * cfg["KD"]], F32)
        X.uprev = A.view(Lp.take(cfg["KC"] * 2 * 4), [cfg["KC"], 2], F32)
        X.gprev = A.view(Lp.take(cfg["KF"] * 2 * 4), [cfg["KF"], 2], F32)
        X.ssn = [0]
        X.bank_rr = [0]
        X.loc_keys = {}

        def lockey(nm, tag):
            k = ("loc", nm, tag)
            X.loc_keys.setdefault(nm, []).append(k)
            return k
        X.lockey = lockey
        P.dma("sp", lambda e: e.dma_start(out=X.ident, in_=dr["ident"]), writes=[("ident",)])
        P.dma("sp", lambda e: e.dma_start(out=X.sel, in_=dr["sel"]), writes=[("sel",)])
        P.dma("sp", lambda e: e.dma_start(out=X.sel2, in_=dr["sel2"]), writes=[("sel",)])
        P.dma("sp", lambda e: e.dma_start(out=lam2, in_=dr["laminit"]), writes=[("laminit",)])
        P.dma("sp", lambda e: e.dma_start(out=X.colb, in_=dr["colb"]), writes=[("colb",)])

        ws = WStream(P, wviews)
        X.ws = ws
        plans = []
        if mode == "fused":
            for l in range(L):
                pa, sa = kv_plan(X, l); pb, sb = mixer_plan(X, l); pc, sc = ffn_plan(X, l)
                ws.plan(sa); ws.plan(sb); ws.plan(sc)
                plans.append((pa, pb, pc))
        else:
            pl, sq = {"A": kv_plan, "B": mixer_plan, "C": ffn_plan}[mode](X, 0)
            ws.plan(sq)
        ws.start()
        groups = [[b * NR + r for r in range(NR)] for b in range(cfg["NB"])]

        def allgather(nm, idx=None):
            loc, al = dr[nm + "_loc"], dr[nm + "_all"]
            kk = nm
            akey = (nm + "_all",)
            if idx is not None:
                loc, al = loc[idx], al[idx]
                kk = (nm,) + idx
                akey = (nm + "_all",) + idx
            op = P.cc(lambda e: e.collective_compute("AllGather", ALU.bypass, replica_groups=groups, ins=[loc.opt()], outs=[al.opt()]),
                      reads=list(X.loc_keys.get(kk, [])), writes=[akey])
            X.loc_keys[kk] = []

        def after_kv_tile(t):
            for kh in range(H // HK):
                allgather("kT", (t, kh))
            for vh in range(H // 2):
                allgather("v", (t, vh))
        X.after_kv_tile = after_kv_tile if mode == "fused" else None
        X.allgather = allgather

        if mode == "fused":
            for l in range(L):
                xin = dr["x"] if l == 0 else dr["x2"]
                xk = (lambda t: ("x_in", t)) if l == 0 else (lambda t: ("x2", t))
                pa, pb, pc = plans[l]
                kv_phase(X, l, pa, xin, xk)
                import os
                stop = os.environ.get("FUSED_STOP")
                if stop == "A0":
                    break
                if stop == "A":
                    break
                mixer_phase(X, l, pb, xin, xk, dr["xmid"], lambda t: ("xmid", t))
                P.barrier()
                if stop == "B0":
                    break
                allgather("mh")
                if stop == "B":
                    break
                xo = dr["x2"] if l < L - 1 else dr["out"]
                ok = (lambda t: ("x2", t)) if l < L - 1 else (lambda t: ("out", t))
                ffn_phase(X, l, pc, dr["xmid"], lambda t: ("xmid", t), xo, ok)
                P.barrier()
        elif mode == "A":
            kv_phase(X, 0, pl, dr["x"], lambda t: ("x_in", t))
        elif mode == "B":
            mixer_phase(X, 0, pl, dr["x"], lambda t: ("x_in", t), dr["out"], lambda t: ("out", t))
        else:
            ffn_phase(X, 0, pl, dr["x"], lambda t: ("x_in", t), dr["out"], lambda t: ("out", t))
        import os
        assert os.environ.get("FUSED_STOP") or ws.next_use == len(ws.blocks), (ws.next_use, len(ws.blocks))
        P.check()
        P.emit(stack)
    return nc


def core_tables(cfg, r):
    H = cfg["H"]; NR = cfg["NR"]; NBr = cfg["NBr"]; NE = cfg["NE"]; EOFF = cfg["EOFF"]; L = cfg["L"]
    slopes = np.asarray(cfg["slopes"], np.float64)
    ki = np.arange(128, dtype=np.float64)
    colb = np.empty((128, H, NE), np.float32)
    for e in range(NE):
        delta = NBr * r + (e - EOFF)
        for h in range(H):
            if delta >= -1:
                colb[:, h, e] = -slopes[h] * (128.0 * delta + 128.0 - ki)
            else:
                colb[:, h, e] = -1e30
    mm1 = np.empty((128, H, 2, 256), np.float32)
    qi = np.arange(256)
    for typ in range(2):
        kk = 128 * typ + np.arange(128)
        vis = kk[:, None] <= qi[None, :]
        same = (kk[:, None] // 64) == (qi[None, :] // 64)
        dist = (kk[:, None] - qi[None, :]).astype(np.float64)
        for h in range(H):
            m = np.where(vis, 1.0, np.where(same, np.exp(-2.0 * slopes[h] * np.maximum(dist, 0.0)), 0.0))
            mm1[:, h, typ, :] = m - 1.0
    sel = np.zeros((128, NR), np.float32); sel[:, r] = 1.0
    sel2 = np.zeros((128, NR), np.float32)
    if r > 0:
        sel2[:, r - 1] = 1.0
    lam = np.zeros((128, 2 * L), np.float32)
    for l in range(L):
        lam[:, l] = cfg["lam_init"][l + cfg.get("layer0", 0)]
        lam[:, L + l] = 1.0 - cfg["lam_init"][l + cfg.get("layer0", 0)]
    return dict(ident=np.eye(128, dtype=np.float32), colb=colb, mm1=mm1, sel=sel, sel2=sel2, laminit=lam)


_CACHE = {}


def run_fused(cfg, inputs):
    key = ("fused", cfg["D"], cfg["H"], cfg["DFF"], cfg["T"], cfg["L"])
    if key not in _CACHE:
        _CACHE[key] = build_program(cfg, "fused")
    nc = _CACHE[key]
    NR = cfg["NR"]; NB = cfg["NB"]; T = cfg["T"]
    x = np.ascontiguousarray(np.asarray(inputs["x"], dtype=np.float32))
    w = {nm: np.ascontiguousarray(np.asarray(inputs[nm], dtype=np.float32)) for nm, _ in WEIGHT_SPECS}
    in_maps = []
    for c in range(NB * NR):
        b, r = divmod(c, NR)
        m = dict(w)
        m.update(core_tables(cfg, r))
        m["x"] = np.ascontiguousarray(x[b, r * T:(r + 1) * T, :])
        in_maps.append(m)
    res = run_bass_kernel_spmd(nc, in_maps, core_ids=list(range(NB * NR)))
    out = np.empty_like(x)
    for c in range(NB * NR):
        b, r = divmod(c, NR)
        out[b, r * T:(r + 1) * T, :] = res.results[c]["out"]
    return out


def kernel(**inputs):
    cfg = make_cfg()
    return run_fused(cfg, inputs)
```

```python
import math
from contextlib import ExitStack

import numpy as np
import concourse.bass as bass
import concourse.mybir as mybir
from concourse.bass_utils import run_bass_kernel_spmd

F32 = mybir.dt.float32
BF16 = mybir.dt.bfloat16
U8 = mybir.dt.uint8
AF = mybir.ActivationFunctionType
ALU = mybir.AluOpType
AX = mybir.AxisListType

EPS = 1e-6
COMPUTE = ("pe", "act", "dve", "pool")
QUEUES = ("sp", "act", "pool")
ARENA_BYTES = 207 * 1024


class Op:
    __slots__ = ("stream", "fn", "kind", "cs", "deps", "sig", "tick", "idx")


class Prog:
    def __init__(self, nc, n_lanes=8, same_engine_sync=True):
        self.nc = nc
        self.ops = []
        self.n_lanes = n_lanes
        self.same_engine_sync = same_engine_sync
        self.lane_rr = {q: 0 for q in QUEUES}
        self.lane_last = {}
        self.lane_cnt = {}
        self.reg_w = {}
        self.reg_r = {}
        self.last_cs = {}
        self.pending_barrier = {}
        self.outputs = []
        self.cc_cnt = 0

    def _record(self, op, reads, writes):
        psr = [k for k in reads if k[0] == "ps"]
        if psr:
            reads = [k for k in reads if k[0] != "ps"]
            writes = list(writes) + psr
        deps = {}
        for k in reads:
            w = self.reg_w.get(k)
            if w is not None:
                deps[w.idx] = w
        for k in writes:
            w = self.reg_w.get(k)
            if w is not None:
                deps[w.idx] = w
            for r in self.reg_r.get(k, {}).values():
                deps[r.idx] = r
        pb = self.pending_barrier.pop(op.stream, None)
        if pb:
            for d in pb:
                deps[d.idx] = d
        if op.kind == "dma":
            prev = self.lane_last.get(op.cs)
            if prev is not None:
                deps[prev.idx] = prev
        out = []
        for d in deps.values():
            if d.kind == "cmp" and d.stream == op.stream and op.kind == "cmp":
                if op.stream == "pe" or not self.same_engine_sync:
                    continue
            d.sig = True
            out.append(d)
        op.deps = out
        op.idx = len(self.ops)
        self.ops.append(op)
        for k in reads:
            self.reg_r.setdefault(k, {})[op.cs] = op
        for k in writes:
            self.reg_w[k] = op
            self.reg_r[k] = {}
        self.last_cs[op.cs] = op
        if op.kind != "cmp":
            self.lane_last[op.cs] = op
        return op

    def add(self, stream, fn, reads=(), writes=()):
        op = Op()
        op.stream = stream; op.fn = fn; op.kind = "cmp"; op.cs = stream
        op.sig = False; op.tick = None
        return self._record(op, reads, writes)

    def dma(self, queue, fn, reads=(), writes=(), output=False):
        op = Op()
        lane = "%s_l%d" % (queue, self.lane_rr[queue])
        self.lane_rr[queue] = (self.lane_rr[queue] + 1) % self.n_lanes
        op.stream = queue; op.fn = fn; op.kind = "dma"; op.cs = lane
        op.sig = True
        self.lane_cnt[lane] = self.lane_cnt.get(lane, 0) + 1
        op.tick = 16 * self.lane_cnt[lane]
        self._record(op, reads, writes)
        if output:
            self.outputs.append(op)
        return op

    def cc(self, fn, reads=(), writes=()):
        op = Op()
        op.stream = "pool"; op.fn = fn; op.kind = "cc"; op.cs = "cc"
        op.sig = True
        self.cc_cnt += 1
        op.tick = self.cc_cnt
        return self._record(op, reads, writes)

    def barrier(self):
        lst = [op for cs, op in self.last_cs.items() if cs != "cc"]
        for s in ("pe", "act", "dve", "pool", "sp"):
            self.pending_barrier[s] = list(lst)

    def _assign(self):
        cnt = {s: 0 for s in COMPUTE}
        for op in self.ops:
            if op.kind == "cmp" and op.sig:
                cnt[op.stream] += 1
                op.tick = cnt[op.stream]

    def check(self):
        self._assign()
        streams = {}
        for op in self.ops:
            streams.setdefault(op.stream, []).append(op)
        pos = {s: 0 for s in streams}
        val = {}
        progress = True
        while progress:
            progress = False
            for s, lst in streams.items():
                while pos[s] < len(lst):
                    op = lst[pos[s]]
                    if not all(val.get(d.cs, 0) >= d.tick for d in op.deps):
                        break
                    if op.kind == "dma":
                        val[op.cs] = val.get(op.cs, 0) + 16
                        assert val[op.cs] == op.tick
                    elif op.kind == "cc" or op.sig:
                        val[op.cs] = val.get(op.cs, 0) + 1
                        assert val[op.cs] == op.tick
                    pos[s] += 1
                    progress = True
        stuck = {s: (pos[s], len(lst)) for s, lst in streams.items() if pos[s] < len(lst)}
        if stuck:
            raise RuntimeError("deadlock in generated program %s" % stuck)

    def emit(self, stack):
        nc = self.nc
        self._assign()
        sems = {}
        for s in COMPUTE:
            sems[s] = stack.enter_context(nc.semaphore("s_" + s))
        for lane in self.lane_cnt:
            sems[lane] = stack.enter_context(nc.semaphore("s_" + lane))
        if self.cc_cnt:
            sems["cc"] = stack.enter_context(nc.semaphore("s_cc"))
        fin = Op(); fin.stream = "sp"; fin.kind = "fin"; fin.cs = "sp"; fin.sig = False
        fin.deps = list(self.outputs); fin.fn = None; fin.idx = len(self.ops)
        by_stream = {s: [] for s in ("pe", "act", "dve", "pool", "sp")}
        for op in self.ops + [fin]:
            by_stream[op.stream].append(op)
        block = stack.enter_context(nc.Block())

        def run_stream(eng, lst):
            seen = {}
            for op in lst:
                need = {}
                for d in op.deps:
                    if need.get(d.cs, 0) < d.tick:
                        need[d.cs] = d.tick
                for sname, t in need.items():
                    if seen.get(sname, 0) >= t:
                        continue
                    eng.wait_ge(sems[sname], t)
                    seen[sname] = t
                if op.fn is None:
                    continue
                ins = op.fn(eng)
                if op.kind == "dma":
                    ins.then_inc(sems[op.cs], 16)
                elif op.kind == "cc":
                    ins.then_inc(sems[op.cs])
                elif op.sig:
                    ins.then_inc(sems[op.cs], 1)

        @block.tensor
        def _(e):
            run_stream(e, by_stream["pe"])

        @block.scalar
        def _(e):
            run_stream(e, by_stream["act"])

        @block.vector
        def _(e):
            run_stream(e, by_stream["dve"])

        @block.gpsimd
        def _(e):
            run_stream(e, by_stream["pool"])

        @block.sync
        def _(e):
            run_stream(e, by_stream["sp"])


class Arena:
    def __init__(self, nc, nbytes):
        self.ap = nc.alloc_sbuf_tensor("arena", [128, nbytes], U8).ap()
        self.nbytes = nbytes

    def view(self, off, shape, dt):
        esz = {F32: 4, BF16: 2, U8: 1}[dt]
        n = int(np.prod(shape))
        assert off % 4 == 0 and off + n * esz <= self.nbytes, (off, n * esz, self.nbytes)
        v = self.ap[:, off:off + n * esz]
        if dt != U8:
            v = v.bitcast(dt)
        if len(shape) == 2:
            v = v.rearrange("p (a b) -> p a b", b=shape[1])
        elif len(shape) == 3:
            v = v.rearrange("p (a b c) -> p a b c", b=shape[1], c=shape[2])
        return v


class Layout:
    def __init__(self, base=0, limit=None):
        self.off = base
        self.limit = limit

    def take(self, nbytes):
        o = self.off
        self.off += (nbytes + 31) // 32 * 32
        if self.limit is not None:
            assert self.off <= self.limit, ("SBUF layout overflow", self.off, self.limit)
        return o


class WStream:
    def __init__(self, P, views, queue="pool"):
        self.P = P; self.views = views; self.R = len(views); self.queue = queue
        self.blocks = []; self.next_issue = 0; self.next_use = 0

    def plan(self, blocks):
        self.blocks.extend(blocks)

    def _issue(self):
        j = self.next_issue
        if j >= len(self.blocks):
            return
        self.next_issue += 1
        src = self.blocks[j]
        slot = j % self.R
        dst = self.views[slot][:, 0:src.shape[1], 0:src.shape[2]]
        self.P.dma(self.queue, lambda e: e.dma_start(out=dst, in_=src), writes=[("W", slot)])

    def start(self):
        for _ in range(self.R):
            self._issue()

    def acquire(self, expect):
        j = self.next_use
        assert j < self.next_issue and self.blocks[j] is expect, "weight plan mismatch at block %d" % j
        slot = j % self.R
        return self.views[slot], ("W", slot)

    def release(self):
        self.next_use += 1
        self._issue()


def wblocks(wv, k_chunks, c0, ncols):
    return [wv[:, kb:kb + min(8, k_chunks - kb), c0:c0 + ncols] for kb in range(0, k_chunks, 8)]


def wview(ap2d):
    return ap2d.rearrange("(kc p) n -> p kc n", p=128)


def make_cfg(D=4096, H=8, DFF=11008, T=2048, NR=4, NB=2, L=2, slopes=None, ring=4):
    c = dict(D=D, H=H, DFF=DFF, T=T, NR=NR, NB=NB, L=L, ring=ring)
    c["DC"] = D // 2
    c["KD"] = D // 128
    c["KC"] = c["DC"] // 128
    c["KF"] = DFF // 128
    c["DQK"] = H * 256
    c["DA"] = H * 256
    c["KA"] = c["DA"] // 128
    c["DIN"] = 3 * c["DC"] + 2 * c["DQK"] + c["DA"] + 2 * D
    c["o_ain"] = 0
    c["o_ab"] = c["DC"]
    c["o_ac"] = 2 * c["DC"]
    c["o_q"] = 3 * c["DC"]
    c["o_k"] = c["o_q"] + c["DQK"]
    c["o_v"] = c["o_k"] + c["DQK"]
    c["o_g"] = c["o_v"] + c["DA"]
    c["VE"] = 264
    c["HK"] = min(H, 4)
    c["NT"] = T // 512
    c["NBr"] = T // 128
    c["EOFF"] = NR * c["NBr"] - 1
    c["NE"] = c["NBr"] * (NR + 1) - 2
    if slopes is None:
        slopes = [2.0 ** (-8.0 * (i + 1) / H) for i in range(H)]
    c["slopes"] = slopes
    c["lam_init"] = [0.8 - 0.6 * math.exp(-0.3 * l) for l in range(8)]
    assert D % 512 == 0 and T % 512 == 0 and c["DC"] % 128 == 0 and DFF % 128 == 0
    return c


class Ctx:
    pass


def bank(X, b):
    return X.PS[:, b * 512:(b + 1) * 512]


def load_featmajor(X, src_rows, nk, C, stage_view, dst_view, name):
    P = X.P
    src = src_rows.rearrange("k (c p) -> c k p", p=128)
    P.dma("sp", lambda e: e.dma_start(out=stage_view[0:C, 0:nk, :], in_=src), writes=[("fstage",)])
    for k in range(nk):
        pv = bank(X, 7)[:, 0:C]
        P.add("pe", lambda e, k=k, pv=pv: e.transpose(pv, stage_view[0:C, k, :], X.ident[0:C, 0:C]),
              reads=[("fstage",), ("ident",)], writes=[("ps", 7)])
        P.add("dve", lambda e, k=k, pv=pv: e.tensor_copy(out=dst_view[:, k, :], in_=pv),
              reads=[("ps", 7)], writes=[(name,)])


def rstd_ops(P, sv, key, n):
    P.add("dve", lambda e: e.tensor_scalar(out=sv, in0=sv, scalar1=1.0 / n, scalar2=EPS, op0=ALU.mult, op1=ALU.add),
          reads=[key], writes=[key])
    P.add("act", lambda e: e.activation(out=sv, in_=sv, func=AF.Sqrt), reads=[key], writes=[key])
    P.add("dve", lambda e: e.reciprocal(out=sv, in_=sv), reads=[key], writes=[key])


def rms_rows(X, xs_v, xkey, junk_v, rows, D, gb_v, gbkey):
    P = X.P
    j = X.ssn[0] % 16
    X.ssn[0] += 1
    ss_v = X.ss[:, j:j + 1]
    sskey = ("ss", j)
    P.add("dve", lambda e: e.memset(ss_v[0:rows, :], 0.0), writes=[sskey])
    P.add("act", lambda e: e.activation(out=junk_v[0:rows, :], in_=xs_v[0:rows, :], func=AF.Square,
                                        accum_out=ss_v[0:rows, :]),
          reads=[xkey, sskey], writes=[("junk",), sskey])
    rstd_ops(P, ss_v[0:rows, :], sskey, D)
    P.add("dve", lambda e: e.scalar_tensor_tensor(out=xs_v[0:rows, :], in0=xs_v[0:rows, :], scalar=ss_v[0:rows, 0:1],
                                                  in1=gb_v[0:rows, :], op0=ALU.mult, op1=ALU.mult),
          reads=[xkey, sskey, gbkey], writes=[xkey])


def transpose_rows(X, xs_v, xkey, rows, KD, hT, hname, col0):
    P = X.P
    for c4 in range(0, KD, 4):
        n = min(4, KD - c4)
        b = X.bank_rr[0] % 8
        X.bank_rr[0] += 1
        pv = bank(X, b).rearrange("p (a b) -> p a b", b=128)

        def tr(e, c4=c4, n=n, pv=pv):
            last = None
            for i in range(n):
                last = e.transpose(pv[:, i, 0:rows], xs_v[0:rows, (c4 + i) * 128:(c4 + i + 1) * 128],
                                   X.ident[0:rows, 0:rows])
            return last
        P.add("pe", tr, reads=[xkey, ("ident",)], writes=[("ps", b)])
        dst = hT[:, c4:c4 + n, col0:col0 + rows]
        srcv = pv[:, 0:n, 0:rows]
        wk = [(hname, c4 + i) for i in range(n)]
        if (c4 // 4) % 2 == 0:
            P.add("act", lambda e, dst=dst, srcv=srcv: e.activation(out=dst, in_=srcv, func=AF.Copy),
                  reads=[("ps", b)], writes=wk)
        else:
            P.add("dve", lambda e, dst=dst, srcv=srcv: e.tensor_copy(out=dst, in_=srcv),
                  reads=[("ps", b)], writes=wk)


def load_gb(X, gb, src_row):
    X.P.dma("sp", lambda e: e.dma_start(out=gb, in_=src_row.partition_broadcast(128)), writes=[("gb",)])


def norm_stage_a(X, t, s, xin, xkey_fn, xs, junk, gb):
    P = X.P
    b = s % 2
    r0 = t * 512 + s * 128
    xv = xs[b]
    P.dma("sp", lambda e: e.dma_start(out=xv, in_=xin[r0:r0 + 128, :]), reads=[xkey_fn(t)], writes=[("xs", b)])
    rms_rows(X, xv, ("xs", b), junk, 128, X.cfg["D"], gb, ("gb",))


def norm_stage_b(X, s, xs, hT, hname):
    b = s % 2
    transpose_rows(X, xs[b], ("xs", b), 128, X.cfg["KD"], hT, hname, s * 128)


def norm_tile(X, t, xin, xkey_fn, gsrc, xs, junk, gb, hT, hname, halo_src=None):
    P = X.P; cfg = X.cfg; D = cfg["D"]; KD = cfg["KD"]
    load_gb(X, gb, gsrc)

    def stage_a(s):
        b = s % 2
        r0 = t * 512 + s * 128
        xv = xs[b]
        P.dma("sp", lambda e, xv=xv, r0=r0: e.dma_start(out=xv, in_=xin[r0:r0 + 128, :]),
              reads=[xkey_fn(t)], writes=[("xs", b)])
        rms_rows(X, xv, ("xs", b), junk, 128, D, gb, ("gb",))

    def stage_b(s):
        b = s % 2
        transpose_rows(X, xs[b], ("xs", b), 128, KD, hT, hname, s * 128)
    stage_a(0); stage_a(1); stage_b(0); stage_a(2); stage_b(1); stage_a(3); stage_b(2); stage_b(3)
    if halo_src is not None:
        halo_rows(X, halo_src[0], halo_src[1], xs[0], ("xs", 0), xs[1], ("xs", 1), junk, gb, hT, hname)


def halo_rows(X, hall, hkey, xv, xkey, tmp4, tkey, junk, gb, hT, hname):
    P = X.P; cfg = X.cfg; D = cfg["D"]; NR = cfg["NR"]
    src = hall.rearrange("(r two) d -> two r d", two=2)
    import os
    if os.environ.get("HALO_DIRECT"):
        P.dma("sp", lambda e: e.dma_start(out=xv[0:2, :], in_=hall[0:2, :]), reads=[hkey], writes=[xkey])
        rms_rows(X, xv, xkey, junk, 2, D, gb, ("gb",))
        transpose_rows(X, xv, xkey, 2, cfg["KD"], hT, hname, 512)
        return
    t4 = tmp4[0:2, 0:NR * D].rearrange("p (r d) -> p r d", d=D) if NR * D <= tmp4.shape[1] else None
    for r in range(NR):
        P.dma("sp", lambda e, r=r: e.dma_start(out=tmp4[0:2, 0:D], in_=src[:, r, :]), reads=[hkey], writes=[tkey])
        if r == 0:
            P.add("dve", lambda e, r=r: e.tensor_scalar(out=xv[0:2, :], in0=tmp4[0:2, 0:D], scalar1=X.sel2[0:2, r:r + 1],
                                                         scalar2=None, op0=ALU.mult),
                  reads=[tkey, ("sel",)], writes=[xkey])
        else:
            P.add("dve", lambda e, r=r: e.scalar_tensor_tensor(out=xv[0:2, :], in0=tmp4[0:2, 0:D], scalar=X.sel2[0:2, r:r + 1],
                                                                in1=xv[0:2, :], op0=ALU.mult, op1=ALU.add),
                  reads=[tkey, ("sel",), xkey], writes=[xkey])
    rms_rows(X, xv, xkey, junk, 2, D, gb, ("gb",))
    transpose_rows(X, xv, xkey, 2, cfg["KD"], hT, hname, 512)


def mm_ws(X, blk_list, nch, bank0, rhs_fn, rkeys_fn, ncols=512, col0=0):
    P = X.P
    nkb = len(blk_list)
    for kb, blk in enumerate(blk_list):
        wv, wkey = X.ws.acquire(blk)
        kc = blk.shape[1]
        for n in range(nch):
            def mm(e, wv=wv, kb=kb, kc=kc, n=n):
                last = None
                for k in range(kc):
                    last = e.matmul(bank(X, bank0 + n)[:, col0:col0 + ncols], wv[:, k, n * 128:(n + 1) * 128],
                                    rhs_fn(kb * 8 + k),
                                    start=(kb == 0 and k == 0), stop=(kb == nkb - 1 and k == kc - 1))
                return last
            P.add("pe", mm, reads=[wkey] + [rkeys_fn(kb * 8 + k) for k in range(kc)], writes=[("ps", bank0 + n)])
        X.ws.release()


def mm_ws_halo(X, blk_list, nch, b, rhs_fn, rkeys_fn, first):
    P = X.P
    nkb = len(blk_list)
    for kb, blk in enumerate(blk_list):
        wv, wkey = X.ws.acquire(blk)
        kc = blk.shape[1]

        def mm(e, wv=wv, kb=kb, kc=kc):
            last = None
            for n in range(nch):
                for k in range(kc):
                    last = e.matmul(bank(X, b)[:, n * 2:n * 2 + 2], wv[:, k, n * 128:(n + 1) * 128], rhs_fn(kb * 8 + k),
                                    start=(kb == 0 and k == 0 and n == 0), stop=(kb == nkb - 1 and k == kc - 1),
                                    skip_group_check=True)
            return last
        P.add("pe", mm, reads=[wkey] + [rkeys_fn(kb * 8 + k) for k in range(kc)], writes=[("ps", b)])
        X.ws.release()


def mm_as(X, blk_list, bank0, lhs_fn, lkeys_fn):
    P = X.P
    nkb = len(blk_list)
    for kb, blk in enumerate(blk_list):
        wv, wkey = X.ws.acquire(blk)
        kc = blk.shape[1]
        ncol = blk.shape[2]
        for s in range(4):
            def mm(e, wv=wv, kb=kb, kc=kc, s=s, ncol=ncol):
                last = None
                for k in range(kc):
                    last = e.matmul(bank(X, bank0 + s)[:, 0:ncol], lhs_fn(kb * 8 + k, s), wv[:, k, 0:ncol],
                                    start=(kb == 0 and k == 0), stop=(kb == nkb - 1 and k == kc - 1))
                return last
            P.add("pe", mm, reads=[wkey] + [lkeys_fn(kb * 8 + k) for k in range(kc)], writes=[("ps", bank0 + s)])
        X.ws.release()


def conv3(P, ext, ekey, cw, c, cbv, ckey, wkey):
    P.add("dve", lambda e: e.tensor_scalar(out=cbv, in0=ext[:, 0:512], scalar1=cw[:, 0, c:c + 1], scalar2=None, op0=ALU.mult),
          reads=[ekey, wkey], writes=[ckey])
    for kk in (1, 2):
        P.add("dve", lambda e, kk=kk: e.scalar_tensor_tensor(out=cbv, in0=ext[:, kk:kk + 512], scalar=cw[:, kk, c:c + 1],
                                                              in1=cbv, op0=ALU.mult, op1=ALU.add),
              reads=[ekey, wkey, ckey], writes=[ckey])


def out_proj_post(X, t, plan_out, actT, aname, xin, xkey_fn, gsrc, xout, okey_fn, res, junk3, xs, gb, halo_out=None):
    P = X.P; cfg = X.cfg; D = cfg["D"]
    ssq = X.ssq
    P.add("dve", lambda e: e.memset(ssq, 0.0), writes=[("ssq",)])
    for cb in range(D // 512):
        b0 = 4 * (cb % 2)
        mm_as(X, plan_out[cb], b0, lambda k, s: actT[:, k, s * 128:(s + 1) * 128], lambda k: (aname, k))
        for s in range(4):
            P.add("dve", lambda e, s=s, cb=cb, b0=b0: e.tensor_copy(out=res[:, s, cb * 512:(cb + 1) * 512], in_=bank(X, b0 + s)),
                  reads=[("ps", b0 + s)], writes=[("res", s, cb)])
            P.add("act", lambda e, s=s, cb=cb: e.activation(out=junk3, in_=res[:, s, cb * 512:(cb + 1) * 512], func=AF.Square,
                                                            accum_out=ssq[:, s, cb:cb + 1]),
                  reads=[("res", s, cb), ("ssq",)], writes=[("junk3",), ("ssq",)])
    P.barrier()
    load_gb(X, gb, gsrc)
    for s in range(4):
        b = s % 2
        xv = xs[b]
        r0 = t * 512 + s * 128
        P.dma("sp", lambda e, xv=xv, r0=r0: e.dma_start(out=xv, in_=xin[r0:r0 + 128, :]), reads=[xkey_fn(t)], writes=[("xs", b)])
        j = X.ssn[0] % 16
        X.ssn[0] += 1
        sv = X.ss[:, j:j + 1]
        P.add("dve", lambda e, s=s, sv=sv: e.reduce_sum(out=sv, in_=ssq[:, s, 0:D // 512], axis=AX.X),
              reads=[("ssq",)], writes=[("ss", j)])
        rstd_ops(P, sv, ("ss", j), D)
        P.add("dve", lambda e, s=s, sv=sv: e.scalar_tensor_tensor(out=res[:, s, :], in0=res[:, s, :], scalar=sv, in1=gb,
                                                                  op0=ALU.mult, op1=ALU.mult),
              reads=[("res", s, cb_) for cb_ in range(D // 512)] + [("ss", j), ("gb",)], writes=[("res", s)])
        P.add("dve", lambda e, s=s, xv=xv: e.tensor_tensor(out=xv, in0=xv, in1=res[:, s, :], op=ALU.add),
              reads=[("res", s), ("xs", b)], writes=[("xs", b)])
        P.dma("sp", lambda e, xv=xv, r0=r0: e.dma_start(out=xout[r0:r0 + 128, :], in_=xv),
              reads=[("xs", b)], writes=[okey_fn(t)], output=X.is_out(xout))
        if halo_out is not None and t == cfg["NT"] - 1 and s == 3:
            P.dma("sp", lambda e, xv=xv: e.dma_start(out=halo_out[0][0:2, :], in_=xv[126:128, :]),
                  reads=[("xs", b)], writes=[halo_out[1]], output=X.is_out(halo_out[0]))
    P.barrier()


def mixer_prologue(X, l, mplan, xs, junk, gb, hTh, hname, lv, ain8):
    P = X.P; cfg = X.cfg; dr = X.dr
    KC = cfg["KC"]; KD = cfg["KD"]
    load_featmajor(X, dr["conv_a"][l], 3, KC, X.fstage, X.cwa, "cwa")
    load_featmajor(X, dr["b_gate"][l:l + 1, :], 1, 2 * KD, X.fstage, X.bgT, "bgT")
    for i, nm in enumerate(("lam_q1", "lam_k1", "lam_q2", "lam_k2")):
        P.dma("sp", lambda e, i=i, nm=nm: e.dma_start(out=lv[i], in_=dr[nm][l:l + 1, :].partition_broadcast(128)),
              writes=[("lv", i)])
    lsum = X.lsum
    for j in range(2):
        P.add("dve", lambda e, j=j: e.tensor_tensor(out=lv[2 * j], in0=lv[2 * j], in1=lv[2 * j + 1], op=ALU.mult),
              reads=[("lv", 2 * j), ("lv", 2 * j + 1)], writes=[("lv", 2 * j)])
        P.add("dve", lambda e, j=j: e.reduce_sum(out=lsum[:, j:j + 1], in_=lv[2 * j], axis=AX.X),
              reads=[("lv", 2 * j)], writes=[("lsum",)])
    P.add("act", lambda e: e.activation(out=lsum[:, 0:2], in_=lsum[:, 0:2], func=AF.Exp), reads=[("lsum",)], writes=[("lsum",)])
    P.add("dve", lambda e: e.tensor_tensor(out=X.nlam, in0=lsum[:, 1:2], in1=lsum[:, 0:1], op=ALU.subtract),
          reads=[("lsum",)], writes=[("nlam",)])
    P.add("dve", lambda e: e.tensor_scalar(out=X.nlam, in0=X.nlam, scalar1=X.laminit[:, l:l + 1], scalar2=None, op0=ALU.subtract),
          reads=[("nlam",), ("laminit",)], writes=[("nlam",)])
    P.dma("sp", lambda e: e.dma_start(out=X.sgb, in_=dr["subln_g"][l:l + 1, :].partition_broadcast(128)), writes=[("sgb",)])
    P.add("dve", lambda e: e.tensor_scalar(out=X.sgb, in0=X.sgb, scalar1=X.omlam[:, l:l + 1], scalar2=None, op0=ALU.mult),
          reads=[("sgb",), ("laminit",)], writes=[("sgb",)])
    halo_rows(X, dr["xh_all"], ("xh_all",), xs[0], ("xs", 0), xs[1], ("xs", 1), junk, gb, hTh, hname)
    uprev = X.uprev
    for gi, c0 in enumerate(range(0, KC, 4)):
        nch = min(4, KC - c0)
        mm_ws_halo(X, mplan["ain"][gi], nch, 0, lambda k: hTh[:, k, 512:514], lambda k: (hname, k), True)
        mm_ws_halo(X, mplan["ac"][gi], nch, 1, lambda k: hTh[:, k, 512:514], lambda k: (hname, k), True)
        P.add("act", lambda e, nch=nch: e.activation(out=ain8[:, 0:2 * nch], in_=bank(X, 0)[:, 0:2 * nch], func=AF.Copy),
              reads=[("ps", 0)], writes=[("ain8",)])
        P.add("dve", lambda e, nch=nch, c0=c0: e.tensor_tensor(out=uprev[:, c0:c0 + nch, :],
                                                                in0=ain8[:, 0:2 * nch].rearrange("p (n two) -> p n two", two=2),
                                                                in1=bank(X, 1)[:, 0:2 * nch].rearrange("p (n two) -> p n two", two=2),
                                                                op=ALU.mult),
              reads=[("ain8",), ("ps", 1)], writes=[("uprev", c0 + i) for i in range(nch)])


def kv_plan(X, l, mplan=None):
    cfg = X.cfg
    wv = wview(X.dr["w_in"][l])
    plan = {"k": [], "v": []}
    for cg in range(0, 2 * cfg["H"], 4):
        plan["k"].append(wblocks(wv, cfg["KD"], cfg["o_k"] + cg * 128, 512))
    for vb in range(cfg["DA"] // 512):
        plan["v"].append(wblocks(wv, cfg["KD"], cfg["o_v"] + vb * 512, 512))
    seq = []
    for t in range(cfg["NT"]):
        for lst in plan["k"]:
            seq.extend(lst)
        if t == 0 and mplan is not None:
            for gi in range(len(mplan["ain"])):
                seq.extend(mplan["ain"][gi]); seq.extend(mplan["ac"][gi])
        for lst in plan["v"]:
            seq.extend(lst)
    plan["mixer"] = mplan
    return plan, seq


def kv_phase(X, l, plan, xin, xkey_fn):
    P = X.P; cfg = X.cfg; D = cfg["D"]; KD = cfg["KD"]; H = cfg["H"]; NBr = cfg["NBr"]
    A = X.A
    L = Layout(0, X.act_limit)
    hT = A.view(L.take(KD * 514 * 2), [KD, 514], BF16)
    hT_b = A.view(L.take(KD * 514 * 2), [KD, 514], BF16)
    xs = [A.view(L.take(D * 4), [D], F32) for _ in range(2)]
    junk = A.view(L.take(D * 2), [D], BF16)
    gb = A.view(L.take(D * 4), [D], F32)
    kst = A.view(L.take(2 * H * 512 * 2), [2 * H, 512], BF16)
    VE = cfg["VE"]; HK = cfg["HK"]
    vst = [A.view(L.take(H * VE * 2), [H, VE], BF16) for _ in range(4)]
    lv = [A.view(L.take(512), [128], F32) for _ in range(4)]
    ain8 = A.view(L.take(64), [8], F32)
    dr = X.dr
    for s in range(4):
        P.add("dve", lambda e, s=s: e.memset(vst[s][:, :, 256:VE], 1.0), writes=[("vst", s)])
    P.dma("sp", lambda e: e.dma_start(out=dr["xh_loc"][0:2, :], in_=xin[cfg["T"] - 2:cfg["T"], :]),
          reads=[xkey_fn(cfg["NT"] - 1)], writes=[X.lockey("xh", 0)], output=X.is_out(dr["xh_loc"]))
    if X.after_kv_tile is not None:
        X.allgather("xh")
    cnt = 0
    NT = cfg["NT"]
    hTs = [hT, hT_b]
    hnames = ["hT", "hTb"]
    norm_tile(X, 0, xin, xkey_fn, dr["norm_mix_pre"][l:l + 1, :], xs, junk, gb, hTs[0], hnames[0])
    for t in range(NT):
        hTc = hTs[t % 2]; hn = hnames[t % 2]
        hTn = hTs[(t + 1) % 2]; hnn = hnames[(t + 1) % 2]
        for gi, cg in enumerate(range(0, 2 * H, 4)):
            b0 = 4 * (cnt % 2); cnt += 1
            mm_ws(X, plan["k"][gi], 4, b0, lambda k, hTc=hTc: hTc[:, k, 0:512], lambda k, hn=hn: (hn, k))
            for n in range(4):
                c = cg + n
                if n % 2 == 0:
                    P.add("act", lambda e, c=c, n=n, b0=b0: e.activation(out=kst[:, c, :], in_=bank(X, b0 + n), func=AF.Copy),
                          reads=[("ps", b0 + n)], writes=[("kst", c)])
                else:
                    P.add("dve", lambda e, c=c, n=n, b0=b0: e.tensor_copy(out=kst[:, c, :], in_=bank(X, b0 + n)),
                          reads=[("ps", b0 + n)], writes=[("kst", c)])
            kh = cg // (2 * HK); cl = cg % (2 * HK)
            kdst = dr["kT_loc"][(t, kh)].rearrange("(c p) t -> p c t", p=128)[:, cl:cl + 4, :]
            P.dma("sp", lambda e, cg=cg, kdst=kdst: e.dma_start(out=kdst, in_=kst[:, cg:cg + 4, :]),
                  reads=[("kst", cg + i) for i in range(4)], writes=[X.lockey(("kT", t, kh), cg)], output=X.is_out(dr["kT_loc"][(t, kh)]))
        if t == 0 and plan.get("mixer") is not None:
            mixer_prologue(X, l, plan["mixer"], xs, junk, gb, hTs[1], "hTh", lv, ain8)
        for vb in range(cfg["DA"] // 512):
            b0 = 4 * (cnt % 2); cnt += 1
            mm_as(X, plan["v"][vb], b0, lambda k, s, hTc=hTc: hTc[:, k, s * 128:(s + 1) * 128], lambda k, hn=hn: (hn, k))
            for s in range(4):
                src = bank(X, b0 + s).rearrange("p (h e) -> p h e", e=256)
                dst = vst[s][:, 2 * vb:2 * vb + 2, 0:256]
                if s % 2 == 0:
                    P.add("act", lambda e, src=src, dst=dst: e.activation(out=dst, in_=src, func=AF.Copy),
                          reads=[("ps", b0 + s)], writes=[("vst", s)])
                else:
                    P.add("dve", lambda e, src=src, dst=dst: e.tensor_copy(out=dst, in_=src),
                          reads=[("ps", b0 + s)], writes=[("vst", s)])
            if t + 1 < NT:
                nvb = cfg["DA"] // 512
                if vb == 0:
                    norm_stage_a(X, t + 1, 0, xin, xkey_fn, xs, junk, gb)
                    norm_stage_a(X, t + 1, 1, xin, xkey_fn, xs, junk, gb)
                if vb == min(1, nvb - 1):
                    norm_stage_b(X, 0, xs, hTn, hnn)
                    norm_stage_b(X, 1, xs, hTn, hnn)
                    norm_stage_a(X, t + 1, 2, xin, xkey_fn, xs, junk, gb)
                    norm_stage_a(X, t + 1, 3, xin, xkey_fn, xs, junk, gb)
                if vb == nvb - 1:
                    norm_stage_b(X, 2, xs, hTn, hnn)
                    norm_stage_b(X, 3, xs, hTn, hnn)
        for s in range(4):
            for vh in range(H // 2):
                vdst = dr["v_loc"][(t, vh)].rearrange("(h p) (k e) -> p h k e", p=128, e=VE)[:, :, s, :]
                P.dma("sp", lambda e, s=s, vh=vh, vdst=vdst: e.dma_start(out=vdst, in_=vst[s][:, 2 * vh:2 * vh + 2, :]),
                      reads=[("vst", s)], writes=[X.lockey(("v", t, vh), s)], output=X.is_out(dr["v_loc"][(t, vh)]))
        if X.after_kv_tile is not None:
            X.after_kv_tile(t)
    P.barrier()


def mixer_plan(X, l):
    cfg = X.cfg; KD = cfg["KD"]; KC = cfg["KC"]; KA = cfg["KA"]; D = cfg["D"]
    wv = wview(X.dr["w_in"][l])
    wa = wview(X.dr["w_a_out"][l]); wb = wview(X.dr["w_b_out"][l]); wo = wview(X.dr["w_o"][l])
    plan = {"ain": [], "ac": [], "ab": [], "q": [], "ga": [], "ya": [], "gb": [], "yb": [], "out": []}
    for c0 in range(0, KC, 4):
        n = min(4, KC - c0) * 128
        plan["ain"].append(wblocks(wv, KD, cfg["o_ain"] + c0 * 128, n))
        plan["ac"].append(wblocks(wv, KD, cfg["o_ac"] + c0 * 128, n))
        plan["ab"].append(wblocks(wv, KD, cfg["o_ab"] + c0 * 128, n))
    for c0 in range(0, 2 * cfg["H"], 4):
        plan["q"].append(wblocks(wv, KD, cfg["o_q"] + c0 * 128, 512))
    for og in range(KD // 4):
        plan["ga"].append(wblocks(wv, KD, cfg["o_g"] + og * 512, 512))
        plan["ya"].append(wblocks(wa, KC, og * 512, 512))
        plan["gb"].append(wblocks(wv, KD, cfg["o_g"] + D + og * 512, 512))
        plan["yb"].append(wblocks(wb, KA, og * 512, 512))
        plan["out"].append(wblocks(wo, KD, og * 512, 512))
    seq = []
    for t in range(cfg["NT"]):
        for gi in range(len(plan["ain"])):
            seq.extend(plan["ain"][gi]); seq.extend(plan["ac"][gi]); seq.extend(plan["ab"][gi])
        for lst in plan["q"]:
            seq.extend(lst)
        for og in range(KD // 4):
            for nm in ("ga", "ya", "gb", "yb"):
                seq.extend(plan[nm][og])
        for og in range(KD // 4):
            seq.extend(plan["out"][og])
    return plan, seq


def mixer_phase(X, l, plan, xin, xkey_fn, xout, okey_fn):
    P = X.P; cfg = X.cfg; A = X.A; dr = X.dr
    D = cfg["D"]; KD = cfg["KD"]; KC = cfg["KC"]; H = cfg["H"]; KA = cfg["KA"]; NR = cfg["NR"]; NBr = cfg["NBr"]
    T = cfg["T"]
    scale = 128.0 ** -0.5
    L = Layout(0, X.act_limit)
    o_hT = L.take(KD * 514 * 2)
    o_ua = L.take(KC * 512 * 2)
    o_at = L.take(KA * 512 * 2)
    o_qt = L.take(max(2 * H * 512 * 2, 4 * 512 * 4 * 2, D * 4))
    VE = cfg["VE"]; HK = cfg["HK"]
    o_mT = L.take(max(KD * 512 * 2, 2 * D * 4, 2 * (2 * T * 2) + 2 * (NBr * VE * 2) + 64))
    o_S = L.take(max(2 * D * 4 - (L.off - o_mT) if False else 0, 32768))
    hT = A.view(o_hT, [KD, 514], BF16)
    mix = A.view(0, [4, D], F32)
    assert 4 * D * 4 <= o_mT, "mix overlay must stay below mT"
    uaT = A.view(o_ua, [KC, 512], BF16)
    attnT = A.view(o_at, [KA, 512], BF16)
    QT = A.view(o_qt, [2 * H, 512], BF16)
    mT = A.view(o_mT, [KD, 512], BF16)
    xs1 = [A.view(o_mT + i * D * 4, [D], F32) for i in range(2)]
    gb1 = A.view(o_at, [D], F32) if KA * 512 * 2 >= D * 4 else None
    LS = Layout(o_S, o_S + 32768)
    junk1 = A.view(LS.take(D * 2), [D], BF16)
    if gb1 is None:
        gb1 = A.view(LS.take(D * 4), [D], F32)
    ain_t = [A.view(LS.take(512 * 4), [512], F32) for _ in range(4)]
    uext = [A.view(LS.take(516 * 4), [516], F32) for _ in range(4)]
    cbuf = [A.view(LS.take(512 * 4), [512], F32) for _ in range(2)]
    LK = Layout(o_mT, o_S)
    kbuf = [A.view(LK.take(2 * T * 2), [2, T], BF16) for _ in range(2)]
    vbuf = [A.view(LK.take(NBr * VE * 2), [NBr, VE], BF16) for _ in range(2)]
    L2 = Layout(o_S, o_S + 32768)
    PT = [A.view(L2.take(512 * 2), [512], BF16) for _ in range(4)]
    E32 = [A.view(L2.take(512 * 4), [512], F32) for _ in range(2)]
    mtp = A.view(L2.take(256 * 4), [256], F32)
    obuf = [A.view(L2.take(4 * 260 * 4), [4, 260], F32) for _ in range(1)]
    junk2 = A.view(L2.take(256 * 2), [256], BF16)
    mm1 = A.view(L2.take(H * 2 * 256 * 4), [H, 2, 256], F32)
    L3 = Layout(o_qt, o_mT)
    ga_t = [A.view(L3.take(512 * 4), [512], F32) for _ in range(4)]
    m_t = [A.view(L3.take(512 * 4), [512], F32) for _ in range(4)]
    xs4 = [A.view(o_S + i * D * 4, [D], F32) for i in range(2)] if 2 * D * 4 <= 32768 else None
    assert xs4 is not None
    gb4 = A.view(o_qt, [D], F32)
    junk3 = X.junk3

    cnt = 0
    uprev = X.uprev
    for t in range(cfg["NT"]):
        norm_tile(X, t, xin, xkey_fn, dr["norm_mix_pre"][l:l + 1, :], xs1, junk1, gb1, hT, "hT")
        P.barrier()
        for gi, c0 in enumerate(range(0, KC, 4)):
            nch = min(4, KC - c0)
            mm_ws(X, plan["ain"][gi], nch, 0, lambda k: hT[:, k, 0:512], lambda k: ("hT", k))
            for n in range(nch):
                P.add("act", lambda e, n=n: e.activation(out=ain_t[n], in_=bank(X, n), func=AF.Copy),
                      reads=[("ps", n)], writes=[("ain_t", n)])
            mm_ws(X, plan["ac"][gi], nch, 4, lambda k: hT[:, k, 0:512], lambda k: ("hT", k))
            for n in range(nch):
                c = c0 + n
                P.add("dve", lambda e, n=n, c=c: e.tensor_copy(out=uext[n][:, 0:2], in_=uprev[:, c, :]),
                      reads=[("uprev", c)], writes=[("uext", n)])
                P.add("dve", lambda e, n=n: e.tensor_tensor(out=uext[n][:, 2:514], in0=ain_t[n], in1=bank(X, 4 + n), op=ALU.mult),
                      reads=[("ain_t", n), ("ps", 4 + n)], writes=[("uext", n)])
                P.add("dve", lambda e, n=n, c=c: e.tensor_copy(out=uprev[:, c, :], in_=uext[n][:, 512:514]),
                      reads=[("uext", n)], writes=[("uprev", c)])
            mm_ws(X, plan["ab"][gi], nch, 0, lambda k: hT[:, k, 0:512], lambda k: ("hT", k))
            for n in range(nch):
                c = c0 + n
                cbv = cbuf[n % 2]
                conv3(P, uext[n], ("uext", n), X.cwa, c, cbv, ("cbuf", n % 2), ("cwa",))
                P.add("dve", lambda e, n=n, c=c, cbv=cbv: e.tensor_tensor(out=uaT[:, c, :], in0=cbv, in1=bank(X, n), op=ALU.mult),
                      reads=[("cbuf", n % 2), ("ps", n)], writes=[("uaT", c)])
        for gi, c0 in enumerate(range(0, 2 * H, 4)):
            b0 = 4 * (cnt % 2); cnt += 1
            mm_ws(X, plan["q"][gi], 4, b0, lambda k: hT[:, k, 0:512], lambda k: ("hT", k))
            for n in range(4):
                c = c0 + n
                if n % 2 == 0:
                    P.add("act", lambda e, c=c, n=n, b0=b0: e.activation(out=QT[:, c, :], in_=bank(X, b0 + n), func=AF.Copy),
                          reads=[("ps", b0 + n)], writes=[("QT", c)])
                else:
                    P.add("dve", lambda e, c=c, n=n, b0=b0: e.tensor_copy(out=QT[:, c, :], in_=bank(X, b0 + n)),
                          reads=[("ps", b0 + n)], writes=[("QT", c)])
        P.barrier()
        P.dma("sp", lambda e: e.dma_start(out=mm1, in_=dr["mm1"]), writes=[("mm1",)])
        iters = [(qg, h, r) for qg in range(2) for h in range(H) for r in range(NR)]

        def load_kv(i):
            if i >= len(iters):
                return
            qg, h, r = iters[i]
            bi = i % 2
            kh = h // HK; cl = 2 * (h % HK); vh = h // 2; h2 = h % 2
            for tk in range(cfg["NT"]):
                ksrc = dr["kT_all"][(tk, kh)].rearrange("(r c p) t -> r p c t", r=NR, p=128)[r, :, cl:cl + 2, :]
                P.dma("sp", lambda e, tk=tk, ksrc=ksrc: e.dma_start(out=kbuf[bi][:, :, tk * 512:(tk + 1) * 512], in_=ksrc),
                      reads=[("kT_all", tk, kh)], writes=[("kbuf", bi, tk)])
                vsrc = dr["v_all"][(tk, vh)].rearrange("(r h p) (k e) -> r p h k e", r=NR, p=128, e=VE)[r, :, h2, :, :]
                P.dma("sp", lambda e, tk=tk, vsrc=vsrc: e.dma_start(out=vbuf[bi][:, tk * 4:(tk + 1) * 4, :], in_=vsrc),
                      reads=[("v_all", tk, vh)], writes=[("vbuf", bi, tk)])

        blocks = []
        for i, (qg, h, r) in enumerate(iters):
            lg = 2 * t + qg
            todo = [j for j in range(NBr) if NBr * (NR - 1) + 2 * lg - (NBr * r + j) >= -1]
            for j in todo:
                blocks.append(dict(i=i, qg=qg, h=h, r=r, j=j, lg=lg,
                                   first=(r == 0 and j == todo[0]), last=(r == NR - 1 and j == todo[-1]),
                                   it_last=(j == todo[-1])))
        NBLK = len(blocks)
        LA = 2
        SB = (4, 5, 6)

        def rec_sc(n):
            b = blocks[n]
            sb = SB[n % 3]; bi = b["i"] % 2; j = b["j"]; h = b["h"]; q0 = b["qg"] * 256

            def sc(e, sb=sb, bi=bi, j=j, h=h, q0=q0):
                last = None
                for m in range(2):
                    last = e.matmul(bank(X, sb)[:, m * 256:(m + 1) * 256], kbuf[bi][:, m, j * 128:(j + 1) * 128],
                                    QT[:, 2 * h + m, q0:q0 + 256], start=True, stop=True)
                return last
            P.add("pe", sc, reads=[("kbuf", bi, j // 4), ("QT", 2 * h), ("QT", 2 * h + 1)], writes=[("ps", sb)])

        def rec_exp_av(n):
            b = blocks[n]
            sb = SB[n % 3]; bi = b["i"] % 2; j = b["j"]; h = b["h"]; r = b["r"]; lg = b["lg"]
            jg = NBr * r + j
            e_idx = 2 * lg - jg + cfg["EOFF"]
            d = jg - 2 * lg
            pdiag = d >= 0 and (d % NBr) in (0, 1)
            pi = n % 4
            bias = X.colb[:, h, e_idx:e_idx + 1]
            if not pdiag:
                P.add("act", lambda e: e.activation(out=PT[pi], in_=bank(X, sb), func=AF.Exp, bias=bias, scale=scale),
                      reads=[("ps", sb), ("colb",)], writes=[("PT", pi)])
            else:
                rc = d // NBr; typ = d % NBr
                ei = n % 2
                P.add("act", lambda e: e.activation(out=E32[ei], in_=bank(X, sb), func=AF.Exp, bias=bias, scale=scale),
                      reads=[("ps", sb), ("colb",)], writes=[("E32", ei)])
                P.add("dve", lambda e: e.tensor_scalar(out=mtp, in0=mm1[:, h, typ, :], scalar1=X.sel[:, rc:rc + 1],
                                                       scalar2=1.0, op0=ALU.mult, op1=ALU.add),
                      reads=[("mm1",), ("sel",)], writes=[("mtp",)])
                for m in range(2):
                    P.add("dve", lambda e, m=m: e.tensor_tensor(out=PT[pi][:, m * 256:(m + 1) * 256],
                                                                in0=E32[ei][:, m * 256:(m + 1) * 256], in1=mtp, op=ALU.mult),
                          reads=[("E32", ei), ("mtp",)], writes=[("PT", pi)])
            is_first = b["first"]; is_last = b["last"]

            def av(e):
                last = None
                for sub in range(2):
                    for m in range(2):
                        last = e.matmul(bank(X, sub * 2 + m)[:, 0:257], PT[pi][:, m * 256 + sub * 128:m * 256 + sub * 128 + 128],
                                        vbuf[bi][:, j, 0:257], start=is_first, stop=is_last)
                return last
            P.add("pe", av, reads=[("PT", pi), ("vbuf", bi, j // 4)], writes=[("ps", 0), ("ps", 1), ("ps", 2), ("ps", 3)])

        def rec_finalize(b):
            h = b["h"]; q0 = b["qg"] * 256
            fi = 0
            ob = obuf[fi]
            for q in range(4):
                if q % 2 == 0:
                    P.add("act", lambda e, q=q: e.activation(out=ob[:, q, 0:257], in_=bank(X, q)[:, 0:257], func=AF.Copy),
                          reads=[("ps", q)], writes=[("obuf", fi, q)])
                else:
                    P.add("dve", lambda e, q=q: e.tensor_copy(out=ob[:, q, 0:257], in_=bank(X, q)[:, 0:257]),
                          reads=[("ps", q)], writes=[("obuf", fi, q)])
            rd = X.rd
            for sub in range(2):
                O1 = ob[:, sub * 2, :]; O2 = ob[:, sub * 2 + 1, :]
                k1 = ("obuf", fi, sub * 2); k2 = ("obuf", fi, sub * 2 + 1)
                rds = rd[:, 2 * sub:2 * sub + 2]
                P.add("dve", lambda e, O1=O1, rds=rds: e.reciprocal(out=rds[:, 0:1], in_=O1[:, 256:257]), reads=[k1], writes=[("rd", sub)])
                P.add("dve", lambda e, O2=O2, rds=rds: e.reciprocal(out=rds[:, 1:2], in_=O2[:, 256:257]), reads=[k2], writes=[("rd", sub)])
                P.add("dve", lambda e, rds=rds: e.tensor_tensor(out=rds[:, 1:2], in0=rds[:, 1:2], in1=X.nlam, op=ALU.mult),
                      reads=[("rd", sub), ("nlam",)], writes=[("rd", sub)])
                P.add("dve", lambda e, O1=O1, rds=rds: e.tensor_scalar(out=O1[:, 0:256], in0=O1[:, 0:256], scalar1=rds[:, 0:1], scalar2=None, op0=ALU.mult),
                      reads=[k1, ("rd", sub)], writes=[k1])
                P.add("dve", lambda e, O1=O1, O2=O2, rds=rds: e.scalar_tensor_tensor(out=O1[:, 0:256], in0=O2[:, 0:256], scalar=rds[:, 1:2], in1=O1[:, 0:256],
                                                                                     op0=ALU.mult, op1=ALU.add),
                      reads=[k1, k2, ("rd", sub)], writes=[k1])
                jj = X.ssn[0] % 16
                X.ssn[0] += 1
                sv = X.ss[:, jj:jj + 1]
                P.add("dve", lambda e, sv=sv: e.memset(sv, 0.0), writes=[("ss", jj)])
                P.add("act", lambda e, O1=O1, sv=sv: e.activation(out=junk2, in_=O1[:, 0:256], func=AF.Square, accum_out=sv),
                      reads=[k1, ("ss", jj)], writes=[("junk2",), ("ss", jj)])
                rstd_ops(P, sv, ("ss", jj), 256)
                P.add("dve", lambda e, O1=O1, sv=sv: e.scalar_tensor_tensor(out=O1[:, 0:256], in0=O1[:, 0:256], scalar=sv, in1=X.sgb,
                                                                            op0=ALU.mult, op1=ALU.mult),
                      reads=[k1, ("ss", jj), ("sgb",)], writes=[k1])

            def deferred():
                pv = bank(X, 7).rearrange("p (a b) -> p a b", b=128)

                def tr(e):
                    last = None
                    for sub in range(2):
                        for ee in range(2):
                            last = e.transpose(pv[:, sub * 2 + ee, :], ob[:, sub * 2, ee * 128:(ee + 1) * 128], X.ident)
                    return last
                P.add("pe", tr, reads=[("obuf", fi, 0), ("obuf", fi, 2), ("ident",)], writes=[("ps", 7)])
                for sub in range(2):
                    c0 = q0 + sub * 128
                    P.add("act", lambda e, sub=sub, c0=c0: e.activation(out=attnT[:, 2 * h:2 * h + 2, c0:c0 + 128],
                                                                        in_=pv[:, 2 * sub:2 * sub + 2, :], func=AF.Copy),
                          reads=[("ps", 7)], writes=[("attnT", 2 * h), ("attnT", 2 * h + 1)])
            return deferred

        load_kv(0)
        load_kv(1)
        pending = []
        for n in range(min(LA, NBLK)):
            rec_sc(n)
        for n in range(NBLK):
            if n + LA < NBLK:
                rec_sc(n + LA)
            rec_exp_av(n)
            b = blocks[n]
            if b["it_last"]:
                load_kv(b["i"] + 2)
            if b["last"]:
                pending.append((n + 8, rec_finalize(b)))
            while pending and (pending[0][0] <= n or n == NBLK - 1):
                pending.pop(0)[1]()
        P.barrier()
        for og in range(KD // 4):
            mm_ws(X, plan["ga"][og], 4, 0, lambda k: hT[:, k, 0:512], lambda k: ("hT", k))
            for n in range(4):
                c = og * 4 + n
                P.add("act", lambda e, n=n, c=c: e.activation(out=ga_t[n], in_=bank(X, n), func=AF.Sigmoid, bias=X.bgT[:, 0, c:c + 1]),
                      reads=[("ps", n), ("bgT",)], writes=[("ga_t", n)])
            mm_ws(X, plan["ya"][og], 4, 4, lambda k: uaT[:, k, :], lambda k: ("uaT", k))
            for n in range(4):
                P.add("dve", lambda e, n=n: e.tensor_tensor(out=m_t[n], in0=ga_t[n], in1=bank(X, 4 + n), op=ALU.mult),
                      reads=[("ga_t", n), ("ps", 4 + n)], writes=[("m_t", n)])
            mm_ws(X, plan["gb"][og], 4, 0, lambda k: hT[:, k, 0:512], lambda k: ("hT", k))
            for n in range(4):
                c = KD + og * 4 + n
                P.add("act", lambda e, n=n, c=c: e.activation(out=ga_t[n], in_=bank(X, n), func=AF.Sigmoid, bias=X.bgT[:, 0, c:c + 1]),
                      reads=[("ps", n), ("bgT",)], writes=[("ga_t", n)])
            mm_ws(X, plan["yb"][og], 4, 4, lambda k: attnT[:, k, :], lambda k: ("attnT", k))
            for n in range(4):
                c = og * 4 + n
                P.add("dve", lambda e, n=n: e.tensor_tensor(out=ga_t[n], in0=ga_t[n], in1=bank(X, 4 + n), op=ALU.mult),
                      reads=[("ga_t", n), ("ps", 4 + n)], writes=[("ga_t", n)])
                P.add("dve", lambda e, n=n, c=c: e.tensor_tensor(out=mT[:, c, :], in0=m_t[n], in1=ga_t[n], op=ALU.add),
                      reads=[("ga_t", n), ("m_t", n)], writes=[("mT", c)])
        P.barrier()
        out_proj_post(X, t, plan["out"], mT, "mT", xin, xkey_fn, dr["norm_mix_post"][l:l + 1, :], xout, okey_fn,
                      mix, junk3, xs4, gb4, halo_out=(dr["mh_loc"], X.lockey("mh", 0)))


def ffn_plan(X, l):
    cfg = X.cfg; KD = cfg["KD"]; KF = cfg["KF"]; DFF = cfg["DFF"]; D = cfg["D"]
    w1v = wview(X.dr["w_ffn_in"][l]); w2v = wview(X.dr["w_ffn_out"][l])
    plan = {"gate": [], "up": [], "out": []}
    for c0 in range(0, KF, 4):
        n = min(4, KF - c0) * 128
        plan["gate"].append(wblocks(w1v, KD, c0 * 128, n))
        plan["up"].append(wblocks(w1v, KD, DFF + c0 * 128, n))
    for cb in range(D // 512):
        plan["out"].append(wblocks(w2v, KF, cb * 512, 512))
    seq = []
    for t in range(cfg["NT"]):
        for gi in range(len(plan["gate"])):
            seq.extend(plan["gate"][gi]); seq.extend(plan["up"][gi])
        for cb in range(D // 512):
            seq.extend(plan["out"][cb])
    return plan, seq


def ffn_phase(X, l, plan, xin, xkey_fn, xout, okey_fn):
    P = X.P; cfg = X.cfg; A = X.A; dr = X.dr
    D = cfg["D"]; KD = cfg["KD"]; KF = cfg["KF"]
    L = Layout(0, X.act_limit)
    oA = L.take(max(KD * 514 * 2, 4 * D * 4))
    h2T = A.view(oA, [KD, 514], BF16)
    res = A.view(oA, [4, D], F32)
    oB = L.take(max(KF * 512 * 2, 2 * D * 4 + D * 2 + D * 4))
    actT = A.view(oB, [KF, 512], BF16)
    xs = [A.view(oB + i * D * 4, [D], F32) for i in range(2)]
    junk = A.view(oB + 2 * D * 4, [D], BF16)
    gb = A.view(oB + 2 * D * 4 + D * 2, [D], F32)
    gext = [A.view(L.take(516 * 4), [516], F32) for _ in range(4)]
    cbuf = [A.view(L.take(512 * 4), [512], F32) for _ in range(2)]
    gprev = X.gprev
    load_featmajor(X, dr["conv_ffn"][l], 3, KF, X.fstage, X.cwf, "cwf")
    P.barrier()
    for t in range(cfg["NT"]):
        norm_tile(X, t, xin, xkey_fn, dr["norm_ffn_pre"][l:l + 1, :], xs, junk, gb, h2T, "h2T",
                  halo_src=(dr["mh_all"], ("mh_all",)) if t == 0 else None)
        P.barrier()
        for gi, c0 in enumerate(range(0, KF, 4)):
            nch = min(4, KF - c0)
            if t == 0:
                blk_list = plan["gate"][gi]
                nkb = len(blk_list)
                for kb, blk in enumerate(blk_list):
                    wv, wkey = X.ws.acquire(blk)
                    kc = blk.shape[1]
                    for n in range(nch):
                        def mm(e, wv=wv, kb=kb, kc=kc, n=n, nkb=nkb):
                            last = None
                            for k in range(kc):
                                last = e.matmul(bank(X, n), wv[:, k, n * 128:(n + 1) * 128], h2T[:, kb * 8 + k, 0:512],
                                                start=(kb == 0 and k == 0), stop=(kb == nkb - 1 and k == kc - 1))
                            return last
                        P.add("pe", mm, reads=[wkey] + [("h2T", kb * 8 + k) for k in range(kc)], writes=[("ps", n)])

                    def mmh(e, wv=wv, kb=kb, kc=kc, nch=nch, nkb=nkb):
                        last = None
                        for n in range(nch):
                            for k in range(kc):
                                last = e.matmul(bank(X, 4)[:, n * 2:n * 2 + 2], wv[:, k, n * 128:(n + 1) * 128],
                                                h2T[:, kb * 8 + k, 512:514],
                                                start=(kb == 0 and k == 0 and n == 0), stop=(kb == nkb - 1 and k == kc - 1),
                                                skip_group_check=True)
                        return last
                    P.add("pe", mmh, reads=[wkey] + [("h2T", kb * 8 + k) for k in range(kc)], writes=[("ps", 4)])
                    X.ws.release()
                for n in range(nch):
                    P.add("dve", lambda e, n=n: e.tensor_copy(out=gext[n][:, 0:2], in_=bank(X, 4)[:, n * 2:n * 2 + 2]),
                          reads=[("ps", 4)], writes=[("gext", n)])
            else:
                mm_ws(X, plan["gate"][gi], nch, 0, lambda k: h2T[:, k, 0:512], lambda k: ("h2T", k))
                for n in range(nch):
                    c = c0 + n
                    P.add("dve", lambda e, n=n, c=c: e.tensor_copy(out=gext[n][:, 0:2], in_=gprev[:, c, :]),
                          reads=[("gprev", c)], writes=[("gext", n)])
            mm_ws(X, plan["up"][gi], nch, 4, lambda k: h2T[:, k, 0:512], lambda k: ("h2T", k))
            for n in range(nch):
                c = c0 + n
                ge = gext[n]; cbv = cbuf[n % 2]
                P.add("act", lambda e, n=n, ge=ge: e.activation(out=ge[:, 2:514], in_=bank(X, n), func=AF.Copy),
                      reads=[("ps", n)], writes=[("gext", n)])
                P.add("dve", lambda e, c=c, ge=ge: e.tensor_copy(out=gprev[:, c, :], in_=ge[:, 512:514]),
                      reads=[("gext", n)], writes=[("gprev", c)])
                conv3(P, ge, ("gext", n), X.cwf, c, cbv, ("cbuf", n % 2), ("cwf",))
                P.add("act", lambda e, cbv=cbv: e.activation(out=cbv, in_=cbv, func=AF.Gelu_apprx_tanh),
                      reads=[("cbuf", n % 2)], writes=[("cbuf", n % 2)])
                P.add("dve", lambda e, c=c, n=n, cbv=cbv: e.tensor_tensor(out=actT[:, c, :], in0=cbv, in1=bank(X, 4 + n), op=ALU.mult),
                      reads=[("cbuf", n % 2), ("ps", 4 + n)], writes=[("actT", c)])
        P.barrier()
        out_proj_post(X, t, plan["out"], actT, "actT", xin, xkey_fn, dr["norm_ffn_post"][l:l + 1, :], xout, okey_fn,
                      res, X.junk3, xs, gb)


WEIGHT_SPECS = [
    ("w_in", lambda c: [c["L"], c["D"], c["DIN"]]), ("b_gate", lambda c: [c["L"], 2 * c["D"]]),
    ("conv_a", lambda c: [c["L"], 3, c["DC"]]), ("w_a_out", lambda c: [c["L"], c["DC"], c["D"]]),
    ("lam_q1", lambda c: [c["L"], 128]), ("lam_k1", lambda c: [c["L"], 128]),
    ("lam_q2", lambda c: [c["L"], 128]), ("lam_k2", lambda c: [c["L"], 128]),
    ("subln_g", lambda c: [c["L"], 256]), ("w_b_out", lambda c: [c["L"], c["DA"], c["D"]]),
    ("w_o", lambda c: [c["L"], c["D"], c["D"]]), ("norm_mix_pre", lambda c: [c["L"], c["D"]]),
    ("norm_mix_post", lambda c: [c["L"], c["D"]]), ("w_ffn_in", lambda c: [c["L"], c["D"], 2 * c["DFF"]]),
    ("conv_ffn", lambda c: [c["L"], 3, c["DFF"]]), ("w_ffn_out", lambda c: [c["L"], c["DFF"], c["D"]]),
    ("norm_ffn_pre", lambda c: [c["L"], c["D"]]), ("norm_ffn_post", lambda c: [c["L"], c["D"]]),
]


def build_program(cfg, mode="fused"):
    nc = bass.Bass("TRN2", target_bir_lowering=False)
    D = cfg["D"]; T = cfg["T"]; H = cfg["H"]; NR = cfg["NR"]; NBr = cfg["NBr"]; L = cfg["L"]
    X = Ctx()
    X.cfg = cfg; X.nc = nc
    dr = {}
    outs = set()

    def ext_in(name, shape, dt=F32):
        dr[name] = nc.dram_tensor(name, shape, dt, kind="ExternalInput").ap()

    def ext_out(name, shape, dt=F32):
        dr[name] = nc.dram_tensor(name, shape, dt, kind="ExternalOutput").ap()
        outs.add(name)

    def internal(name, shape, dt=F32):
        dr[name] = nc.dram_tensor(name, shape, dt).ap()

    for nm, shp in WEIGHT_SPECS:
        ext_in(nm, shp(cfg))
    for nm, shp in (("ident", [128, 128]), ("colb", [128, H, cfg["NE"]]), ("mm1", [128, H, 2, 256]),
                    ("sel", [128, NR]), ("sel2", [128, NR]), ("laminit", [128, 2 * L])):
        ext_in(nm, shp)
    ext_in("x", [T, D])
    HK = cfg["HK"]; VE = cfg["VE"]; NT = cfg["NT"]
    dr["kT_loc"] = {}; dr["kT_all"] = {}; dr["v_loc"] = {}; dr["v_all"] = {}
    kv_list = [("xh", None, [2, D], F32), ("mh", None, [2, D], F32)]
    for t in range(NT):
        for kh in range(H // HK):
            kv_list.append(("kT", (t, kh), [2 * HK * 128, 512], BF16))
        for vh in range(H // 2):
            kv_list.append(("v", (t, vh), [2 * 128, 4 * VE], BF16))
    for nm, idx, shp, dt in kv_list:
        sfx = "" if idx is None else "_%d_%d" % idx
        loc, al = nm + "_loc" + sfx, nm + "_all" + sfx
        shp_all = [NR * shp[0], shp[1]]
        if mode == "fused":
            internal(loc, shp, dt); internal(al, shp_all, dt)
        else:
            producer = "B" if nm == "mh" else "A"
            consumer = "C" if nm == "mh" else "B"
            if mode == producer:
                ext_out(loc, shp, dt)
            else:
                internal(loc, shp, dt)
            if mode == consumer:
                ext_in(al, shp_all, dt)
            else:
                internal(al, shp_all, dt)
        if idx is not None:
            dr[nm + "_loc"][idx] = dr.pop(loc); dr[nm + "_all"][idx] = dr.pop(al)
    if mode == "fused":
        internal("xmid", [T, D]); internal("x2", [T, D]); ext_out("out", [T, D])
    elif mode == "A":
        pass
    else:
        ext_out("out", [T, D])
    X.dr = dr
    X.is_out = lambda ap: ap.tensor.name in outs

    with ExitStack() as stack:
        A = Arena(nc, ARENA_BYTES)
        X.A = A
        X.PS = nc.alloc_psum_tensor("ps", [128, 4096], F32).ap()
        P = Prog(nc)
        X.P = P
        top = Layout(0)
        sizes = []
        R = cfg["ring"]
        Lp = Layout(ARENA_BYTES - (R * 8192 + 10240), ARENA_BYTES)
        X.act_limit = Lp.off
        wviews = [A.view(Lp.take(8 * 512 * 2), [8, 512], BF16) for _ in range(R)]
        X.ident = A.view(Lp.take(512), [128], F32)
        X.sel = A.view(Lp.take(NR * 4), [NR], F32)
        X.sel2 = A.view(Lp.take(NR * 4), [NR], F32)
        lam2 = A.view(Lp.take(2 * L * 4), [2 * L], F32)
        X.laminit = lam2[:, 0:L]
        X.omlam = lam2[:, L:2 * L]
        X.colb = A.view(Lp.take(H * cfg["NE"] * 4), [H, cfg["NE"]], F32)
        X.ss = A.view(Lp.take(64), [16], F32)
        X.ssq = A.view(Lp.take(4 * 8 * 4), [4, 8], F32)
        X.rd = A.view(Lp.take(16), [4], F32)
        X.fin_rr = [0]
        X.lsum = A.view(Lp.take(8), [2], F32)
        X.nlam = A.view(Lp.take(4), [1], F32)
        X.sgb = A.view(Lp.take(1024), [256], F32)
        X.junk3 = A.view(Lp.take(1024), [512], BF16)
        X.fstage = A.view(Lp.take(3 * 128 * 4), [3, 128], F32)
        X.cwa = A.view(Lp.take(3 * cfg["KC"] * 4), [3, cfg["KC"]], F32)
        X.cwf = A.view(Lp.take(3 * cfg["KF"] * 4), [3, cfg["KF"]], F32)
        X.bgT = A.view(Lp.take(2 * cfg["KD"] * 4), [1, 2 * cfg["KD"]], F32)
        X.uprev = A.view(Lp.take(cfg["KC"] * 2 * 4), [cfg["KC"], 2], F32)
        X.gprev = A.view(Lp.take(cfg["KF"] * 2 * 4), [cfg["KF"], 2], F32)
        X.ssn = [0]
        X.bank_rr = [0]
        X.loc_keys = {}

        def lockey(nm, tag):
            k = ("loc", nm, tag)
            X.loc_keys.setdefault(nm, []).append(k)
            return k
        X.lockey = lockey
        P.dma("sp", lambda e: e.dma_start(out=X.ident, in_=dr["ident"]), writes=[("ident",)])
        P.dma("sp", lambda e: e.dma_start(out=X.sel, in_=dr["sel"]), writes=[("sel",)])
        P.dma("sp", lambda e: e.dma_start(out=X.sel2, in_=dr["sel2"]), writes=[("sel",)])
        P.dma("sp", lambda e: e.dma_start(out=lam2, in_=dr["laminit"]), writes=[("laminit",)])
        P.dma("sp", lambda e: e.dma_start(out=X.colb, in_=dr["colb"]), writes=[("colb",)])

        ws = WStream(P, wviews)
        X.ws = ws
        plans = []
        if mode == "fused":
            for l in range(L):
                pb, sb = mixer_plan(X, l); pa, sa = kv_plan(X, l, pb); pc, sc = ffn_plan(X, l)
                ws.plan(sa); ws.plan(sb); ws.plan(sc)
                plans.append((pa, pb, pc))
        else:
            pl, sq = {"A": kv_plan, "B": mixer_plan, "C": ffn_plan}[mode](X, 0)
            ws.plan(sq)
        ws.start()
        groups = [[b * NR + r for r in range(NR)] for b in range(cfg["NB"])]

        def allgather(nm, idx=None):
            loc, al = dr[nm + "_loc"], dr[nm + "_all"]
            kk = nm
            akey = (nm + "_all",)
            if idx is not None:
                loc, al = loc[idx], al[idx]
                kk = (nm,) + idx
                akey = (nm + "_all",) + idx
            op = P.cc(lambda e: e.collective_compute("AllGather", ALU.bypass, replica_groups=groups, ins=[loc.opt()], outs=[al.opt()]),
                      reads=list(X.loc_keys.get(kk, [])), writes=[akey])
            X.loc_keys[kk] = []

        def after_kv_tile(t):
            for kh in range(H // HK):
                allgather("kT", (t, kh))
            for vh in range(H // 2):
                allgather("v", (t, vh))
        X.after_kv_tile = after_kv_tile if mode == "fused" else None
        X.allgather = allgather

        if mode == "fused":
            for l in range(L):
                xin = dr["x"] if l == 0 else dr["x2"]
                xk = (lambda t: ("x_in", t)) if l == 0 else (lambda t: ("x2", t))
                pa, pb, pc = plans[l]
                kv_phase(X, l, pa, xin, xk)
                import os
                stop = os.environ.get("FUSED_STOP")
                if stop == "A0":
                    break
                if stop == "A":
                    break
                mixer_phase(X, l, pb, xin, xk, dr["xmid"], lambda t: ("xmid", t))
                P.barrier()
                if stop == "B0":
                    break
                allgather("mh")
                if stop == "B":
                    break
                xo = dr["x2"] if l < L - 1 else dr["out"]
                ok = (lambda t: ("x2", t)) if l < L - 1 else (lambda t: ("out", t))
                ffn_phase(X, l, pc, dr["xmid"], lambda t: ("xmid", t), xo, ok)
                P.barrier()
        elif mode == "A":
            kv_phase(X, 0, pl, dr["x"], lambda t: ("x_in", t))
        elif mode == "B":
            mixer_phase(X, 0, pl, dr["x"], lambda t: ("x_in", t), dr["out"], lambda t: ("out", t))
        else:
            ffn_phase(X, 0, pl, dr["x"], lambda t: ("x_in", t), dr["out"], lambda t: ("out", t))
        import os
        assert os.environ.get("FUSED_STOP") or ws.next_use == len(ws.blocks), (ws.next_use, len(ws.blocks))
        P.check()
        P.emit(stack)
    return nc


def core_tables(cfg, r):
    H = cfg["H"]; NR = cfg["NR"]; NBr = cfg["NBr"]; NE = cfg["NE"]; EOFF = cfg["EOFF"]; L = cfg["L"]
    slopes = np.asarray(cfg["slopes"], np.float64)
    ki = np.arange(128, dtype=np.float64)
    colb = np.empty((128, H, NE), np.float32)
    for e in range(NE):
        delta = NBr * r + (e - EOFF)
        for h in range(H):
            if delta >= -1:
                colb[:, h, e] = -slopes[h] * (128.0 * delta + 128.0 - ki)
            else:
                colb[:, h, e] = -1e30
    mm1 = np.empty((128, H, 2, 256), np.float32)
    qi = np.arange(256)
    for typ in range(2):
        kk = 128 * typ + np.arange(128)
        vis = kk[:, None] <= qi[None, :]
        same = (kk[:, None] // 64) == (qi[None, :] // 64)
        dist = (kk[:, None] - qi[None, :]).astype(np.float64)
        for h in range(H):
            m = np.where(vis, 1.0, np.where(same, np.exp(-2.0 * slopes[h] * np.maximum(dist, 0.0)), 0.0))
            mm1[:, h, typ, :] = m - 1.0
    sel = np.zeros((128, NR), np.float32); sel[:, r] = 1.0
    sel2 = np.zeros((128, NR), np.float32)
    if r > 0:
        sel2[:, r - 1] = 1.0
    lam = np.zeros((128, 2 * L), np.float32)
    for l in range(L):
        lam[:, l] = cfg["lam_init"][l + cfg.get("layer0", 0)]
        lam[:, L + l] = 1.0 - cfg["lam_init"][l + cfg.get("layer0", 0)]
    return dict(ident=np.eye(128, dtype=np.float32), colb=colb, mm1=mm1, sel=sel, sel2=sel2, laminit=lam)


_CACHE = {}


def run_fused(cfg, inputs):
    key = ("fused", cfg["D"], cfg["H"], cfg["DFF"], cfg["T"], cfg["L"])
    if key not in _CACHE:
        _CACHE[key] = build_program(cfg, "fused")
    nc = _CACHE[key]
    NR = cfg["NR"]; NB = cfg["NB"]; T = cfg["T"]
    x = np.ascontiguousarray(np.asarray(inputs["x"], dtype=np.float32))
    w = {nm: np.ascontiguousarray(np.asarray(inputs[nm], dtype=np.float32)) for nm, _ in WEIGHT_SPECS}
    in_maps = []
    for c in range(NB * NR):
        b, r = divmod(c, NR)
        m = dict(w)
        m.update(core_tables(cfg, r))
        m["x"] = np.ascontiguousarray(x[b, r * T:(r + 1) * T, :])
        in_maps.append(m)
    res = run_bass_kernel_spmd(nc, in_maps, core_ids=list(range(NB * NR)))
    out = np.empty_like(x)
    for c in range(NB * NR):
        b, r = divmod(c, NR)
        out[b, r * T:(r + 1) * T, :] = res.results[c]["out"]
    return out


def kernel(**inputs):
    cfg = make_cfg()
    return run_fused(cfg, inputs)
```

```python
import math
from contextlib import ExitStack

import numpy as np
import concourse.bass as bass
import concourse.mybir as mybir
from concourse.bass_utils import run_bass_kernel_spmd

F32 = mybir.dt.float32
BF16 = mybir.dt.bfloat16
U8 = mybir.dt.uint8
AF = mybir.ActivationFunctionType
ALU = mybir.AluOpType
AX = mybir.AxisListType

EPS = 1e-6
COMPUTE = ("pe", "act", "dve", "pool")
QUEUES = ("sp", "act", "pool")
ARENA_BYTES = 207 * 1024


class Op:
    __slots__ = ("stream", "fn", "kind", "cs", "deps", "sig", "tick", "idx")


class Prog:
    def __init__(self, nc, n_lanes=8, same_engine_sync=True):
        self.nc = nc
        self.ops = []
        self.n_lanes = n_lanes
        self.same_engine_sync = same_engine_sync
        self.lane_rr = {q: 0 for q in QUEUES}
        self.lane_last = {}
        self.lane_cnt = {}
        self.reg_w = {}
        self.reg_r = {}
        self.last_cs = {}
        self.pending_barrier = {}
        self.outputs = []
        self.cc_cnt = 0

    def _record(self, op, reads, writes):
        psr = [k for k in reads if k[0] == "ps"]
        if psr:
            reads = [k for k in reads if k[0] != "ps"]
            writes = list(writes) + psr
        deps = {}
        for k in reads:
            w = self.reg_w.get(k)
            if w is not None:
                deps[w.idx] = w
        for k in writes:
            w = self.reg_w.get(k)
            if w is not None:
                deps[w.idx] = w
            for r in self.reg_r.get(k, {}).values():
                deps[r.idx] = r
        pb = self.pending_barrier.pop(op.stream, None)
        if pb:
            for d in pb:
                deps[d.idx] = d
        if op.kind == "dma":
            prev = self.lane_last.get(op.cs)
            if prev is not None:
                deps[prev.idx] = prev
        out = []
        for d in deps.values():
            if d.kind == "cmp" and d.stream == op.stream and op.kind == "cmp":
                if op.stream == "pe" or not self.same_engine_sync:
                    continue
            d.sig = True
            out.append(d)
        op.deps = out
        op.idx = len(self.ops)
        self.ops.append(op)
        for k in reads:
            self.reg_r.setdefault(k, {})[op.cs] = op
        for k in writes:
            self.reg_w[k] = op
            self.reg_r[k] = {}
        self.last_cs[op.cs] = op
        if op.kind != "cmp":
            self.lane_last[op.cs] = op
        return op

    def add(self, stream, fn, reads=(), writes=()):
        op = Op()
        op.stream = stream; op.fn = fn; op.kind = "cmp"; op.cs = stream
        op.sig = False; op.tick = None
        return self._record(op, reads, writes)

    def dma(self, queue, fn, reads=(), writes=(), output=False):
        op = Op()
        lane = "%s_l%d" % (queue, self.lane_rr[queue])
        self.lane_rr[queue] = (self.lane_rr[queue] + 1) % self.n_lanes
        op.stream = queue; op.fn = fn; op.kind = "dma"; op.cs = lane
        op.sig = True
        self.lane_cnt[lane] = self.lane_cnt.get(lane, 0) + 1
        op.tick = 16 * self.lane_cnt[lane]
        self._record(op, reads, writes)
        if output:
            self.outputs.append(op)
        return op

    def cc(self, fn, reads=(), writes=()):
        op = Op()
        op.stream = "pool"; op.fn = fn; op.kind = "cc"; op.cs = "cc"
        op.sig = True
        self.cc_cnt += 1
        op.tick = self.cc_cnt
        return self._record(op, reads, writes)

    def barrier(self):
        lst = [op for cs, op in self.last_cs.items() if cs != "cc"]
        for s in ("pe", "act", "dve", "pool", "sp"):
            self.pending_barrier[s] = list(lst)

    def _assign(self):
        cnt = {s: 0 for s in COMPUTE}
        for op in self.ops:
            if op.kind == "cmp" and op.sig:
                cnt[op.stream] += 1
                op.tick = cnt[op.stream]

    def check(self):
        self._assign()
        streams = {}
        for op in self.ops:
            streams.setdefault(op.stream, []).append(op)
        pos = {s: 0 for s in streams}
        val = {}
        progress = True
        while progress:
            progress = False
            for s, lst in streams.items():
                while pos[s] < len(lst):
                    op = lst[pos[s]]
                    if not all(val.get(d.cs, 0) >= d.tick for d in op.deps):
                        break
                    if op.kind == "dma":
                        val[op.cs] = val.get(op.cs, 0) + 16
                        assert val[op.cs] == op.tick
                    elif op.kind == "cc" or op.sig:
                        val[op.cs] = val.get(op.cs, 0) + 1
                        assert val[op.cs] == op.tick
                    pos[s] += 1
                    progress = True
        stuck = {s: (pos[s], len(lst)) for s, lst in streams.items() if pos[s] < len(lst)}
        if stuck:
            raise RuntimeError("deadlock in generated program %s" % stuck)

    def emit(self, stack):
        nc = self.nc
        self._assign()
        sems = {}
        for s in COMPUTE:
            sems[s] = stack.enter_context(nc.semaphore("s_" + s))
        for lane in self.lane_cnt:
            sems[lane] = stack.enter_context(nc.semaphore("s_" + lane))
        if self.cc_cnt:
            sems["cc"] = stack.enter_context(nc.semaphore("s_cc"))
        fin = Op(); fin.stream = "sp"; fin.kind = "fin"; fin.cs = "sp"; fin.sig = False
        fin.deps = list(self.outputs); fin.fn = None; fin.idx = len(self.ops)
        by_stream = {s: [] for s in ("pe", "act", "dve", "pool", "sp")}
        for op in self.ops + [fin]:
            by_stream[op.stream].append(op)
        block = stack.enter_context(nc.Block())

        def run_stream(eng, lst):
            seen = {}
            for op in lst:
                need = {}
                for d in op.deps:
                    if need.get(d.cs, 0) < d.tick:
                        need[d.cs] = d.tick
                for sname, t in need.items():
                    if seen.get(sname, 0) >= t:
                        continue
                    eng.wait_ge(sems[sname], t)
                    seen[sname] = t
                if op.fn is None:
                    continue
                ins = op.fn(eng)
                if op.kind == "dma":
                    ins.then_inc(sems[op.cs], 16)
                elif op.kind == "cc":
                    ins.then_inc(sems[op.cs])
                elif op.sig:
                    ins.then_inc(sems[op.cs], 1)

        @block.tensor
        def _(e):
            run_stream(e, by_stream["pe"])

        @block.scalar
        def _(e):
            run_stream(e, by_stream["act"])

        @block.vector
        def _(e):
            run_stream(e, by_stream["dve"])

        @block.gpsimd
        def _(e):
            run_stream(e, by_stream["pool"])

        @block.sync
        def _(e):
            run_stream(e, by_stream["sp"])


class Arena:
    def __init__(self, nc, nbytes):
        self.ap = nc.alloc_sbuf_tensor("arena", [128, nbytes], U8).ap()
        self.nbytes = nbytes

    def view(self, off, shape, dt):
        esz = {F32: 4, BF16: 2, U8: 1}[dt]
        n = int(np.prod(shape))
        assert off % 4 == 0 and off + n * esz <= self.nbytes, (off, n * esz, self.nbytes)
        v = self.ap[:, off:off + n * esz]
        if dt != U8:
            v = v.bitcast(dt)
        if len(shape) == 2:
            v = v.rearrange("p (a b) -> p a b", b=shape[1])
        elif len(shape) == 3:
            v = v.rearrange("p (a b c) -> p a b c", b=shape[1], c=shape[2])
        return v


class Layout:
    def __init__(self, base=0, limit=None):
        self.off = base
        self.limit = limit

    def take(self, nbytes):
        o = self.off
        self.off += (nbytes + 31) // 32 * 32
        if self.limit is not None:
            assert self.off <= self.limit, ("SBUF layout overflow", self.off, self.limit)
        return o


class WStream:
    def __init__(self, P, views, queue="pool"):
        self.P = P; self.views = views; self.R = len(views); self.queue = queue
        self.blocks = []; self.next_issue = 0; self.next_use = 0

    def plan(self, blocks):
        self.blocks.extend(blocks)

    def _issue(self):
        j = self.next_issue
        if j >= len(self.blocks):
            return
        self.next_issue += 1
        src = self.blocks[j]
        slot = j % self.R
        dst = self.views[slot][:, 0:src.shape[1], 0:src.shape[2]]
        self.P.dma(self.queue, lambda e: e.dma_start(out=dst, in_=src), writes=[("W", slot)])

    def start(self):
        for _ in range(self.R):
            self._issue()

    def acquire(self, expect):
        j = self.next_use
        assert j < self.next_issue and self.blocks[j] is expect, "weight plan mismatch at block %d" % j
        slot = j % self.R
        return self.views[slot], ("W", slot)

    def release(self):
        self.next_use += 1
        self._issue()


def wblocks(wv, k_chunks, c0, ncols):
    return [wv[:, kb:kb + min(8, k_chunks - kb), c0:c0 + ncols] for kb in range(0, k_chunks, 8)]


def wview(ap2d):
    return ap2d.rearrange("(kc p) n -> p kc n", p=128)


def make_cfg(D=4096, H=8, DFF=11008, T=2048, NR=4, NB=2, L=2, slopes=None, ring=4):
    c = dict(D=D, H=H, DFF=DFF, T=T, NR=NR, NB=NB, L=L, ring=ring)
    c["DC"] = D // 2
    c["KD"] = D // 128
    c["KC"] = c["DC"] // 128
    c["KF"] = DFF // 128
    c["DQK"] = H * 256
    c["DA"] = H * 256
    c["KA"] = c["DA"] // 128
    c["DIN"] = 3 * c["DC"] + 2 * c["DQK"] + c["DA"] + 2 * D
    c["o_ain"] = 0
    c["o_ab"] = c["DC"]
    c["o_ac"] = 2 * c["DC"]
    c["o_q"] = 3 * c["DC"]
    c["o_k"] = c["o_q"] + c["DQK"]
    c["o_v"] = c["o_k"] + c["DQK"]
    c["o_g"] = c["o_v"] + c["DA"]
    c["VE"] = 264
    c["HK"] = min(H, 4)
    c["NT"] = T // 512
    c["NBr"] = T // 128
    c["EOFF"] = NR * c["NBr"] - 1
    c["NE"] = c["NBr"] * (NR + 1) - 2
    if slopes is None:
        slopes = [2.0 ** (-8.0 * (i + 1) / H) for i in range(H)]
    c["slopes"] = slopes
    c["lam_init"] = [0.8 - 0.6 * math.exp(-0.3 * l) for l in range(8)]
    assert D % 512 == 0 and T % 512 == 0 and c["DC"] % 128 == 0 and DFF % 128 == 0
    return c


class Ctx:
    pass


def bank(X, b):
    return X.PS[:, b * 512:(b + 1) * 512]


def load_featmajor(X, src_rows, nk, C, stage_view, dst_view, name):
    P = X.P
    src = src_rows.rearrange("k (c p) -> c k p", p=128)
    P.dma("sp", lambda e: e.dma_start(out=stage_view[0:C, 0:nk, :], in_=src), writes=[("fstage",)])
    for k in range(nk):
        pv = bank(X, 7)[:, 0:C]
        P.add("pe", lambda e, k=k, pv=pv: e.transpose(pv, stage_view[0:C, k, :], X.ident[0:C, 0:C]),
              reads=[("fstage",), ("ident",)], writes=[("ps", 7)])
        P.add("dve", lambda e, k=k, pv=pv: e.tensor_copy(out=dst_view[:, k, :], in_=pv),
              reads=[("ps", 7)], writes=[(name,)])


def rstd_ops(P, sv, key, n):
    P.add("dve", lambda e: e.tensor_scalar(out=sv, in0=sv, scalar1=1.0 / n, scalar2=EPS, op0=ALU.mult, op1=ALU.add),
          reads=[key], writes=[key])
    P.add("act", lambda e: e.activation(out=sv, in_=sv, func=AF.Sqrt), reads=[key], writes=[key])
    P.add("dve", lambda e: e.reciprocal(out=sv, in_=sv), reads=[key], writes=[key])


def rms_rows(X, xs_v, xkey, junk_v, rows, D, gb_v, gbkey):
    P = X.P
    j = X.ssn[0] % 16
    X.ssn[0] += 1
    ss_v = X.ss[:, j:j + 1]
    sskey = ("ss", j)
    P.add("dve", lambda e: e.memset(ss_v[0:rows, :], 0.0), writes=[sskey])
    P.add("act", lambda e: e.activation(out=junk_v[0:rows, :], in_=xs_v[0:rows, :], func=AF.Square,
                                        accum_out=ss_v[0:rows, :]),
          reads=[xkey, sskey], writes=[("junk",), sskey])
    rstd_ops(P, ss_v[0:rows, :], sskey, D)
    P.add("dve", lambda e: e.scalar_tensor_tensor(out=xs_v[0:rows, :], in0=xs_v[0:rows, :], scalar=ss_v[0:rows, 0:1],
                                                  in1=gb_v[0:rows, :], op0=ALU.mult, op1=ALU.mult),
          reads=[xkey, sskey, gbkey], writes=[xkey])


def transpose_rows(X, xs_v, xkey, rows, KD, hT, hname, col0):
    P = X.P
    for c4 in range(0, KD, 4):
        n = min(4, KD - c4)
        b = X.bank_rr[0] % 8
        X.bank_rr[0] += 1
        pv = bank(X, b).rearrange("p (a b) -> p a b", b=128)

        def tr(e, c4=c4, n=n, pv=pv):
            last = None
            for i in range(n):
                last = e.transpose(pv[:, i, 0:rows], xs_v[0:rows, (c4 + i) * 128:(c4 + i + 1) * 128],
                                   X.ident[0:rows, 0:rows])
            return last
        P.add("pe", tr, reads=[xkey, ("ident",)], writes=[("ps", b)])
        dst = hT[:, c4:c4 + n, col0:col0 + rows]
        srcv = pv[:, 0:n, 0:rows]
        wk = [(hname, c4 + i) for i in range(n)]
        if (c4 // 4) % 2 == 0:
            P.add("act", lambda e, dst=dst, srcv=srcv: e.activation(out=dst, in_=srcv, func=AF.Copy),
                  reads=[("ps", b)], writes=wk)
        else:
            P.add("dve", lambda e, dst=dst, srcv=srcv: e.tensor_copy(out=dst, in_=srcv),
                  reads=[("ps", b)], writes=wk)


def load_gb(X, gb, src_row):
    X.P.dma("sp", lambda e: e.dma_start(out=gb, in_=src_row.partition_broadcast(128)), writes=[("gb",)])


def norm_stage_a(X, t, s, xin, xkey_fn, xs, junk, gb):
    P = X.P
    b = s % 2
    r0 = t * 512 + s * 128
    xv = xs[b]
    P.dma("sp", lambda e: e.dma_start(out=xv, in_=xin[r0:r0 + 128, :]), reads=[xkey_fn(t)], writes=[("xs", b)])
    rms_rows(X, xv, ("xs", b), junk, 128, X.cfg["D"], gb, ("gb",))


def norm_stage_b(X, s, xs, hT, hname):
    b = s % 2
    transpose_rows(X, xs[b], ("xs", b), 128, X.cfg["KD"], hT, hname, s * 128)


def norm_tile(X, t, xin, xkey_fn, gsrc, xs, junk, gb, hT, hname, halo_src=None):
    P = X.P; cfg = X.cfg; D = cfg["D"]; KD = cfg["KD"]
    load_gb(X, gb, gsrc)

    def stage_a(s):
        b = s % 2
        r0 = t * 512 + s * 128
        xv = xs[b]
        P.dma("sp", lambda e, xv=xv, r0=r0: e.dma_start(out=xv, in_=xin[r0:r0 + 128, :]),
              reads=[xkey_fn(t)], writes=[("xs", b)])
        rms_rows(X, xv, ("xs", b), junk, 128, D, gb, ("gb",))

    def stage_b(s):
        b = s % 2
        transpose_rows(X, xs[b], ("xs", b), 128, KD, hT, hname, s * 128)
    stage_a(0); stage_a(1); stage_b(0); stage_a(2); stage_b(1); stage_a(3); stage_b(2); stage_b(3)
    if halo_src is not None:
        halo_rows(X, halo_src[0], halo_src[1], xs[0], ("xs", 0), xs[1], ("xs", 1), junk, gb, hT, hname)


def halo_rows(X, hall, hkey, xv, xkey, tmp4, tkey, junk, gb, hT, hname):
    P = X.P; cfg = X.cfg; D = cfg["D"]; NR = cfg["NR"]
    src = hall.rearrange("(r two) d -> two r d", two=2)
    import os
    if os.environ.get("HALO_DIRECT"):
        P.dma("sp", lambda e: e.dma_start(out=xv[0:2, :], in_=hall[0:2, :]), reads=[hkey], writes=[xkey])
        rms_rows(X, xv, xkey, junk, 2, D, gb, ("gb",))
        transpose_rows(X, xv, xkey, 2, cfg["KD"], hT, hname, 512)
        return
    t4 = tmp4[0:2, 0:NR * D].rearrange("p (r d) -> p r d", d=D) if NR * D <= tmp4.shape[1] else None
    for r in range(NR):
        P.dma("sp", lambda e, r=r: e.dma_start(out=tmp4[0:2, 0:D], in_=src[:, r, :]), reads=[hkey], writes=[tkey])
        if r == 0:
            P.add("dve", lambda e, r=r: e.tensor_scalar(out=xv[0:2, :], in0=tmp4[0:2, 0:D], scalar1=X.sel2[0:2, r:r + 1],
                                                         scalar2=None, op0=ALU.mult),
                  reads=[tkey, ("sel",)], writes=[xkey])
        else:
            P.add("dve", lambda e, r=r: e.scalar_tensor_tensor(out=xv[0:2, :], in0=tmp4[0:2, 0:D], scalar=X.sel2[0:2, r:r + 1],
                                                                in1=xv[0:2, :], op0=ALU.mult, op1=ALU.add),
                  reads=[tkey, ("sel",), xkey], writes=[xkey])
    rms_rows(X, xv, xkey, junk, 2, D, gb, ("gb",))
    transpose_rows(X, xv, xkey, 2, cfg["KD"], hT, hname, 512)


def mm_ws(X, blk_list, nch, bank0, rhs_fn, rkeys_fn, ncols=512, col0=0):
    P = X.P
    nkb = len(blk_list)
    for kb, blk in enumerate(blk_list):
        wv, wkey = X.ws.acquire(blk)
        kc = blk.shape[1]
        for n in range(nch):
            def mm(e, wv=wv, kb=kb, kc=kc, n=n):
                last = None
                for k in range(kc):
                    last = e.matmul(bank(X, bank0 + n)[:, col0:col0 + ncols], wv[:, k, n * 128:(n + 1) * 128],
                                    rhs_fn(kb * 8 + k),
                                    start=(kb == 0 and k == 0), stop=(kb == nkb - 1 and k == kc - 1))
                return last
            P.add("pe", mm, reads=[wkey] + [rkeys_fn(kb * 8 + k) for k in range(kc)], writes=[("ps", bank0 + n)])
        X.ws.release()


def mm_ws_halo(X, blk_list, nch, b, rhs_fn, rkeys_fn, first):
    P = X.P
    nkb = len(blk_list)
    for kb, blk in enumerate(blk_list):
        wv, wkey = X.ws.acquire(blk)
        kc = blk.shape[1]

        def mm(e, wv=wv, kb=kb, kc=kc):
            last = None
            for n in range(nch):
                for k in range(kc):
                    last = e.matmul(bank(X, b)[:, n * 2:n * 2 + 2], wv[:, k, n * 128:(n + 1) * 128], rhs_fn(kb * 8 + k),
                                    start=(kb == 0 and k == 0 and n == 0), stop=(kb == nkb - 1 and k == kc - 1),
                                    skip_group_check=True)
            return last
        P.add("pe", mm, reads=[wkey] + [rkeys_fn(kb * 8 + k) for k in range(kc)], writes=[("ps", b)])
        X.ws.release()


def mm_as(X, blk_list, bank0, lhs_fn, lkeys_fn):
    P = X.P
    nkb = len(blk_list)
    for kb, blk in enumerate(blk_list):
        wv, wkey = X.ws.acquire(blk)
        kc = blk.shape[1]
        ncol = blk.shape[2]
        for s in range(4):
            def mm(e, wv=wv, kb=kb, kc=kc, s=s, ncol=ncol):
                last = None
                for k in range(kc):
                    last = e.matmul(bank(X, bank0 + s)[:, 0:ncol], lhs_fn(kb * 8 + k, s), wv[:, k, 0:ncol],
                                    start=(kb == 0 and k == 0), stop=(kb == nkb - 1 and k == kc - 1))
                return last
            P.add("pe", mm, reads=[wkey] + [lkeys_fn(kb * 8 + k) for k in range(kc)], writes=[("ps", bank0 + s)])
        X.ws.release()


def conv3(P, ext, ekey, cw, c, cbv, ckey, wkey):
    P.add("dve", lambda e: e.tensor_scalar(out=cbv, in0=ext[:, 0:512], scalar1=cw[:, 0, c:c + 1], scalar2=None, op0=ALU.mult),
          reads=[ekey, wkey], writes=[ckey])
    for kk in (1, 2):
        P.add("dve", lambda e, kk=kk: e.scalar_tensor_tensor(out=cbv, in0=ext[:, kk:kk + 512], scalar=cw[:, kk, c:c + 1],
                                                              in1=cbv, op0=ALU.mult, op1=ALU.add),
              reads=[ekey, wkey, ckey], writes=[ckey])


def out_proj_post(X, t, plan_out, actT, aname, xin, xkey_fn, gsrc, xout, okey_fn, res, junk3, xs, gb, halo_out=None):
    P = X.P; cfg = X.cfg; D = cfg["D"]
    ssq = X.ssq
    P.add("dve", lambda e: e.memset(ssq, 0.0), writes=[("ssq",)])
    for cb in range(D // 512):
        b0 = 4 * (cb % 2)
        mm_as(X, plan_out[cb], b0, lambda k, s: actT[:, k, s * 128:(s + 1) * 128], lambda k: (aname, k))
        for s in range(4):
            P.add("dve", lambda e, s=s, cb=cb, b0=b0: e.tensor_copy(out=res[:, s, cb * 512:(cb + 1) * 512], in_=bank(X, b0 + s)),
                  reads=[("ps", b0 + s)], writes=[("res", s, cb)])
            P.add("act", lambda e, s=s, cb=cb: e.activation(out=junk3, in_=res[:, s, cb * 512:(cb + 1) * 512], func=AF.Square,
                                                            accum_out=ssq[:, s, cb:cb + 1]),
                  reads=[("res", s, cb), ("ssq",)], writes=[("junk3",), ("ssq",)])
    P.barrier()
    load_gb(X, gb, gsrc)
    for s in range(4):
        b = s % 2
        xv = xs[b]
        r0 = t * 512 + s * 128
        P.dma("sp", lambda e, xv=xv, r0=r0: e.dma_start(out=xv, in_=xin[r0:r0 + 128, :]), reads=[xkey_fn(t)], writes=[("xs", b)])
        j = X.ssn[0] % 16
        X.ssn[0] += 1
        sv = X.ss[:, j:j + 1]
        P.add("dve", lambda e, s=s, sv=sv: e.reduce_sum(out=sv, in_=ssq[:, s, 0:D // 512], axis=AX.X),
              reads=[("ssq",)], writes=[("ss", j)])
        rstd_ops(P, sv, ("ss", j), D)
        P.add("dve", lambda e, s=s, sv=sv: e.scalar_tensor_tensor(out=res[:, s, :], in0=res[:, s, :], scalar=sv, in1=gb,
                                                                  op0=ALU.mult, op1=ALU.mult),
              reads=[("res", s, cb_) for cb_ in range(D // 512)] + [("ss", j), ("gb",)], writes=[("res", s)])
        P.add("dve", lambda e, s=s, xv=xv: e.tensor_tensor(out=xv, in0=xv, in1=res[:, s, :], op=ALU.add),
              reads=[("res", s), ("xs", b)], writes=[("xs", b)])
        P.dma("sp", lambda e, xv=xv, r0=r0: e.dma_start(out=xout[r0:r0 + 128, :], in_=xv),
              reads=[("xs", b)], writes=[okey_fn(t)], output=X.is_out(xout))
        if halo_out is not None and t == cfg["NT"] - 1 and s == 3:
            P.dma("sp", lambda e, xv=xv: e.dma_start(out=halo_out[0][0:2, :], in_=xv[126:128, :]),
                  reads=[("xs", b)], writes=[halo_out[1]], output=X.is_out(halo_out[0]))
    P.barrier()


def mixer_prologue(X, l, mplan, xs, junk, gb, hTh, hname, lv, ain8):
    P = X.P; cfg = X.cfg; dr = X.dr
    KC = cfg["KC"]; KD = cfg["KD"]
    load_featmajor(X, dr["conv_a"][l], 3, KC, X.fstage, X.cwa, "cwa")
    load_featmajor(X, dr["b_gate"][l:l + 1, :], 1, 2 * KD, X.fstage, X.bgT, "bgT")
    for i, nm in enumerate(("lam_q1", "lam_k1", "lam_q2", "lam_k2")):
        P.dma("sp", lambda e, i=i, nm=nm: e.dma_start(out=lv[i], in_=dr[nm][l:l + 1, :].partition_broadcast(128)),
              writes=[("lv", i)])
    lsum = X.lsum
    for j in range(2):
        P.add("dve", lambda e, j=j: e.tensor_tensor(out=lv[2 * j], in0=lv[2 * j], in1=lv[2 * j + 1], op=ALU.mult),
              reads=[("lv", 2 * j), ("lv", 2 * j + 1)], writes=[("lv", 2 * j)])
        P.add("dve", lambda e, j=j: e.reduce_sum(out=lsum[:, j:j + 1], in_=lv[2 * j], axis=AX.X),
              reads=[("lv", 2 * j)], writes=[("lsum",)])
    P.add("act", lambda e: e.activation(out=lsum[:, 0:2], in_=lsum[:, 0:2], func=AF.Exp), reads=[("lsum",)], writes=[("lsum",)])
    P.add("dve", lambda e: e.tensor_tensor(out=X.nlam, in0=lsum[:, 1:2], in1=lsum[:, 0:1], op=ALU.subtract),
          reads=[("lsum",)], writes=[("nlam",)])
    P.add("dve", lambda e: e.tensor_scalar(out=X.nlam, in0=X.nlam, scalar1=X.laminit[:, l:l + 1], scalar2=None, op0=ALU.subtract),
          reads=[("nlam",), ("laminit",)], writes=[("nlam",)])
    P.dma("sp", lambda e: e.dma_start(out=X.sgb, in_=dr["subln_g"][l:l + 1, :].partition_broadcast(128)), writes=[("sgb",)])
    P.add("dve", lambda e: e.tensor_scalar(out=X.sgb, in0=X.sgb, scalar1=X.omlam[:, l:l + 1], scalar2=None, op0=ALU.mult),
          reads=[("sgb",), ("laminit",)], writes=[("sgb",)])
    halo_rows(X, dr["xh_all"], ("xh_all",), xs[0], ("xs", 0), xs[1], ("xs", 1), junk, gb, hTh, hname)
    uprev = X.uprev
    for gi, c0 in enumerate(range(0, KC, 4)):
        nch = min(4, KC - c0)
        mm_ws_halo(X, mplan["ain"][gi], nch, 0, lambda k: hTh[:, k, 512:514], lambda k: (hname, k), True)
        mm_ws_halo(X, mplan["ac"][gi], nch, 1, lambda k: hTh[:, k, 512:514], lambda k: (hname, k), True)
        P.add("act", lambda e, nch=nch: e.activation(out=ain8[:, 0:2 * nch], in_=bank(X, 0)[:, 0:2 * nch], func=AF.Copy),
              reads=[("ps", 0)], writes=[("ain8",)])
        P.add("dve", lambda e, nch=nch, c0=c0: e.tensor_tensor(out=uprev[:, c0:c0 + nch, :],
                                                                in0=ain8[:, 0:2 * nch].rearrange("p (n two) -> p n two", two=2),
                                                                in1=bank(X, 1)[:, 0:2 * nch].rearrange("p (n two) -> p n two", two=2),
                                                                op=ALU.mult),
              reads=[("ain8",), ("ps", 1)], writes=[("uprev", c0 + i) for i in range(nch)])


def kv_plan(X, l, mplan=None):
    cfg = X.cfg
    wv = wview(X.dr["w_in"][l])
    plan = {"k": [], "v": []}
    for cg in range(0, 2 * cfg["H"], 4):
        plan["k"].append(wblocks(wv, cfg["KD"], cfg["o_k"] + cg * 128, 512))
    for vb in range(cfg["DA"] // 512):
        plan["v"].append(wblocks(wv, cfg["KD"], cfg["o_v"] + vb * 512, 512))
    seq = []
    for t in range(cfg["NT"]):
        for lst in plan["k"]:
            seq.extend(lst)
        if t == 0 and mplan is not None:
            for gi in range(len(mplan["ain"])):
                seq.extend(mplan["ain"][gi]); seq.extend(mplan["ac"][gi])
        for lst in plan["v"]:
            seq.extend(lst)
    plan["mixer"] = mplan
    return plan, seq


def kv_phase(X, l, plan, xin, xkey_fn):
    P = X.P; cfg = X.cfg; D = cfg["D"]; KD = cfg["KD"]; H = cfg["H"]; NBr = cfg["NBr"]
    A = X.A
    L = Layout(0, X.act_limit)
    hT = A.view(L.take(KD * 514 * 2), [KD, 514], BF16)
    hT_b = A.view(L.take(KD * 514 * 2), [KD, 514], BF16)
    xs = [A.view(L.take(D * 4), [D], F32) for _ in range(2)]
    junk = A.view(L.take(D * 2), [D], BF16)
    gb = A.view(L.take(D * 4), [D], F32)
    kst = A.view(L.take(2 * H * 512 * 2), [2 * H, 512], BF16)
    VE = cfg["VE"]; HK = cfg["HK"]
    vst = [A.view(L.take(H * VE * 2), [H, VE], BF16) for _ in range(4)]
    lv = [A.view(L.take(512), [128], F32) for _ in range(4)]
    ain8 = A.view(L.take(64), [8], F32)
    dr = X.dr
    for s in range(4):
        P.add("dve", lambda e, s=s: e.memset(vst[s][:, :, 256:VE], 1.0), writes=[("vst", s)])
    P.dma("sp", lambda e: e.dma_start(out=dr["xh_loc"][0:2, :], in_=xin[cfg["T"] - 2:cfg["T"], :]),
          reads=[xkey_fn(cfg["NT"] - 1)], writes=[X.lockey("xh", 0)], output=X.is_out(dr["xh_loc"]))
    if X.after_kv_tile is not None:
        X.allgather("xh")
    cnt = 0
    NT = cfg["NT"]
    hTs = [hT, hT_b]
    hnames = ["hT", "hTb"]
    norm_tile(X, 0, xin, xkey_fn, dr["norm_mix_pre"][l:l + 1, :], xs, junk, gb, hTs[0], hnames[0])
    for t in range(NT):
        hTc = hTs[t % 2]; hn = hnames[t % 2]
        hTn = hTs[(t + 1) % 2]; hnn = hnames[(t + 1) % 2]
        for gi, cg in enumerate(range(0, 2 * H, 4)):
            b0 = 4 * (cnt % 2); cnt += 1
            mm_ws(X, plan["k"][gi], 4, b0, lambda k, hTc=hTc: hTc[:, k, 0:512], lambda k, hn=hn: (hn, k))
            for n in range(4):
                c = cg + n
                if n % 2 == 0:
                    P.add("act", lambda e, c=c, n=n, b0=b0: e.activation(out=kst[:, c, :], in_=bank(X, b0 + n), func=AF.Copy),
                          reads=[("ps", b0 + n)], writes=[("kst", c)])
                else:
                    P.add("dve", lambda e, c=c, n=n, b0=b0: e.tensor_copy(out=kst[:, c, :], in_=bank(X, b0 + n)),
                          reads=[("ps", b0 + n)], writes=[("kst", c)])
            kh = cg // (2 * HK); cl = cg % (2 * HK)
            kdst = dr["kT_loc"][(t, kh)].rearrange("(c p) t -> p c t", p=128)[:, cl:cl + 4, :]
            P.dma("sp", lambda e, cg=cg, kdst=kdst: e.dma_start(out=kdst, in_=kst[:, cg:cg + 4, :]),
                  reads=[("kst", cg + i) for i in range(4)], writes=[X.lockey(("kT", t, kh), cg)], output=X.is_out(dr["kT_loc"][(t, kh)]))
        if t == 0 and plan.get("mixer") is not None:
            mixer_prologue(X, l, plan["mixer"], xs, junk, gb, hTs[1], "hTh", lv, ain8)
        for vb in range(cfg["DA"] // 512):
            b0 = 4 * (cnt % 2); cnt += 1
            mm_as(X, plan["v"][vb], b0, lambda k, s, hTc=hTc: hTc[:, k, s * 128:(s + 1) * 128], lambda k, hn=hn: (hn, k))
            for s in range(4):
                src = bank(X, b0 + s).rearrange("p (h e) -> p h e", e=256)
                dst = vst[s][:, 2 * vb:2 * vb + 2, 0:256]
                if s % 2 == 0:
                    P.add("act", lambda e, src=src, dst=dst: e.activation(out=dst, in_=src, func=AF.Copy),
                          reads=[("ps", b0 + s)], writes=[("vst", s)])
                else:
                    P.add("dve", lambda e, src=src, dst=dst: e.tensor_copy(out=dst, in_=src),
                          reads=[("ps", b0 + s)], writes=[("vst", s)])
            if t + 1 < NT:
                nvb = cfg["DA"] // 512
                if vb == 0:
                    norm_stage_a(X, t + 1, 0, xin, xkey_fn, xs, junk, gb)
                    norm_stage_a(X, t + 1, 1, xin, xkey_fn, xs, junk, gb)
                if vb == min(1, nvb - 1):
                    norm_stage_b(X, 0, xs, hTn, hnn)
                    norm_stage_b(X, 1, xs, hTn, hnn)
                    norm_stage_a(X, t + 1, 2, xin, xkey_fn, xs, junk, gb)
                    norm_stage_a(X, t + 1, 3, xin, xkey_fn, xs, junk, gb)
                if vb == nvb - 1:
                    norm_stage_b(X, 2, xs, hTn, hnn)
                    norm_stage_b(X, 3, xs, hTn, hnn)
        for s in range(4):
            for vh in range(H // 2):
                vdst = dr["v_loc"][(t, vh)].rearrange("(h p) (k e) -> p h k e", p=128, e=VE)[:, :, s, :]
                P.dma("sp", lambda e, s=s, vh=vh, vdst=vdst: e.dma_start(out=vdst, in_=vst[s][:, 2 * vh:2 * vh + 2, :]),
                      reads=[("vst", s)], writes=[X.lockey(("v", t, vh), s)], output=X.is_out(dr["v_loc"][(t, vh)]))
        if X.after_kv_tile is not None:
            X.after_kv_tile(t)
    P.barrier()


def mixer_plan(X, l):
    cfg = X.cfg; KD = cfg["KD"]; KC = cfg["KC"]; KA = cfg["KA"]; D = cfg["D"]
    wv = wview(X.dr["w_in"][l])
    wa = wview(X.dr["w_a_out"][l]); wb = wview(X.dr["w_b_out"][l]); wo = wview(X.dr["w_o"][l])
    plan = {"ain": [], "ac": [], "ab": [], "q": [], "ga": [], "ya": [], "gb": [], "yb": [], "out": []}
    for c0 in range(0, KC, 4):
        n = min(4, KC - c0) * 128
        plan["ain"].append(wblocks(wv, KD, cfg["o_ain"] + c0 * 128, n))
        plan["ac"].append(wblocks(wv, KD, cfg["o_ac"] + c0 * 128, n))
        plan["ab"].append(wblocks(wv, KD, cfg["o_ab"] + c0 * 128, n))
    for c0 in range(0, 2 * cfg["H"], 4):
        plan["q"].append(wblocks(wv, KD, cfg["o_q"] + c0 * 128, 512))
    for og in range(KD // 4):
        plan["ga"].append(wblocks(wv, KD, cfg["o_g"] + og * 512, 512))
        plan["ya"].append(wblocks(wa, KC, og * 512, 512))
        plan["gb"].append(wblocks(wv, KD, cfg["o_g"] + D + og * 512, 512))
        plan["yb"].append(wblocks(wb, KA, og * 512, 512))
        plan["out"].append(wblocks(wo, KD, og * 512, 512))
    seq = []
    for t in range(cfg["NT"]):
        for gi in range(len(plan["ain"])):
            seq.extend(plan["ain"][gi]); seq.extend(plan["ac"][gi]); seq.extend(plan["ab"][gi])
        for lst in plan["q"]:
            seq.extend(lst)
        for og in range(KD // 4):
            for nm in ("ga", "ya", "gb", "yb"):
                seq.extend(plan[nm][og])
        for og in range(KD // 4):
            seq.extend(plan["out"][og])
    return plan, seq


def mixer_phase(X, l, plan, xin, xkey_fn, xout, okey_fn):
    P = X.P; cfg = X.cfg; A = X.A; dr = X.dr
    D = cfg["D"]; KD = cfg["KD"]; KC = cfg["KC"]; H = cfg["H"]; KA = cfg["KA"]; NR = cfg["NR"]; NBr = cfg["NBr"]
    T = cfg["T"]
    scale = 128.0 ** -0.5
    L = Layout(0, X.act_limit)
    o_hT = L.take(KD * 514 * 2)
    o_ua = L.take(KC * 512 * 2)
    o_at = L.take(KA * 512 * 2)
    o_qt = L.take(max(2 * H * 512 * 2, 4 * 512 * 4 * 2, D * 4))
    VE = cfg["VE"]; HK = cfg["HK"]
    o_mT = L.take(max(KD * 512 * 2, 2 * D * 4, 2 * (2 * T * 2) + 2 * (NBr * VE * 2) + 64))
    o_S = L.take(max(2 * D * 4 - (L.off - o_mT) if False else 0, 32768))
    hT = A.view(o_hT, [KD, 514], BF16)
    mix = A.view(0, [4, D], F32)
    assert 4 * D * 4 <= o_mT, "mix overlay must stay below mT"
    uaT = A.view(o_ua, [KC, 512], BF16)
    attnT = A.view(o_at, [KA, 512], BF16)
    QT = A.view(o_qt, [2 * H, 512], BF16)
    mT = A.view(o_mT, [KD, 512], BF16)
    xs1 = [A.view(o_mT + i * D * 4, [D], F32) for i in range(2)]
    gb1 = A.view(o_at, [D], F32) if KA * 512 * 2 >= D * 4 else None
    LS = Layout(o_S, o_S + 32768)
    junk1 = A.view(LS.take(D * 2), [D], BF16)
    if gb1 is None:
        gb1 = A.view(LS.take(D * 4), [D], F32)
    ain_t = [A.view(LS.take(512 * 4), [512], F32) for _ in range(4)]
    uext = [A.view(LS.take(516 * 4), [516], F32) for _ in range(4)]
    cbuf = [A.view(LS.take(512 * 4), [512], F32) for _ in range(2)]
    LK = Layout(o_mT, o_S)
    kbuf = [A.view(LK.take(2 * T * 2), [2, T], BF16) for _ in range(2)]
    vbuf = [A.view(LK.take(NBr * VE * 2), [NBr, VE], BF16) for _ in range(2)]
    L2 = Layout(o_S, o_S + 32768)
    PT = [A.view(L2.take(512 * 2), [512], BF16) for _ in range(4)]
    E32 = [A.view(L2.take(512 * 4), [512], F32) for _ in range(2)]
    mtp = A.view(L2.take(256 * 4), [256], F32)
    obuf = [A.view(L2.take(4 * 260 * 4), [4, 260], F32) for _ in range(1)]
    junk2 = A.view(L2.take(256 * 2), [256], BF16)
    mm1 = A.view(L2.take(H * 2 * 256 * 4), [H, 2, 256], F32)
    L3 = Layout(o_qt, o_mT)
    ga_t = [A.view(L3.take(512 * 4), [512], F32) for _ in range(4)]
    m_t = [A.view(L3.take(512 * 4), [512], F32) for _ in range(4)]
    xs4 = [A.view(o_S + i * D * 4, [D], F32) for i in range(2)] if 2 * D * 4 <= 32768 else None
    assert xs4 is not None
    gb4 = A.view(o_qt, [D], F32)
    junk3 = X.junk3

    cnt = 0
    uprev = X.uprev
    for t in range(cfg["NT"]):
        norm_tile(X, t, xin, xkey_fn, dr["norm_mix_pre"][l:l + 1, :], xs1, junk1, gb1, hT, "hT")
        P.barrier()
        for gi, c0 in enumerate(range(0, KC, 4)):
            nch = min(4, KC - c0)
            mm_ws(X, plan["ain"][gi], nch, 0, lambda k: hT[:, k, 0:512], lambda k: ("hT", k))
            for n in range(nch):
                P.add("act", lambda e, n=n: e.activation(out=ain_t[n], in_=bank(X, n), func=AF.Copy),
                      reads=[("ps", n)], writes=[("ain_t", n)])
            mm_ws(X, plan["ac"][gi], nch, 4, lambda k: hT[:, k, 0:512], lambda k: ("hT", k))
            for n in range(nch):
                c = c0 + n
                P.add("dve", lambda e, n=n, c=c: e.tensor_copy(out=uext[n][:, 0:2], in_=uprev[:, c, :]),
                      reads=[("uprev", c)], writes=[("uext", n)])
                P.add("dve", lambda e, n=n: e.tensor_tensor(out=uext[n][:, 2:514], in0=ain_t[n], in1=bank(X, 4 + n), op=ALU.mult),
                      reads=[("ain_t", n), ("ps", 4 + n)], writes=[("uext", n)])
                P.add("dve", lambda e, n=n, c=c: e.tensor_copy(out=uprev[:, c, :], in_=uext[n][:, 512:514]),
                      reads=[("uext", n)], writes=[("uprev", c)])
            mm_ws(X, plan["ab"][gi], nch, 0, lambda k: hT[:, k, 0:512], lambda k: ("hT", k))
            for n in range(nch):
                c = c0 + n
                cbv = cbuf[n % 2]
                conv3(P, uext[n], ("uext", n), X.cwa, c, cbv, ("cbuf", n % 2), ("cwa",))
                P.add("dve", lambda e, n=n, c=c, cbv=cbv: e.tensor_tensor(out=uaT[:, c, :], in0=cbv, in1=bank(X, n), op=ALU.mult),
                      reads=[("cbuf", n % 2), ("ps", n)], writes=[("uaT", c)])
        for gi, c0 in enumerate(range(0, 2 * H, 4)):
            b0 = 4 * (cnt % 2); cnt += 1
            mm_ws(X, plan["q"][gi], 4, b0, lambda k: hT[:, k, 0:512], lambda k: ("hT", k))
            for n in range(4):
                c = c0 + n
                if n % 2 == 0:
                    P.add("act", lambda e, c=c, n=n, b0=b0: e.activation(out=QT[:, c, :], in_=bank(X, b0 + n), func=AF.Copy),
                          reads=[("ps", b0 + n)], writes=[("QT", c)])
                else:
                    P.add("dve", lambda e, c=c, n=n, b0=b0: e.tensor_copy(out=QT[:, c, :], in_=bank(X, b0 + n)),
                          reads=[("ps", b0 + n)], writes=[("QT", c)])
        P.barrier()
        P.dma("sp", lambda e: e.dma_start(out=mm1, in_=dr["mm1"]), writes=[("mm1",)])
        iters = [(qg, h, r) for qg in range(2) for h in range(H) for r in range(NR)]

        def load_kv(i):
            if i >= len(iters):
                return
            qg, h, r = iters[i]
            bi = i % 2
            kh = h // HK; cl = 2 * (h % HK); vh = h // 2; h2 = h % 2
            for tk in range(cfg["NT"]):
                ksrc = dr["kT_all"][(tk, kh)].rearrange("(r c p) t -> r p c t", r=NR, p=128)[r, :, cl:cl + 2, :]
                P.dma("sp", lambda e, tk=tk, ksrc=ksrc: e.dma_start(out=kbuf[bi][:, :, tk * 512:(tk + 1) * 512], in_=ksrc),
                      reads=[("kT_all", tk, kh)], writes=[("kbuf", bi, tk)])
                vsrc = dr["v_all"][(tk, vh)].rearrange("(r h p) (k e) -> r p h k e", r=NR, p=128, e=VE)[r, :, h2, :, :]
                P.dma("sp", lambda e, tk=tk, vsrc=vsrc: e.dma_start(out=vbuf[bi][:, tk * 4:(tk + 1) * 4, :], in_=vsrc),
                      reads=[("v_all", tk, vh)], writes=[("vbuf", bi, tk)])

        blocks = []
        for i, (qg, h, r) in enumerate(iters):
            lg = 2 * t + qg
            todo = [j for j in range(NBr) if NBr * (NR - 1) + 2 * lg - (NBr * r + j) >= -1]
            for j in todo:
                blocks.append(dict(i=i, qg=qg, h=h, r=r, j=j, lg=lg,
                                   first=(r == 0 and j == todo[0]), last=(r == NR - 1 and j == todo[-1]),
                                   it_last=(j == todo[-1])))
        NBLK = len(blocks)
        LA = 2
        SB = (4, 5, 6)

        def rec_sc(n):
            b = blocks[n]
            sb = SB[n % 3]; bi = b["i"] % 2; j = b["j"]; h = b["h"]; q0 = b["qg"] * 256

            def sc(e, sb=sb, bi=bi, j=j, h=h, q0=q0):
                last = None
                for m in range(2):
                    last = e.matmul(bank(X, sb)[:, m * 256:(m + 1) * 256], kbuf[bi][:, m, j * 128:(j + 1) * 128],
                                    QT[:, 2 * h + m, q0:q0 + 256], start=True, stop=True)
                return last
            P.add("pe", sc, reads=[("kbuf", bi, j // 4), ("QT", 2 * h), ("QT", 2 * h + 1)], writes=[("ps", sb)])

        def rec_exp_av(n):
            b = blocks[n]
            sb = SB[n % 3]; bi = b["i"] % 2; j = b["j"]; h = b["h"]; r = b["r"]; lg = b["lg"]
            jg = NBr * r + j
            e_idx = 2 * lg - jg + cfg["EOFF"]
            d = jg - 2 * lg
            pdiag = d >= 0 and (d % NBr) in (0, 1)
            pi = n % 4
            bias = X.colb[:, h, e_idx:e_idx + 1]
            if not pdiag:
                P.add("act", lambda e: e.activation(out=PT[pi], in_=bank(X, sb), func=AF.Exp, bias=bias, scale=scale),
                      reads=[("ps", sb), ("colb",)], writes=[("PT", pi)])
            else:
                rc = d // NBr; typ = d % NBr
                ei = n % 2
                P.add("act", lambda e: e.activation(out=E32[ei], in_=bank(X, sb), func=AF.Exp, bias=bias, scale=scale),
                      reads=[("ps", sb), ("colb",)], writes=[("E32", ei)])
                P.add("dve", lambda e: e.tensor_scalar(out=mtp, in0=mm1[:, h, typ, :], scalar1=X.sel[:, rc:rc + 1],
                                                       scalar2=1.0, op0=ALU.mult, op1=ALU.add),
                      reads=[("mm1",), ("sel",)], writes=[("mtp",)])
                for m in range(2):
                    P.add("dve", lambda e, m=m: e.tensor_tensor(out=PT[pi][:, m * 256:(m + 1) * 256],
                                                                in0=E32[ei][:, m * 256:(m + 1) * 256], in1=mtp, op=ALU.mult),
                          reads=[("E32", ei), ("mtp",)], writes=[("PT", pi)])
            is_first = b["first"]; is_last = b["last"]

            def av(e):
                last = None
                for sub in range(2):
                    for m in range(2):
                        last = e.matmul(bank(X, sub * 2 + m)[:, 0:257], PT[pi][:, m * 256 + sub * 128:m * 256 + sub * 128 + 128],
                                        vbuf[bi][:, j, 0:257], start=is_first, stop=is_last)
                return last
            P.add("pe", av, reads=[("PT", pi), ("vbuf", bi, j // 4)], writes=[("ps", 0), ("ps", 1), ("ps", 2), ("ps", 3)])

        def rec_finalize(b):
            h = b["h"]; q0 = b["qg"] * 256
            ob = obuf[0]
            fi = 0
            for q in range(4):
                if q % 2 == 0:
                    P.add("act", lambda e, q=q: e.activation(out=ob[:, q, 0:257], in_=bank(X, q)[:, 0:257], func=AF.Copy),
                          reads=[("ps", q)], writes=[("obuf", fi, q)])
                else:
                    P.add("dve", lambda e, q=q: e.tensor_copy(out=ob[:, q, 0:257], in_=bank(X, q)[:, 0:257]),
                          reads=[("ps", q)], writes=[("obuf", fi, q)])
            rd = X.rd
            svs = []
            for sub in range(2):
                O1 = ob[:, sub * 2, :]; O2 = ob[:, sub * 2 + 1, :]
                k1 = ("obuf", fi, sub * 2); k2 = ("obuf", fi, sub * 2 + 1)
                rds = rd[:, 2 * sub:2 * sub + 2]
                P.add("dve", lambda e, O1=O1, rds=rds: e.reciprocal(out=rds[:, 0:1], in_=O1[:, 256:257]), reads=[k1], writes=[("rd", sub)])
                P.add("dve", lambda e, O2=O2, rds=rds: e.reciprocal(out=rds[:, 1:2], in_=O2[:, 256:257]), reads=[k2], writes=[("rd", sub)])
                P.add("dve", lambda e, rds=rds: e.tensor_tensor(out=rds[:, 1:2], in0=rds[:, 1:2], in1=X.nlam, op=ALU.mult),
                      reads=[("rd", sub), ("nlam",)], writes=[("rd", sub)])
                P.add("dve", lambda e, O1=O1, rds=rds: e.tensor_scalar(out=O1[:, 0:256], in0=O1[:, 0:256], scalar1=rds[:, 0:1], scalar2=None, op0=ALU.mult),
                      reads=[k1, ("rd", sub)], writes=[k1])
                P.add("dve", lambda e, O1=O1, O2=O2, rds=rds: e.scalar_tensor_tensor(out=O1[:, 0:256], in0=O2[:, 0:256], scalar=rds[:, 1:2], in1=O1[:, 0:256],
                                                                                     op0=ALU.mult, op1=ALU.add),
                      reads=[k1, k2, ("rd", sub)], writes=[k1])
                jj = X.ssn[0] % 16
                X.ssn[0] += 1
                sv = X.ss[:, jj:jj + 1]
                svs.append((sv, ("ss", jj)))
                P.add("dve", lambda e, O1=O1, O2=O2: e.tensor_tensor(out=O2[:, 0:256], in0=O1[:, 0:256], in1=O1[:, 0:256], op=ALU.mult),
                      reads=[k1], writes=[k2])
                P.add("dve", lambda e, O2=O2, sv=sv: e.reduce_sum(out=sv, in_=O2[:, 0:256], axis=AX.X), reads=[k2], writes=[("ss", jj)])
                P.add("dve", lambda e, sv=sv: e.tensor_scalar(out=sv, in0=sv, scalar1=1.0 / 256, scalar2=EPS, op0=ALU.mult, op1=ALU.add),
                      reads=[("ss", jj)], writes=[("ss", jj)])

            def stage2():
                for sv, key in svs:
                    P.add("act", lambda e, sv=sv: e.activation(out=sv, in_=sv, func=AF.Sqrt), reads=[key], writes=[key])

            def stage3():
                for sub in range(2):
                    sv, key = svs[sub]
                    O1 = ob[:, sub * 2, :]
                    k1 = ("obuf", fi, sub * 2)
                    P.add("dve", lambda e, sv=sv: e.reciprocal(out=sv, in_=sv), reads=[key], writes=[key])
                    P.add("dve", lambda e, O1=O1, sv=sv: e.scalar_tensor_tensor(out=O1[:, 0:256], in0=O1[:, 0:256], scalar=sv, in1=X.sgb,
                                                                                op0=ALU.mult, op1=ALU.mult),
                          reads=[k1, key, ("sgb",)], writes=[k1])

            def stage4():
                pv = bank(X, 7).rearrange("p (a b) -> p a b", b=128)

                def tr(e):
                    last = None
                    for sub in range(2):
                        for ee in range(2):
                            last = e.transpose(pv[:, sub * 2 + ee, :], ob[:, sub * 2, ee * 128:(ee + 1) * 128], X.ident)
                    return last
                P.add("pe", tr, reads=[("obuf", fi, 0), ("obuf", fi, 2), ("ident",)], writes=[("ps", 7)])
                for sub in range(2):
                    c0 = q0 + sub * 128
                    P.add("act", lambda e, sub=sub, c0=c0: e.activation(out=attnT[:, 2 * h:2 * h + 2, c0:c0 + 128],
                                                                        in_=pv[:, 2 * sub:2 * sub + 2, :], func=AF.Copy),
                          reads=[("ps", 7)], writes=[("attnT", 2 * h), ("attnT", 2 * h + 1)])
            return [(4, stage2), (7, stage3), (11, stage4)]

        load_kv(0)
        load_kv(1)
        pending = []
        for n in range(min(LA, NBLK)):
            rec_sc(n)
        for n in range(NBLK):
            if n + LA < NBLK:
                rec_sc(n + LA)
            rec_exp_av(n)
            b = blocks[n]
            if b["it_last"]:
                load_kv(b["i"] + 2)
            if b["last"]:
                for off, fn in rec_finalize(b):
                    pending.append((n + off, fn))
            while pending and (pending[0][0] <= n or n == NBLK - 1):
                pending.pop(0)[1]()
        P.barrier()
        for og in range(KD // 4):
            mm_ws(X, plan["ga"][og], 4, 0, lambda k: hT[:, k, 0:512], lambda k: ("hT", k))
            for n in range(4):
                c = og * 4 + n
                P.add("act", lambda e, n=n, c=c: e.activation(out=ga_t[n], in_=bank(X, n), func=AF.Sigmoid, bias=X.bgT[:, 0, c:c + 1]),
                      reads=[("ps", n), ("bgT",)], writes=[("ga_t", n)])
            mm_ws(X, plan["ya"][og], 4, 4, lambda k: uaT[:, k, :], lambda k: ("uaT", k))
            for n in range(4):
                P.add("dve", lambda e, n=n: e.tensor_tensor(out=m_t[n], in0=ga_t[n], in1=bank(X, 4 + n), op=ALU.mult),
                      reads=[("ga_t", n), ("ps", 4 + n)], writes=[("m_t", n)])
            mm_ws(X, plan["gb"][og], 4, 0, lambda k: hT[:, k, 0:512], lambda k: ("hT", k))
            for n in range(4):
                c = KD + og * 4 + n
                P.add("act", lambda e, n=n, c=c: e.activation(out=ga_t[n], in_=bank(X, n), func=AF.Sigmoid, bias=X.bgT[:, 0, c:c + 1]),
                      reads=[("ps", n), ("bgT",)], writes=[("ga_t", n)])
            mm_ws(X, plan["yb"][og], 4, 4, lambda k: attnT[:, k, :], lambda k: ("attnT", k))
            for n in range(4):
                c = og * 4 + n
                P.add("dve", lambda e, n=n: e.tensor_tensor(out=ga_t[n], in0=ga_t[n], in1=bank(X, 4 + n), op=ALU.mult),
                      reads=[("ga_t", n), ("ps", 4 + n)], writes=[("ga_t", n)])
                P.add("dve", lambda e, n=n, c=c: e.tensor_tensor(out=mT[:, c, :], in0=m_t[n], in1=ga_t[n], op=ALU.add),
                      reads=[("ga_t", n), ("m_t", n)], writes=[("mT", c)])
        P.barrier()
        out_proj_post(X, t, plan["out"], mT, "mT", xin, xkey_fn, dr["norm_mix_post"][l:l + 1, :], xout, okey_fn,
                      mix, junk3, xs4, gb4, halo_out=(dr["mh_loc"], X.lockey("mh", 0)))


def ffn_plan(X, l):
    cfg = X.cfg; KD = cfg["KD"]; KF = cfg["KF"]; DFF = cfg["DFF"]; D = cfg["D"]
    w1v = wview(X.dr["w_ffn_in"][l]); w2v = wview(X.dr["w_ffn_out"][l])
    plan = {"gate": [], "up": [], "out": []}
    for c0 in range(0, KF, 4):
        n = min(4, KF - c0) * 128
        plan["gate"].append(wblocks(w1v, KD, c0 * 128, n))
        plan["up"].append(wblocks(w1v, KD, DFF + c0 * 128, n))
    for cb in range(D // 512):
        plan["out"].append(wblocks(w2v, KF, cb * 512, 512))
    seq = []
    for t in range(cfg["NT"]):
        for gi in range(len(plan["gate"])):
            seq.extend(plan["gate"][gi]); seq.extend(plan["up"][gi])
        for cb in range(D // 512):
            seq.extend(plan["out"][cb])
    return plan, seq


def ffn_phase(X, l, plan, xin, xkey_fn, xout, okey_fn):
    P = X.P; cfg = X.cfg; A = X.A; dr = X.dr
    D = cfg["D"]; KD = cfg["KD"]; KF = cfg["KF"]
    L = Layout(0, X.act_limit)
    oA = L.take(max(KD * 514 * 2, 4 * D * 4))
    h2T = A.view(oA, [KD, 514], BF16)
    res = A.view(oA, [4, D], F32)
    oB = L.take(max(KF * 512 * 2, 2 * D * 4 + D * 2 + D * 4))
    actT = A.view(oB, [KF, 512], BF16)
    xs = [A.view(oB + i * D * 4, [D], F32) for i in range(2)]
    junk = A.view(oB + 2 * D * 4, [D], BF16)
    gb = A.view(oB + 2 * D * 4 + D * 2, [D], F32)
    gext = [A.view(L.take(516 * 4), [516], F32) for _ in range(4)]
    cbuf = [A.view(L.take(512 * 4), [512], F32) for _ in range(2)]
    gprev = X.gprev
    load_featmajor(X, dr["conv_ffn"][l], 3, KF, X.fstage, X.cwf, "cwf")
    P.barrier()
    for t in range(cfg["NT"]):
        norm_tile(X, t, xin, xkey_fn, dr["norm_ffn_pre"][l:l + 1, :], xs, junk, gb, h2T, "h2T",
                  halo_src=(dr["mh_all"], ("mh_all",)) if t == 0 else None)
        P.barrier()
        for gi, c0 in enumerate(range(0, KF, 4)):
            nch = min(4, KF - c0)
            if t == 0:
                blk_list = plan["gate"][gi]
                nkb = len(blk_list)
                for kb, blk in enumerate(blk_list):
                    wv, wkey = X.ws.acquire(blk)
                    kc = blk.shape[1]
                    for n in range(nch):
                        def mm(e, wv=wv, kb=kb, kc=kc, n=n, nkb=nkb):
                            last = None
                            for k in range(kc):
                                last = e.matmul(bank(X, n), wv[:, k, n * 128:(n + 1) * 128], h2T[:, kb * 8 + k, 0:512],
                                                start=(kb == 0 and k == 0), stop=(kb == nkb - 1 and k == kc - 1))
                            return last
                        P.add("pe", mm, reads=[wkey] + [("h2T", kb * 8 + k) for k in range(kc)], writes=[("ps", n)])

                    def mmh(e, wv=wv, kb=kb, kc=kc, nch=nch, nkb=nkb):
                        last = None
                        for n in range(nch):
                            for k in range(kc):
                                last = e.matmul(bank(X, 4)[:, n * 2:n * 2 + 2], wv[:, k, n * 128:(n + 1) * 128],
                                                h2T[:, kb * 8 + k, 512:514],
                                                start=(kb == 0 and k == 0 and n == 0), stop=(kb == nkb - 1 and k == kc - 1),
                                                skip_group_check=True)
                        return last
                    P.add("pe", mmh, reads=[wkey] + [("h2T", kb * 8 + k) for k in range(kc)], writes=[("ps", 4)])
                    X.ws.release()
                for n in range(nch):
                    P.add("dve", lambda e, n=n: e.tensor_copy(out=gext[n][:, 0:2], in_=bank(X, 4)[:, n * 2:n * 2 + 2]),
                          reads=[("ps", 4)], writes=[("gext", n)])
            else:
                mm_ws(X, plan["gate"][gi], nch, 0, lambda k: h2T[:, k, 0:512], lambda k: ("h2T", k))
                for n in range(nch):
                    c = c0 + n
                    P.add("dve", lambda e, n=n, c=c: e.tensor_copy(out=gext[n][:, 0:2], in_=gprev[:, c, :]),
                          reads=[("gprev", c)], writes=[("gext", n)])
            mm_ws(X, plan["up"][gi], nch, 4, lambda k: h2T[:, k, 0:512], lambda k: ("h2T", k))
            for n in range(nch):
                c = c0 + n
                ge = gext[n]; cbv = cbuf[n % 2]
                P.add("act", lambda e, n=n, ge=ge: e.activation(out=ge[:, 2:514], in_=bank(X, n), func=AF.Copy),
                      reads=[("ps", n)], writes=[("gext", n)])
                P.add("dve", lambda e, c=c, ge=ge: e.tensor_copy(out=gprev[:, c, :], in_=ge[:, 512:514]),
                      reads=[("gext", n)], writes=[("gprev", c)])
                conv3(P, ge, ("gext", n), X.cwf, c, cbv, ("cbuf", n % 2), ("cwf",))
                P.add("act", lambda e, cbv=cbv: e.activation(out=cbv, in_=cbv, func=AF.Gelu_apprx_tanh),
                      reads=[("cbuf", n % 2)], writes=[("cbuf", n % 2)])
                P.add("dve", lambda e, c=c, n=n, cbv=cbv: e.tensor_tensor(out=actT[:, c, :], in0=cbv, in1=bank(X, 4 + n), op=ALU.mult),
                      reads=[("cbuf", n % 2), ("ps", 4 + n)], writes=[("actT", c)])
        P.barrier()
        out_proj_post(X, t, plan["out"], actT, "actT", xin, xkey_fn, dr["norm_ffn_post"][l:l + 1, :], xout, okey_fn,
                      res, X.junk3, xs, gb)


WEIGHT_SPECS = [
    ("w_in", lambda c: [c["L"], c["D"], c["DIN"]]), ("b_gate", lambda c: [c["L"], 2 * c["D"]]),
    ("conv_a", lambda c: [c["L"], 3, c["DC"]]), ("w_a_out", lambda c: [c["L"], c["DC"], c["D"]]),
    ("lam_q1", lambda c: [c["L"], 128]), ("lam_k1", lambda c: [c["L"], 128]),
    ("lam_q2", lambda c: [c["L"], 128]), ("lam_k2", lambda c: [c["L"], 128]),
    ("subln_g", lambda c: [c["L"], 256]), ("w_b_out", lambda c: [c["L"], c["DA"], c["D"]]),
    ("w_o", lambda c: [c["L"], c["D"], c["D"]]), ("norm_mix_pre", lambda c: [c["L"], c["D"]]),
    ("norm_mix_post", lambda c: [c["L"], c["D"]]), ("w_ffn_in", lambda c: [c["L"], c["D"], 2 * c["DFF"]]),
    ("conv_ffn", lambda c: [c["L"], 3, c["DFF"]]), ("w_ffn_out", lambda c: [c["L"], c["DFF"], c["D"]]),
    ("norm_ffn_pre", lambda c: [c["L"], c["D"]]), ("norm_ffn_post", lambda c: [c["L"], c["D"]]),
]


def build_program(cfg, mode="fused"):
    nc = bass.Bass("TRN2", target_bir_lowering=False)
    D = cfg["D"]; T = cfg["T"]; H = cfg["H"]; NR = cfg["NR"]; NBr = cfg["NBr"]; L = cfg["L"]
    X = Ctx()
    X.cfg = cfg; X.nc = nc
    dr = {}
    outs = set()

    def ext_in(name, shape, dt=F32):
        dr[name] = nc.dram_tensor(name, shape, dt, kind="ExternalInput").ap()

    def ext_out(name, shape, dt=F32):
        dr[name] = nc.dram_tensor(name, shape, dt, kind="ExternalOutput").ap()
        outs.add(name)

    def internal(name, shape, dt=F32):
        dr[name] = nc.dram_tensor(name, shape, dt).ap()

    for nm, shp in WEIGHT_SPECS:
        ext_in(nm, shp(cfg))
    for nm, shp in (("ident", [128, 128]), ("colb", [128, H, cfg["NE"]]), ("mm1", [128, H, 2, 256]),
                    ("sel", [128, NR]), ("sel2", [128, NR]), ("laminit", [128, 2 * L])):
        ext_in(nm, shp)
    ext_in("x", [T, D])
    HK = cfg["HK"]; VE = cfg["VE"]; NT = cfg["NT"]
    dr["kT_loc"] = {}; dr["kT_all"] = {}; dr["v_loc"] = {}; dr["v_all"] = {}
    kv_list = [("xh", None, [2, D], F32), ("mh", None, [2, D], F32)]
    for t in range(NT):
        for kh in range(H // HK):
            kv_list.append(("kT", (t, kh), [2 * HK * 128, 512], BF16))
        for vh in range(H // 2):
            kv_list.append(("v", (t, vh), [2 * 128, 4 * VE], BF16))
    for nm, idx, shp, dt in kv_list:
        sfx = "" if idx is None else "_%d_%d" % idx
        loc, al = nm + "_loc" + sfx, nm + "_all" + sfx
        shp_all = [NR * shp[0], shp[1]]
        if mode == "fused":
            internal(loc, shp, dt); internal(al, shp_all, dt)
        else:
            producer = "B" if nm == "mh" else "A"
            consumer = "C" if nm == "mh" else "B"
            if mode == producer:
                ext_out(loc, shp, dt)
            else:
                internal(loc, shp, dt)
            if mode == consumer:
                ext_in(al, shp_all, dt)
            else:
                internal(al, shp_all, dt)
        if idx is not None:
            dr[nm + "_loc"][idx] = dr.pop(loc); dr[nm + "_all"][idx] = dr.pop(al)
    if mode == "fused":
        internal("xmid", [T, D]); internal("x2", [T, D]); ext_out("out", [T, D])
    elif mode == "A":
        pass
    else:
        ext_out("out", [T, D])
    X.dr = dr
    X.is_out = lambda ap: ap.tensor.name in outs

    with ExitStack() as stack:
        A = Arena(nc, ARENA_BYTES)
        X.A = A
        X.PS = nc.alloc_psum_tensor("ps", [128, 4096], F32).ap()
        P = Prog(nc)
        X.P = P
        top = Layout(0)
        sizes = []
        R = cfg["ring"]
        Lp = Layout(ARENA_BYTES - (R * 8192 + 10240), ARENA_BYTES)
        X.act_limit = Lp.off
        wviews = [A.view(Lp.take(8 * 512 * 2), [8, 512], BF16) for _ in range(R)]
        X.ident = A.view(Lp.take(512), [128], F32)
        X.sel = A.view(Lp.take(NR * 4), [NR], F32)
        X.sel2 = A.view(Lp.take(NR * 4), [NR], F32)
        lam2 = A.view(Lp.take(2 * L * 4), [2 * L], F32)
        X.laminit = lam2[:, 0:L]
        X.omlam = lam2[:, L:2 * L]
        X.colb = A.view(Lp.take(H * cfg["NE"] * 4), [H, cfg["NE"]], F32)
        X.ss = A.view(Lp.take(64), [16], F32)
        X.ssq = A.view(Lp.take(4 * 8 * 4), [4, 8], F32)
        X.rd = A.view(Lp.take(16), [4], F32)
        X.fin_rr = [0]
        X.lsum = A.view(Lp.take(8), [2], F32)
        X.nlam = A.view(Lp.take(4), [1], F32)
        X.sgb = A.view(Lp.take(1024), [256], F32)
        X.junk3 = A.view(Lp.take(1024), [512], BF16)
        X.fstage = A.view(Lp.take(3 * 128 * 4), [3, 128], F32)
        X.cwa = A.view(Lp.take(3 * cfg["KC"] * 4), [3, cfg["KC"]], F32)
        X.cwf = A.view(Lp.take(3 * cfg["KF"] * 4), [3, cfg["KF"]], F32)
        X.bgT = A.view(Lp.take(2 * cfg["KD"] * 4), [1, 2 * cfg["KD"]], F32)
        X.uprev = A.view(Lp.take(cfg["KC"] * 2 * 4), [cfg["KC"], 2], F32)
        X.gprev = A.view(Lp.take(cfg["KF"] * 2 * 4), [cfg["KF"], 2], F32)
        X.ssn = [0]
        X.bank_rr = [0]
        X.loc_keys = {}

        def lockey(nm, tag):
            k = ("loc", nm, tag)
            X.loc_keys.setdefault(nm, []).append(k)
            return k
        X.lockey = lockey
        P.dma("sp", lambda e: e.dma_start(out=X.ident, in_=dr["ident"]), writes=[("ident",)])
        P.dma("sp", lambda e: e.dma_start(out=X.sel, in_=dr["sel"]), writes=[("sel",)])
        P.dma("sp", lambda e: e.dma_start(out=X.sel2, in_=dr["sel2"]), writes=[("sel",)])
        P.dma("sp", lambda e: e.dma_start(out=lam2, in_=dr["laminit"]), writes=[("laminit",)])
        P.dma("sp", lambda e: e.dma_start(out=X.colb, in_=dr["colb"]), writes=[("colb",)])

        ws = WStream(P, wviews)
        X.ws = ws
        plans = []
        if mode == "fused":
            for l in range(L):
                pb, sb = mixer_plan(X, l); pa, sa = kv_plan(X, l, pb); pc, sc = ffn_plan(X, l)
                ws.plan(sa); ws.plan(sb); ws.plan(sc)
                plans.append((pa, pb, pc))
        else:
            pl, sq = {"A": kv_plan, "B": mixer_plan, "C": ffn_plan}[mode](X, 0)
            ws.plan(sq)
        ws.start()
        groups = [[b * NR + r for r in range(NR)] for b in range(cfg["NB"])]

        def allgather(nm, idx=None):
            loc, al = dr[nm + "_loc"], dr[nm + "_all"]
            kk = nm
            akey = (nm + "_all",)
            if idx is not None:
                loc, al = loc[idx], al[idx]
                kk = (nm,) + idx
                akey = (nm + "_all",) + idx
            op = P.cc(lambda e: e.collective_compute("AllGather", ALU.bypass, replica_groups=groups, ins=[loc.opt()], outs=[al.opt()]),
                      reads=list(X.loc_keys.get(kk, [])), writes=[akey])
            X.loc_keys[kk] = []

        def after_kv_tile(t):
            for kh in range(H // HK):
                allgather("kT", (t, kh))
            for vh in range(H // 2):
                allgather("v", (t, vh))
        X.after_kv_tile = after_kv_tile if mode == "fused" else None
        X.allgather = allgather

        if mode == "fused":
            for l in range(L):
                xin = dr["x"] if l == 0 else dr["x2"]
                xk = (lambda t: ("x_in", t)) if l == 0 else (lambda t: ("x2", t))
                pa, pb, pc = plans[l]
                kv_phase(X, l, pa, xin, xk)
                import os
                stop = os.environ.get("FUSED_STOP")
                if stop == "A0":
                    break
                if stop == "A":
                    break
                mixer_phase(X, l, pb, xin, xk, dr["xmid"], lambda t: ("xmid", t))
                P.barrier()
                if stop == "B0":
                    break
                allgather("mh")
                if stop == "B":
                    break
                xo = dr["x2"] if l < L - 1 else dr["out"]
                ok = (lambda t: ("x2", t)) if l < L - 1 else (lambda t: ("out", t))
                ffn_phase(X, l, pc, dr["xmid"], lambda t: ("xmid", t), xo, ok)
                P.barrier()
        elif mode == "A":
            kv_phase(X, 0, pl, dr["x"], lambda t: ("x_in", t))
        elif mode == "B":
            mixer_phase(X, 0, pl, dr["x"], lambda t: ("x_in", t), dr["out"], lambda t: ("out", t))
        else:
            ffn_phase(X, 0, pl, dr["x"], lambda t: ("x_in", t), dr["out"], lambda t: ("out", t))
        import os
        assert os.environ.get("FUSED_STOP") or ws.next_use == len(ws.blocks), (ws.next_use, len(ws.blocks))
        P.check()
        P.emit(stack)
    return nc


def core_tables(cfg, r):
    H = cfg["H"]; NR = cfg["NR"]; NBr = cfg["NBr"]; NE = cfg["NE"]; EOFF = cfg["EOFF"]; L = cfg["L"]
    slopes = np.asarray(cfg["slopes"], np.float64)
    ki = np.arange(128, dtype=np.float64)
    colb = np.empty((128, H, NE), np.float32)
    for e in range(NE):
        delta = NBr * r + (e - EOFF)
        for h in range(H):
            if delta >= -1:
                colb[:, h, e] = -slopes[h] * (128.0 * delta + 128.0 - ki)
            else:
                colb[:, h, e] = -1e30
    mm1 = np.empty((128, H, 2, 256), np.float32)
    qi = np.arange(256)
    for typ in range(2):
        kk = 128 * typ + np.arange(128)
        vis = kk[:, None] <= qi[None, :]
        same = (kk[:, None] // 64) == (qi[None, :] // 64)
        dist = (kk[:, None] - qi[None, :]).astype(np.float64)
        for h in range(H):
            m = np.where(vis, 1.0, np.where(same, np.exp(-2.0 * slopes[h] * np.maximum(dist, 0.0)), 0.0))
            mm1[:, h, typ, :] = m - 1.0
    sel = np.zeros((128, NR), np.float32); sel[:, r] = 1.0
    sel2 = np.zeros((128, NR), np.float32)
    if r > 0:
        sel2[:, r - 1] = 1.0
    lam = np.zeros((128, 2 * L), np.float32)
    for l in range(L):
        lam[:, l] = cfg["lam_init"][l + cfg.get("layer0", 0)]
        lam[:, L + l] = 1.0 - cfg["lam_init"][l + cfg.get("layer0", 0)]
    return dict(ident=np.eye(128, dtype=np.float32), colb=colb, mm1=mm1, sel=sel, sel2=sel2, laminit=lam)


_CACHE = {}


def run_fused(cfg, inputs):
    key = ("fused", cfg["D"], cfg["H"], cfg["DFF"], cfg["T"], cfg["L"])
    if key not in _CACHE:
        _CACHE[key] = build_program(cfg, "fused")
    nc = _CACHE[key]
    NR = cfg["NR"]; NB = cfg["NB"]; T = cfg["T"]
    x = np.ascontiguousarray(np.asarray(inputs["x"], dtype=np.float32))
    w = {nm: np.ascontiguousarray(np.asarray(inputs[nm], dtype=np.float32)) for nm, _ in WEIGHT_SPECS}
    in_maps = []
    for c in range(NB * NR):
        b, r = divmod(c, NR)
        m = dict(w)
        m.update(core_tables(cfg, r))
        m["x"] = np.ascontiguousarray(x[b, r * T:(r + 1) * T, :])
        in_maps.append(m)
    res = run_bass_kernel_spmd(nc, in_maps, core_ids=list(range(NB * NR)))
    out = np.empty_like(x)
    for c in range(NB * NR):
        b, r = divmod(c, NR)
        out[b, r * T:(r + 1) * T, :] = res.results[c]["out"]
    return out


def kernel(**inputs):
    cfg = make_cfg()
    return run_fused(cfg, inputs)
```
